# Optimizing a Trainium2 kernel written in Bass

```python
import math
import jax, jax.numpy as jnp
from jax import lax
import numpy as np

D_MODEL = 4096
BATCH = 2
SEQ = 8192
DEPTH = 2

MIX_WIDTH = D_MODEL
GROUP_WIDTH = MIX_WIDTH // 4
HEAD_DIM = 128
MOBA_HEADS = GROUP_WIDTH // HEAD_DIM
MOBA_BLOCK = 256
MOBA_TOPK = 3
MOBA_QCHUNK = 32
GDN_HEADS = GROUP_WIDTH // HEAD_DIM
GDN_CONV = 4
GDN_CHUNK = 64
SC_CONV = 3
SC_GROUPS = GROUP_WIDTH // HEAD_DIM
SWA_HEAD_DIM = 64
SWA_Q_HEADS = GROUP_WIDTH // SWA_HEAD_DIM
SWA_KV_HEADS = 2
SWA_WINDOW = 128
D_FF = -(-8 * D_MODEL // (3 * 256)) * 256
EPS = 1e-6

IN_SIZES = (
    GROUP_WIDTH, GROUP_WIDTH, GROUP_WIDTH,
    3 * GROUP_WIDTH, GDN_HEADS, GDN_HEADS, GROUP_WIDTH,
    GROUP_WIDTH, GROUP_WIDTH, GROUP_WIDTH,
    SWA_Q_HEADS * SWA_HEAD_DIM, SWA_KV_HEADS * SWA_HEAD_DIM, SWA_KV_HEADS * SWA_HEAD_DIM,
)
IN_WIDTH = sum(IN_SIZES)
SPLIT_POINTS = tuple(int(v) for v in np.cumsum(IN_SIZES)[:-1])

kernel_name = "hymba_style_moba_gdn_shortconv_swa_block"


def rmsnorm(x, g):
    xf = x.astype(jnp.float32)
    y = xf * lax.rsqrt(jnp.mean(xf * xf, axis=-1, keepdims=True) + EPS) * g.astype(jnp.float32)
    return y.astype(x.dtype)


def l2norm(x):
    xf = x.astype(jnp.float32)
    return xf * lax.rsqrt(jnp.sum(xf * xf, axis=-1, keepdims=True) + EPS)


def causal_conv(x, w):
    kw = w.shape[0]
    s = x.shape[1]
    xp = jnp.pad(x, ((0, 0), (kw - 1, 0), (0, 0)))
    out = xp[:, 0:s] * w[0]
    for i in range(1, kw):
        out = out + xp[:, i:i + s] * w[i]
    return out


def moba_attention(q, k, v):
    b, s, h, d = q.shape
    nb = -(-s // MOBA_BLOCK)
    pad = nb * MOBA_BLOCK - s
    qh = q.transpose(0, 2, 1, 3)
    kh = jnp.pad(k.transpose(0, 2, 1, 3), ((0, 0), (0, 0), (0, pad), (0, 0)))
    vh = jnp.pad(v.transpose(0, 2, 1, 3), ((0, 0), (0, 0), (0, pad), (0, 0)))
    kb = kh.reshape(b, h, nb, MOBA_BLOCK, d)
    vb = vh.reshape(b, h, nb, MOBA_BLOCK, d)
    kmean = jnp.mean(kb.astype(jnp.float32), axis=3)
    topk = min(MOBA_TOPK, nb - 1)
    scale = d ** -0.5
    bi = jnp.arange(b)[:, None, None, None]
    hi = jnp.arange(h)[None, :, None, None]
    neg = -jnp.inf

    def chunk(c):
        q0 = c * MOBA_QCHUNK
        qc = lax.dynamic_slice_in_dim(qh, q0, MOBA_QCHUNK, axis=2)
        qpos = q0 + jnp.arange(MOBA_QCHUNK)
        blk = q0 // MOBA_BLOCK
        k_own = lax.dynamic_slice_in_dim(kh, blk * MOBA_BLOCK, MOBA_BLOCK, axis=2)
        v_own = lax.dynamic_slice_in_dim(vh, blk * MOBA_BLOCK, MOBA_BLOCK, axis=2)
        kpos = blk * MOBA_BLOCK + jnp.arange(MOBA_BLOCK)
        s_own = jnp.einsum('bhqd,bhkd->bhqk', qc, k_own).astype(jnp.float32) * scale
        s_own = jnp.where(kpos[None, :] <= qpos[:, None], s_own, neg)
        if topk > 0:
            gate = jnp.einsum('bhqd,bhnd->bhqn', qc.astype(jnp.float32), kmean)
            gate = jnp.where(jnp.arange(nb) < blk, gate, neg)
            gval, gidx = lax.top_k(gate, topk)
            valid = jnp.isfinite(gval)
            k_sel = kb[bi, hi, gidx]
            v_sel = vb[bi, hi, gidx]
            s_sel = jnp.einsum('bhqd,bhqtkd->bhqtk', qc, k_sel).astype(jnp.float32) * scale
            s_sel = jnp.where(valid[..., None], s_sel, neg)
            s_sel = s_sel.reshape(b, h, MOBA_QCHUNK, topk * MOBA_BLOCK)
            p = jax.nn.softmax(jnp.concatenate([s_sel, s_own], axis=-1), axis=-1)
            p_sel = p[..., :topk * MOBA_BLOCK].reshape(b, h, MOBA_QCHUNK, topk, MOBA_BLOCK)
            p_own = p[..., topk * MOBA_BLOCK:]
            o = (jnp.einsum('bhqtk,bhqtkd->bhqd', p_sel.astype(v.dtype), v_sel)
                 + jnp.einsum('bhqk,bhkd->bhqd', p_own.astype(v.dtype), v_own))
        else:
            p = jax.nn.softmax(s_own, axis=-1)
            o = jnp.einsum('bhqk,bhkd->bhqd', p.astype(v.dtype), v_own)
        return o.astype(q.dtype)

    out = lax.map(chunk, jnp.arange(s // MOBA_QCHUNK))
    return out.transpose(1, 0, 3, 2, 4).reshape(b, s, h * d)


def gated_delta_rule(q, k, v, g, beta):
    b, s, h, dk = q.shape
    dv = v.shape[-1]
    C = GDN_CHUNK
    nc = s // C
    f32 = jnp.float32

    def heads_chunks(t):
        t = jnp.moveaxis(t.astype(f32), 2, 1)
        return t.reshape((b, h, nc, C) + t.shape[3:])

    qc = heads_chunks(q) * (dk ** -0.5)
    kc = heads_chunks(k)
    vc = heads_chunks(v)
    gc = heads_chunks(g)
    bc = heads_chunks(beta)
    G = jnp.cumsum(gc, axis=-1)
    idx = jnp.arange(C)
    tril = idx[:, None] >= idx[None, :]
    strict = idx[:, None] > idx[None, :]
    decay = jnp.exp(jnp.where(tril, G[..., :, None] - G[..., None, :], -jnp.inf))
    k_beta = kc * bc[..., None]
    v_beta = vc * bc[..., None]
    kkt = jnp.einsum('bhnik,bhnjk->bhnij', k_beta, kc) * decay
    M = jnp.eye(C, dtype=f32) + jnp.where(strict, kkt, 0.0)
    u = lax.linalg.triangular_solve(M, v_beta, left_side=True, lower=True, unit_diagonal=True)
    w = lax.linalg.triangular_solve(M, k_beta * jnp.exp(G)[..., None], left_side=True,
                                    lower=True, unit_diagonal=True)
    qk = jnp.einsum('bhnik,bhnjk->bhnij', qc, kc) * decay
    q_dec = qc * jnp.exp(G)[..., None]
    k_dec = kc * jnp.exp(G[..., -1:] - G)[..., None]
    g_last = jnp.exp(G[..., -1])

    def step(state, inp):
        qk_i, u_i, w_i, q_i, k_i, gl = inp
        v_new = u_i - jnp.einsum('bhck,bhkv->bhcv', w_i, state)
        o = jnp.einsum('bhck,bhkv->bhcv', q_i, state) + jnp.einsum('bhcj,bhjv->bhcv', qk_i, v_new)
        state = state * gl[..., None, None] + jnp.einsum('bhck,bhcv->bhkv', k_i, v_new)
        return state, o

    xs = tuple(jnp.moveaxis(t, 2, 0) for t in (qk, u, w, q_dec, k_dec, g_last))
    s0 = jnp.zeros((b, h, dk, dv), f32)
    _, o = lax.scan(step, s0, xs)
    o = jnp.moveaxis(o, 0, 2).reshape(b, h, s, dv)
    return o.transpose(0, 2, 1, 3)


def sliding_window_attention(q, k, v, sinks):
    b, s, hq, d = q.shape
    hkv = k.shape[2]
    grp = hq // hkv
    W = SWA_WINDOW
    nb = s // W
    qb = q.reshape(b, nb, W, hkv, grp, d)
    kp = jnp.pad(k, ((0, 0), (W, 0), (0, 0), (0, 0))).reshape(b, nb + 1, W, hkv, d)
    vp = jnp.pad(v, ((0, 0), (W, 0), (0, 0), (0, 0))).reshape(b, nb + 1, W, hkv, d)
    kk = jnp.concatenate([kp[:, :-1], kp[:, 1:]], axis=2)
    vv = jnp.concatenate([vp[:, :-1], vp[:, 1:]], axis=2)
    scores = jnp.einsum('bnqhgd,bnkhd->bhgnqk', qb, kk).astype(jnp.float32) * (d ** -0.5)
    qrel = W + jnp.arange(W)[:, None]
    krel = jnp.arange(2 * W)[None, :]
    nidx = jnp.arange(nb)[:, None, None]
    mask = (krel <= qrel) & (krel > qrel - W) & ((nidx > 0) | (krel >= W))
    scores = jnp.where(mask, scores, -jnp.inf)
    sink = jnp.broadcast_to(sinks.astype(jnp.float32).reshape(hkv, grp)[None, :, :, None, None, None],
                            scores.shape[:-1] + (1,))
    p = jax.nn.softmax(jnp.concatenate([scores, sink], axis=-1), axis=-1)[..., :-1]
    out = jnp.einsum('bhgnqk,bnkhd->bnqhgd', p.astype(v.dtype), vv)
    return out.reshape(b, s, hq * d).astype(v.dtype)


def hybrid_layer(x, norm_mix, w_in, moba_q_norm, moba_k_norm, gdn_conv, gdn_a_log, gdn_dt_bias,
                 gdn_out_norm, sc_conv, swa_q_norm, swa_k_norm, swa_sinks, w_out, norm_ffn,
                 w_gate, w_up, w_down):
    b, s, _ = x.shape
    h = rmsnorm(x, norm_mix)
    proj = h @ w_in
    (mq, mk, mv, gqkv, ga, gb, gz, scb, scc, scx, sq, sk, sv) = jnp.split(proj, SPLIT_POINTS, axis=-1)

    mq = rmsnorm(mq.reshape(b, s, MOBA_HEADS, HEAD_DIM), moba_q_norm)
    mk = rmsnorm(mk.reshape(b, s, MOBA_HEADS, HEAD_DIM), moba_k_norm)
    mv = mv.reshape(b, s, MOBA_HEADS, HEAD_DIM)
    o_a = moba_attention(mq, mk, mv)

    gqkv = jax.nn.silu(causal_conv(gqkv, gdn_conv))
    gq, gk, gv = jnp.split(gqkv, 3, axis=-1)
    gq = l2norm(gq.reshape(b, s, GDN_HEADS, HEAD_DIM))
    gk = l2norm(gk.reshape(b, s, GDN_HEADS, HEAD_DIM))
    gv = gv.reshape(b, s, GDN_HEADS, HEAD_DIM)
    log_decay = -jnp.exp(gdn_a_log.astype(jnp.float32)) * jax.nn.softplus(
        ga.astype(jnp.float32) + gdn_dt_bias.astype(jnp.float32))
    beta = jax.nn.sigmoid(gb.astype(jnp.float32))
    o = gated_delta_rule(gq, gk, gv, log_decay, beta)
    zg = jax.nn.silu(gz.reshape(b, s, GDN_HEADS, HEAD_DIM).astype(jnp.float32))
    o_b = (rmsnorm(o, gdn_out_norm) * zg).reshape(b, s, GROUP_WIDTH).astype(x.dtype)

    o_c = scb * causal_conv(scc * scx, sc_conv)

    sq = rmsnorm(sq.reshape(b, s, SWA_Q_HEADS, SWA_HEAD_DIM), swa_q_norm)
    sk = rmsnorm(sk.reshape(b, s, SWA_KV_HEADS, SWA_HEAD_DIM), swa_k_norm)
    sv = sv.reshape(b, s, SWA_KV_HEADS, SWA_HEAD_DIM)
    o_d = sliding_window_attention(sq, sk, sv, swa_sinks)

    mix = jnp.concatenate([o_a, o_b, o_c.astype(x.dtype), o_d], axis=-1)
    x = x + mix @ w_out

    h = rmsnorm(x, norm_ffn)
    x = x + (jax.nn.silu(h @ w_gate) * (h @ w_up)) @ w_down
    return x


def setup_inputs(seed: int = 0) -> dict:
    key = jax.random.key(seed)
    ks = jax.random.split(key, 20)

    def nrm(k, shape, scale):
        return jax.random.normal(k, shape, jnp.float32) * scale

    dt = jnp.exp(jax.random.uniform(ks[7], (DEPTH, GDN_HEADS), jnp.float32,
                                    minval=math.log(1e-3), maxval=math.log(1e-1)))
    return {
        "x": nrm(ks[0], (BATCH, SEQ, D_MODEL), 1.0),
        "norm_mix": 1.0 + nrm(ks[1], (DEPTH, D_MODEL), 0.02),
        "w_in": nrm(ks[2], (DEPTH, D_MODEL, IN_WIDTH), D_MODEL ** -0.5),
        "moba_q_norm": 1.0 + nrm(ks[3], (DEPTH, HEAD_DIM), 0.02),
        "moba_k_norm": 1.0 + nrm(ks[4], (DEPTH, HEAD_DIM), 0.02),
        "gdn_conv": nrm(ks[5], (DEPTH, GDN_CONV, 3 * GROUP_WIDTH), GDN_CONV ** -0.5),
        "gdn_a_log": jnp.log(jax.random.uniform(ks[6], (DEPTH, GDN_HEADS), jnp.float32,
                                                minval=1.0, maxval=16.0)),
        "gdn_dt_bias": dt + jnp.log(-jnp.expm1(-dt)),
        "gdn_out_norm": 1.0 + nrm(ks[8], (DEPTH, HEAD_DIM), 0.02),
        "sc_conv": nrm(ks[9], (DEPTH, SC_CONV, GROUP_WIDTH), SC_CONV ** -0.5),
        "swa_q_norm": 1.0 + nrm(ks[10], (DEPTH, SWA_HEAD_DIM), 0.02),
        "swa_k_norm": 1.0 + nrm(ks[11], (DEPTH, SWA_HEAD_DIM), 0.02),
        "swa_sinks": nrm(ks[12], (DEPTH, SWA_Q_HEADS), 1.0),
        "w_out": nrm(ks[13], (DEPTH, MIX_WIDTH, D_MODEL), MIX_WIDTH ** -0.5),
        "norm_ffn": 1.0 + nrm(ks[14], (DEPTH, D_MODEL), 0.02),
        "w_gate": nrm(ks[15], (DEPTH, D_MODEL, D_FF), D_MODEL ** -0.5),
        "w_up": nrm(ks[16], (DEPTH, D_MODEL, D_FF), D_MODEL ** -0.5),
        "w_down": nrm(ks[17], (DEPTH, D_FF, D_MODEL), D_FF ** -0.5),
    }


def reference(x, norm_mix, w_in, moba_q_norm, moba_k_norm, gdn_conv, gdn_a_log, gdn_dt_bias,
              gdn_out_norm, sc_conv, swa_q_norm, swa_k_norm, swa_sinks, w_out, norm_ffn,
              w_gate, w_up, w_down):
    for l in range(DEPTH):
        x = hybrid_layer(x, norm_mix[l], w_in[l], moba_q_norm[l], moba_k_norm[l], gdn_conv[l],
                         gdn_a_log[l], gdn_dt_bias[l], gdn_out_norm[l], sc_conv[l], swa_q_norm[l],
                         swa_k_norm[l], swa_sinks[l], w_out[l], norm_ffn[l], w_gate[l], w_up[l],
                         w_down[l])
    return x
```

```python
import numpy as np
from contextlib import ExitStack
import concourse.bass as bass
import concourse.mybir as mybir
from concourse.bass_utils import run_bass_kernel_spmd

F32 = mybir.dt.float32
BF16 = mybir.dt.bfloat16
AF = mybir.ActivationFunctionType
ALU = mybir.AluOpType

ENGS = ("pe", "act", "dve", "pool", "sp")
NDMA = 8


class Buf:
    __slots__ = ("name", "w", "r", "excl")

    def __init__(self, name="", excl=False):
        self.name = name
        self.excl = excl
        self.w = None
        self.r = {}


class Prog:
    def __init__(self, nc, stack):
        self.nc = nc
        self.ops = {e: [] for e in ENGS}
        self.cnt = {e: 0 for e in ENGS}
        self.dcnt = {e: 0 for e in ENGS}
        self.sem = {e: stack.enter_context(nc.semaphore("s_" + e)) for e in ENGS}
        self.dsem = {e: [stack.enter_context(nc.semaphore("d_%s%d" % (e, i))) for i in range(NDMA)]
                     for e in ("sp", "pool", "act")}
        self.semname = {}
        for e in ENGS:
            self.semname[id(self.sem[e])] = e

    def _deps(self, reads, writes):
        toks = {}

        def add(t):
            if t is None:
                return
            s, v = t
            k = id(s)
            if k not in toks or toks[k][1] < v:
                toks[k] = (s, v)
        for b in reads:
            add(b.w)
        for b in writes:
            add(b.w)
            for t in b.r.values():
                add(t)
        return toks

    def _mark(self, tok, reads, writes):
        s, v = tok
        for b in reads:
            k = id(s)
            if k not in b.r or b.r[k][1] < v:
                b.r[k] = tok
        for b in writes:
            b.w = tok
            b.r = {}

    def op(self, eng, fn, reads=(), writes=()):
        if any(b.excl for b in reads):
            writes = list(writes) + [b for b in reads if b.excl]
            reads = [b for b in reads if not b.excl]
        toks = self._deps(reads, writes)
        self.cnt[eng] += 1
        tok = (self.sem[eng], self.cnt[eng])
        self._mark(tok, reads, writes)
        self.ops[eng].append((toks, fn, self.sem[eng], 1))

    def dma(self, eng, fn, reads=(), writes=()):
        toks = self._deps(reads, writes)
        i = self.dcnt[eng]
        self.dcnt[eng] += 1
        s = self.dsem[eng][i % NDMA]
        if i >= NDMA:
            k = id(s)
            v = 16 * (i // NDMA)
            if k not in toks or toks[k][1] < v:
                toks[k] = (s, v)
        tok = (s, 16 * (i // NDMA + 1))
        self._mark(tok, reads, writes)
        self.ops[eng].append((toks, fn, s, 16))

    def barrier(self):
        toks = {}
        for e in ENGS:
            if self.cnt[e]:
                toks[id(self.sem[e])] = (self.sem[e], self.cnt[e])
        for e in ("sp", "pool", "act"):
            n = self.dcnt[e]
            for j in range(min(n, NDMA)):
                cntj = (n - 1 - j) // NDMA + 1
                toks[id(self.dsem[e][j])] = (self.dsem[e][j], 16 * cntj)
        for e in ENGS:
            self.ops[e].append((dict(toks), None, None, 0))

    def emit(self, final=True):
        nc = self.nc
        if not hasattr(self, "known"):
            self.known = {e: {} for e in ENGS}
        if final:
            self.barrier()

        def run(engname, eng):
            known = self.known[engname]
            own = id(self.sem[engname])
            for toks, fn, s, inc in self.ops[engname]:
                for k, (ws, wv) in toks.items():
                    if engname == "pe" and k == own and fn is not None:
                        continue
                    if known.get(k, 0) >= wv:
                        continue
                    eng.wait_ge(ws, wv)
                    known[k] = wv
                if fn is not None:
                    fn(eng).then_inc(s, inc)
            self.ops[engname] = []

        with nc.Block() as block:
            @block.tensor
            def _(e):
                run("pe", e)

            @block.scalar
            def _(e):
                run("act", e)

            @block.vector
            def _(e):
                run("dve", e)

            @block.gpsimd
            def _(e):
                run("pool", e)

            @block.sync
            def _(e):
                run("sp", e)


class Ctx:
    def __init__(self, nc, stack):
        self.nc = nc
        self.stack = stack
        self.n = 0

    def sb(self, shape, dt, name=None):
        self.n += 1
        return self.stack.enter_context(self.nc.sbuf_tensor(name or ("t%d" % self.n), list(shape), dt))

    def ps(self, shape, dt, name=None):
        self.n += 1
        return self.stack.enter_context(self.nc.psum_tensor(name or ("p%d" % self.n), list(shape), dt))

    def din(self, name, shape, dt=F32):
        return self.nc.dram_tensor(name, list(shape), dt, kind="ExternalInput").ap()

    def dout(self, name, shape, dt=F32):
        return self.nc.dram_tensor(name, list(shape), dt, kind="ExternalOutput").ap()

EPS = 1e-6
GT = 512
KC = 32


def build_stage_a(T=2048, NCB=91):
    NG = T // GT
    nc = bass.Bass("TRN2", target_bir_lowering=False)
    with ExitStack() as st:
        c = Ctx(nc, st)
        xT = c.din("xT", [KC, 128, T])
        gcol = c.din("gcol", [128, KC])
        w = c.din("w", [NCB, 128, KC * 128])
        ones_d = c.din("ones", [128, 128])
        projT = c.dout("projT", [NCB, 128, T])

        P = Prog(nc, st)
        xg = c.sb([128, KC, GT], F32, "xg")
        hT = c.sb([128, KC, GT], BF16, "hT")
        NW = 3
        wt = [c.sb([128, KC * 128], BF16, "wt%d" % i) for i in range(NW)]
        sq = [c.sb([128, GT], F32, "sq%d" % i) for i in range(2)]
        sd = c.sb([128, GT], F32, "sd")
        rstd = c.sb([128, GT], F32, "rstd")
        gt = c.sb([128, KC], F32, "gt")
        ones = c.sb([128, 128], F32, "ones_sb")
        NO = 4
        ost = [c.sb([128, GT], F32, "ost%d" % i) for i in range(NO)]
        ps = [c.ps([128, GT], F32, "ps%d" % i) for i in range(NO)]
        pss = c.ps([128, GT], F32, "pss")

        b_xg = [Buf() for _ in range(4)]
        b_hT = [Buf() for _ in range(KC)]
        b_wt = [Buf() for _ in range(NW)]
        b_sq = [Buf() for _ in range(2)]
        b_sd, b_rstd, b_gt, b_ones, b_pss = Buf(), Buf(), Buf(), Buf(), Buf()
        b_ost = [Buf() for _ in range(NO)]
        b_ps = [Buf() for _ in range(NO)]

        P.dma("sp", lambda e: e.dma_start(out=gt[:], in_=gcol[:, :]), [], [b_gt])
        P.dma("sp", lambda e: e.dma_start(out=ones[:], in_=ones_d[:, :]), [], [b_ones])

        wi = 0
        oi = 0
        for g in range(NG):
            ts = slice(g * GT, (g + 1) * GT)
            for q in range(4):
                P.dma("sp", lambda e, q=q, ts=ts: e.dma_start(
                    out=xg[:, q * 8:(q + 1) * 8, :], in_=xT[q * 8:(q + 1) * 8, :, ts].rearrange("k p t -> p k t")),
                    [], [b_xg[q]])
            for dc in range(KC):
                s = dc % 2
                P.op("act", lambda e, dc=dc, s=s: e.activation(out=sq[s][:], in_=xg[:, dc, :], func=AF.Square),
                     [b_xg[dc // 8]], [b_sq[s]])
                P.op("pe", lambda e, dc=dc, s=s: e.matmul(pss[:], ones[:], sq[s][:], start=(dc == 0), stop=(dc == KC - 1)),
                     [b_sq[s], b_ones], [b_pss])
            P.op("dve", lambda e: e.tensor_scalar(out=sd[:], in0=pss[:], scalar1=1.0 / 4096, scalar2=EPS,
                                                  op0=ALU.mult, op1=ALU.add), [b_pss], [b_sd])
            P.op("act", lambda e: e.activation(out=sd[:], in_=sd[:], func=AF.Sqrt), [b_sd], [b_sd])
            P.op("dve", lambda e: e.reciprocal(out=rstd[:], in_=sd[:]), [b_sd], [b_rstd])
            for dc in range(KC):
                P.op("dve", lambda e, dc=dc: e.scalar_tensor_tensor(
                    out=hT[:, dc, :], in0=xg[:, dc, :], scalar=gt[:, dc:dc + 1], in1=rstd[:],
                    op0=ALU.mult, op1=ALU.mult), [b_xg[dc // 8], b_gt, b_rstd], [b_hT[dc]])
            for cb in range(NCB):
                wb = wi % NW
                wi += 1
                P.dma("pool", lambda e, cb=cb, wb=wb: e.dma_start(out=wt[wb][:], in_=w[cb, :, :]), [], [b_wt[wb]])
                ob = oi % NO
                oi += 1
                for kc in range(KC):
                    P.op("pe", lambda e, kc=kc, wb=wb, ob=ob: e.matmul(
                        ps[ob][:], wt[wb][:, kc * 128:(kc + 1) * 128], hT[:, kc, :],
                        start=(kc == 0), stop=(kc == KC - 1)), [b_wt[wb], b_hT[kc]], [b_ps[ob]])
                if cb % 2 == 0:
                    P.op("act", lambda e, ob=ob: e.copy(out=ost[ob][:], in_=ps[ob][:]), [b_ps[ob]], [b_ost[ob]])
                else:
                    P.op("dve", lambda e, ob=ob: e.tensor_copy(out=ost[ob][:], in_=ps[ob][:]), [b_ps[ob]], [b_ost[ob]])
                P.dma("sp", lambda e, cb=cb, ob=ob, ts=ts: e.dma_start(out=projT[cb, :, ts], in_=ost[ob][:]),
                      [b_ost[ob]], [])
        P.emit()
    return nc


def host_w_layout(w_in_l, perm, NCB):
    wp = w_in_l[:, perm]
    pad = NCB * 128 - wp.shape[1]
    if pad:
        wp = np.concatenate([wp, np.zeros((4096, pad), np.float32)], axis=1)
    a = wp.reshape(KC, 128, NCB, 128).transpose(2, 1, 0, 3)
    return np.ascontiguousarray(a).reshape(NCB, 128, KC * 128)


JB = 86
JH = 43


def build_stage_c(T=2048, NJB=JB, NNB=KC):
    NG = T // GT
    JHALF = NJB // 2
    nc = bass.Bass("TRN2", target_bir_lowering=False)
    with ExitStack() as st:
        c = Ctx(nc, st)
        xT = c.din("xT", [KC, 128, T])
        mixT = c.din("mixT", [KC, 128, T])
        gcol = c.din("gcol", [128, KC])
        wo = c.din("wo", [KC, 128, KC * 128])
        wg = c.din("wg", [NJB, 128, KC * 128])
        wu = c.din("wu", [NJB, 128, KC * 128])
        wd = c.din("wd", [KC, 2, 128, JHALF * 128])
        ones_d = c.din("ones", [128, 128])
        x1T = c.dout("x1T", [KC, 128, T])
        x2T = c.dout("x2T", [KC, 128, T])

        P = Prog(nc, st)
        aT = c.sb([128, NJB, GT], BF16, "aT")
        h2T = c.sb([128, KC, GT], BF16, "h2T")
        NW = 3
        WSZ = max(JHALF * 128, KC * 128)
        wt = [c.sb([128, WSZ], BF16, "wt%d" % i) for i in range(NW)]
        xin = [c.sb([128, GT], F32, "xin%d" % i) for i in range(2)]
        blk = [c.sb([128, GT], F32, "blk%d" % i) for i in range(2)]
        tmp = [c.sb([128, GT], F32, "tmp%d" % i) for i in range(2)]
        sq = c.sb([128, GT], F32, "sq")
        sd = c.sb([128, GT], F32, "sd")
        rstd = c.sb([128, GT], F32, "rstd")
        gt = c.sb([128, KC], F32, "gt")
        ones = c.sb([128, 128], F32, "ones_sb")
        ps = [c.ps([128, GT], F32, "ps%d" % i) for i in range(6)]
        pss = c.ps([128, GT], F32, "pss")

        b_aT = [Buf() for _ in range(NJB)]
        b_h2T = [Buf() for _ in range(KC)]
        b_wt = [Buf() for _ in range(NW)]
        b_xin = [Buf() for _ in range(2)]
        b_blk = [Buf() for _ in range(2)]
        b_tmp = [Buf() for _ in range(2)]
        b_sq, b_sd, b_rstd, b_gt, b_ones, b_pss = Buf(), Buf(), Buf(), Buf(), Buf(), Buf()
        b_ps = [Buf() for _ in range(6)]
        b_x1d = [Buf() for _ in range(KC)]

        P.dma("sp", lambda e: e.dma_start(out=gt[:], in_=gcol[:, :]), [], [b_gt])
        P.dma("sp", lambda e: e.dma_start(out=ones[:], in_=ones_d[:, :]), [], [b_ones])

        cnt = {"w": 0, "p": 0, "x": 0, "b": 0, "t": 0}

        def nxt(k, n):
            v = cnt[k] % n
            cnt[k] += 1
            return v

        for g in range(NG):
            ts = slice(g * GT, (g + 1) * GT)
            for q in range(4):
                P.dma("pool", lambda e, q=q, ts=ts: e.dma_start(
                    out=aT[:, q * 8:(q + 1) * 8, :], in_=mixT[q * 8:(q + 1) * 8, :, ts].rearrange("k p t -> p k t")),
                    [], b_aT[q * 8:(q + 1) * 8])
            for nb in range(NNB):
                wb = nxt("w", NW)
                P.dma("pool", lambda e, nb=nb, wb=wb: e.dma_start(out=wt[wb][:, 0:KC * 128], in_=wo[nb, :, :]),
                      [], [b_wt[wb]])
                xb = nxt("x", 2)
                P.dma("sp", lambda e, nb=nb, xb=xb, ts=ts: e.dma_start(out=xin[xb][:], in_=xT[nb, :, ts]),
                      [], [b_xin[xb]])
                pb = nxt("p", 6)
                for fc in range(KC):
                    P.op("pe", lambda e, fc=fc, wb=wb, pb=pb: e.matmul(
                        ps[pb][:], wt[wb][:, fc * 128:(fc + 1) * 128], aT[:, fc, :],
                        start=(fc == 0), stop=(fc == KC - 1)), [b_wt[wb], b_aT[fc]], [b_ps[pb]])
                bb = nxt("b", 2)
                P.op("dve", lambda e, pb=pb, xb=xb, bb=bb: e.tensor_tensor(
                    out=blk[bb][:], in0=ps[pb][:], in1=xin[xb][:], op=ALU.add),
                    [b_ps[pb], b_xin[xb]], [b_blk[bb]])
                P.dma("sp", lambda e, nb=nb, bb=bb, ts=ts: e.dma_start(out=x1T[nb, :, ts], in_=blk[bb][:]),
                      [b_blk[bb]], [b_x1d[nb]])
                P.op("act", lambda e, bb=bb: e.activation(out=sq[:], in_=blk[bb][:], func=AF.Square),
                     [b_blk[bb]], [b_sq])
                P.op("pe", lambda e, nb=nb: e.matmul(pss[:], ones[:], sq[:], start=(nb == 0), stop=(nb == NNB - 1)),
                     [b_sq, b_ones], [b_pss])
            P.op("dve", lambda e: e.tensor_scalar(out=sd[:], in0=pss[:], scalar1=1.0 / (NNB * 128), scalar2=EPS,
                                                  op0=ALU.mult, op1=ALU.add), [b_pss], [b_sd])
            P.op("act", lambda e: e.activation(out=sd[:], in_=sd[:], func=AF.Sqrt), [b_sd], [b_sd])
            P.op("dve", lambda e: e.reciprocal(out=rstd[:], in_=sd[:]), [b_sd], [b_rstd])
            for kc in range(NNB):
                xb = nxt("x", 2)
                P.dma("sp", lambda e, kc=kc, xb=xb, ts=ts: e.dma_start(out=xin[xb][:], in_=x1T[kc, :, ts]),
                      [b_x1d[kc]], [b_xin[xb]])
                P.op("dve", lambda e, kc=kc, xb=xb: e.scalar_tensor_tensor(
                    out=h2T[:, kc, :], in0=xin[xb][:], scalar=gt[:, kc:kc + 1], in1=rstd[:],
                    op0=ALU.mult, op1=ALU.mult), [b_xin[xb], b_gt, b_rstd], [b_h2T[kc]])
            for jb in range(NJB):
                wbg = nxt("w", NW)
                P.dma("pool", lambda e, jb=jb, wb=wbg: e.dma_start(out=wt[wb][:, 0:KC * 128], in_=wg[jb, :, :]),
                      [], [b_wt[wbg]])
                pg = nxt("p", 6)
                for kc in range(NNB):
                    P.op("pe", lambda e, kc=kc, wb=wbg, pb=pg: e.matmul(
                        ps[pb][:], wt[wb][:, kc * 128:(kc + 1) * 128], h2T[:, kc, :],
                        start=(kc == 0), stop=(kc == NNB - 1)), [b_wt[wbg], b_h2T[kc]], [b_ps[pg]])
                wbu = nxt("w", NW)
                P.dma("pool", lambda e, jb=jb, wb=wbu: e.dma_start(out=wt[wb][:, 0:KC * 128], in_=wu[jb, :, :]),
                      [], [b_wt[wbu]])
                pu = nxt("p", 6)
                for kc in range(NNB):
                    P.op("pe", lambda e, kc=kc, wb=wbu, pb=pu: e.matmul(
                        ps[pb][:], wt[wb][:, kc * 128:(kc + 1) * 128], h2T[:, kc, :],
                        start=(kc == 0), stop=(kc == NNB - 1)), [b_wt[wbu], b_h2T[kc]], [b_ps[pu]])
                tb = nxt("t", 2)
                P.op("act", lambda e, pb=pg, tb=tb: e.activation(out=tmp[tb][:], in_=ps[pb][:], func=AF.Silu),
                     [b_ps[pg]], [b_tmp[tb]])
                P.op("dve", lambda e, jb=jb, pb=pu, tb=tb: e.tensor_tensor(
                    out=aT[:, jb, :], in0=ps[pb][:], in1=tmp[tb][:], op=ALU.mult),
                    [b_ps[pu], b_tmp[tb]], [b_aT[jb]])
            for nb in range(NNB):
                pb = nxt("p", 6)
                for hf in range(2):
                    wb = nxt("w", NW)
                    P.dma("pool", lambda e, nb=nb, hf=hf, wb=wb: e.dma_start(
                        out=wt[wb][:, 0:JHALF * 128], in_=wd[nb, hf, :, :]), [], [b_wt[wb]])
                    for jj in range(JHALF):
                        jc = hf * JHALF + jj
                        P.op("pe", lambda e, jj=jj, jc=jc, wb=wb, pb=pb: e.matmul(
                            ps[pb][:], wt[wb][:, jj * 128:(jj + 1) * 128], aT[:, jc, :],
                            start=(jc == 0), stop=(jc == NJB - 1)), [b_wt[wb], b_aT[jc]], [b_ps[pb]])
                xb = nxt("x", 2)
                P.dma("sp", lambda e, nb=nb, xb=xb, ts=ts: e.dma_start(out=xin[xb][:], in_=x1T[nb, :, ts]),
                      [b_x1d[nb]], [b_xin[xb]])
                bb = nxt("b", 2)
                P.op("dve", lambda e, pb=pb, xb=xb, bb=bb: e.tensor_tensor(
                    out=blk[bb][:], in0=ps[pb][:], in1=xin[xb][:], op=ALU.add),
                    [b_ps[pb], b_xin[xb]], [b_blk[bb]])
                P.dma("sp", lambda e, nb=nb, bb=bb, ts=ts: e.dma_start(out=x2T[nb, :, ts], in_=blk[bb][:]),
                      [b_blk[bb]], [])
        P.emit()
    return nc


def lay_cols(wm, nblk):
    K = wm.shape[0]
    a = wm.reshape(K // 128, 128, nblk, 128).transpose(2, 1, 0, 3)
    return np.ascontiguousarray(a).reshape(nblk, 128, (K // 128) * 128)


def lay_wd(wdm, njb):
    jh = njb // 2
    a = wdm.reshape(2, jh, 128, KC, 128).transpose(3, 0, 2, 1, 4)
    return np.ascontiguousarray(a).reshape(KC, 2, 128, jh * 128)


def fm(a, T):
    return np.ascontiguousarray(a.T).reshape(KC, 128, T)


NEG = -30000.0
F_MQ, F_MK, F_GQ, F_GK, F_GV, F_SCB, F_SCC, F_SCX, F_SQ, F_SK, F_SKB = range(11)
NF = 11
PC_MQG, PC_MKG = 0, 1
PC_CQ, PC_CK, PC_CV = 2, 6, 10
PC_ALOG, PC_DTB = 14, 15
PC_SC = 16
PC_SQG, PC_SKG = 19, 20
PC_SINK = 21
NPC = 23
C_ID, C_ONES, C_TRI, C_NEGLS, C_NEGU, C_BLK64 = range(6)
NCF = 6
B_ID, B_ONES, B_SWAM, B_CM, B_ESEL = 0, 128, 256, 768, 1280
NCB16 = 1280 + 32 * 128


def build_stage_b(S=8192, NB=2, do=("sc", "swa", "moba", "gdn")):
    TT = NB * S
    NT = S // 128
    SEG = min(1024, S)
    nc = bass.Bass("TRN2", target_bir_lowering=False)
    with ExitStack() as st:
        c = Ctx(nc, st)
        fmT = c.din("fmT", [NF, 128, TT])
        mv_tm = c.din("mv_tm", [TT, 128])
        sv_tm = c.din("sv_tm", [TT, 64])
        gz_tm = c.din("gz_tm", [TT, 128])
        gab_tm = c.din("gab_tm", [128, 2, TT // 128])
        pcol_d = c.din("pcol", [128, NPC])
        gnB_d = c.din("gnB", [128, 128])
        cst_d = c.din("cst", [128, NCF * 128])
        negpast_d = c.din("negpast", [128, 32 * 32])
        cbf_d = c.din("cbf", [128, NCB16])
        outB = c.dout("outB", [4, 128, TT])

        P = Prog(nc, st)
        pcol = c.sb([128, NPC], F32, "pcol_sb")
        gnB = c.sb([128, 128], F32, "gnB_sb")
        cst = c.sb([128, NCF * 128], F32, "cst_sb")
        negpast = c.sb([128, 32 * 32], F32, "negpast_sb")
        cbf = c.sb([128, NCB16], BF16, "cbf_sb")
        b_const = Buf()
        P.dma("sp", lambda e: e.dma_start(out=pcol[:], in_=pcol_d[:, :]), [], [b_const])
        P.dma("sp", lambda e: e.dma_start(out=gnB[:], in_=gnB_d[:, :]), [], [b_const])
        P.dma("sp", lambda e: e.dma_start(out=cst[:], in_=cst_d[:, :]), [], [b_const])
        P.dma("sp", lambda e: e.dma_start(out=negpast[:], in_=negpast_d[:, :]), [], [b_const])
        P.dma("pool", lambda e: e.dma_start(out=cbf[:], in_=cbf_d[:, :]), [], [b_const])

        def cf(i):
            return cst[:, i * 128:(i + 1) * 128]

        NPS = 8
        ps = [c.ps([128, 512], F32, "ps%d" % i) for i in range(NPS)]
        b_ps = [Buf(excl=True) for _ in range(NPS)]
        cnt = {}

        def nxt(k, n):
            v = cnt.get(k, 0)
            cnt[k] = v + 1
            return v % n

        class Pool_:
            def __init__(self, name, n, shape, dt, cx=None):
                cx = cx or c
                self.t = [cx.sb(shape, dt, "%s%d" % (name, i)) for i in range(n)]
                self.b = [Buf() for _ in range(n)]
                self.n = n
                self.i = 0

            def get(self):
                k = self.i % self.n
                self.i += 1
                return self.t[k], self.b[k]

        def getps():
            k = 4 + nxt("ps", NPS - 4)
            return ps[k], b_ps[k]

        def getacc(i):
            return ps[i], b_ps[i]

        CB = [b_const]
        ones_f = cf(C_ONES)
        ident_f = cf(C_ID)

        if "sc" in do:
          with ExitStack() as ms:
            cm = Ctx(nc, ms)
            SCS = 1024
            sc_in = Pool_("scin", 4, [128, SCS + 2], F32, cm)
            sc_b = Pool_("scb", 2, [128, SCS], F32, cm)
            sc_u = Pool_("scu", 2, [128, SCS + 2], F32, cm)
            sc_y = Pool_("scy", 2, [128, SCS], F32, cm)
            for b in range(NB):
                for sg in range(S // SCS):
                    t0 = b * S + sg * SCS
                    ct, cb_ = sc_in.get()
                    xt, xb_ = sc_in.get()
                    bt, bb_ = sc_b.get()
                    if sg == 0:
                        P.op("pool", lambda e, ct=ct: e.memset(ct[:, 0:2], 0.0), [], [cb_])
                        P.op("pool", lambda e, xt=xt: e.memset(xt[:, 0:2], 0.0), [], [xb_])
                        P.dma("sp", lambda e, ct=ct, t0=t0: e.dma_start(out=ct[:, 2:], in_=fmT[F_SCC, :, t0:t0 + SCS]), [], [cb_])
                        P.dma("sp", lambda e, xt=xt, t0=t0: e.dma_start(out=xt[:, 2:], in_=fmT[F_SCX, :, t0:t0 + SCS]), [], [xb_])
                    else:
                        P.dma("sp", lambda e, ct=ct, t0=t0: e.dma_start(out=ct[:, :], in_=fmT[F_SCC, :, t0 - 2:t0 + SCS]), [], [cb_])
                        P.dma("sp", lambda e, xt=xt, t0=t0: e.dma_start(out=xt[:, :], in_=fmT[F_SCX, :, t0 - 2:t0 + SCS]), [], [xb_])
                    P.dma("sp", lambda e, bt=bt, t0=t0: e.dma_start(out=bt[:, :], in_=fmT[F_SCB, :, t0:t0 + SCS]), [], [bb_])
                    ut, ub_ = sc_u.get()
                    yt, yb_ = sc_y.get()
                    P.op("pool", lambda e, ut=ut, ct=ct, xt=xt: e.tensor_tensor(out=ut[:], in0=ct[:], in1=xt[:], op=ALU.mult),
                         [cb_, xb_], [ub_])
                    P.op("dve", lambda e, yt=yt, ut=ut: e.tensor_scalar(
                        out=yt[:], in0=ut[:, 2:SCS + 2], scalar1=pcol[:, PC_SC + 2:PC_SC + 3], scalar2=None, op0=ALU.mult),
                        [ub_] + CB, [yb_])
                    for i in (1, 0):
                        P.op("dve", lambda e, yt=yt, ut=ut, i=i: e.scalar_tensor_tensor(
                            out=yt[:], in0=ut[:, i:SCS + i], scalar=pcol[:, PC_SC + i:PC_SC + i + 1], in1=yt[:],
                            op0=ALU.mult, op1=ALU.add), [ub_, yb_] + CB, [yb_])
                    P.op("pool", lambda e, yt=yt, bt=bt: e.tensor_tensor(out=yt[:], in0=yt[:], in1=bt[:], op=ALU.mult),
                         [yb_, bb_], [yb_])
                    P.dma("sp", lambda e, yt=yt, t0=t0: e.dma_start(out=outB[2, :, t0:t0 + SCS], in_=yt[:]), [yb_], [])
            P.barrier()
            P.emit(final=False)

        tmpn = Pool_("tmpn", 2, [128, SEG], F32)
        rsn = Pool_("rsn", 2, [128, SEG], F32)

        def headnorm(src, srcb, n, onesmat, inv_d, out, outb, gain_ap, post_scale=1.0):
            sqt, sqb = tmpn.get()
            P.op("act", lambda e: e.activation(out=sqt[:, :n], in_=src, func=AF.Square), [srcb], [sqb])
            rt, rb = rsn.get()
            for h0 in range(0, n, 512):
                w_ = min(512, n - h0)
                pt, pb = getps()
                P.op("pe", lambda e, pt=pt, h0=h0, w_=w_: e.matmul(pt[:, :w_], onesmat, sqt[:, h0:h0 + w_], start=True, stop=True),
                     [sqb] + CB, [pb])
                P.op("dve", lambda e, pt=pt, h0=h0, w_=w_: e.tensor_scalar(
                    out=rt[:, h0:h0 + w_], in0=pt[:, :w_], scalar1=inv_d, scalar2=EPS, op0=ALU.mult, op1=ALU.add),
                    [pb], [rb])
            P.op("act", lambda e: e.activation(out=rt[:, :n], in_=rt[:, :n], func=AF.Sqrt), [rb], [rb])
            P.op("dve", lambda e: e.reciprocal(out=rt[:, :n], in_=rt[:, :n]), [rb], [rb])
            if gain_ap is None:
                P.op("dve", lambda e: e.scalar_tensor_tensor(out=out, in0=src, scalar=post_scale, in1=rt[:, :n],
                                                             op0=ALU.mult, op1=ALU.mult), [srcb, rb], [outb])
            else:
                P.op("dve", lambda e: e.scalar_tensor_tensor(out=out, in0=src, scalar=gain_ap, in1=rt[:, :n],
                                                             op0=ALU.mult, op1=ALU.mult), [srcb, rb] + CB, [outb])

        segin = Pool_("segin", 3, [128, SEG + 3], F32)
        ms2 = ExitStack()
        cm = Ctx(nc, ms2)
        qn = cm.sb([128, S], BF16, "qn")
        kn = cm.sb([128, S], BF16, "kn")
        kn2 = cm.sb([128, S], BF16, "kn2")
        b_kn2 = Buf()
        vt = cm.sb([128, NT * 128], BF16, "vt")
        b_qn, b_kn, b_vt = Buf(), Buf(), Buf()
        exp_ = Pool_("ex", 3, [128, 512], BF16, cm)

        if "swa" in do:
            sw_o = Pool_("swo", 2, [64, 2, SEG], F32, cm)
            sw_d = Pool_("swd", 2, [64, 256], F32, cm)
            esk = cm.sb([64, 2], F32, "esk")
            eskb = Buf()
            for b in range(NB):
                for sg in range(S // SEG):
                    t0 = b * S + sg * SEG
                    for (fi, dst, dstb, gcolid) in ((F_SQ, qn, b_qn, PC_SQG), (F_SK, kn, b_kn, PC_SKG), (F_SKB, kn2, b_kn2, PC_SKG)):
                        it, ib = segin.get()
                        P.dma("sp", lambda e, it=it, fi=fi, t0=t0: e.dma_start(out=it[:, :SEG], in_=fmT[fi, :, t0:t0 + SEG]), [], [ib])
                        headnorm(it[:, :SEG], ib, SEG, cf(C_BLK64), 1.0 / 64, dst[:, sg * SEG:(sg + 1) * SEG], dstb,
                                 pcol[:, gcolid:gcolid + 1])
                P.dma("pool", lambda e, b=b: e.dma_start(
                    out=vt[:, 0:NT * 64].rearrange("p (n d) -> p n d", d=64),
                    in_=sv_tm[b * S:(b + 1) * S, :].rearrange("(n p) d -> p n d", p=128)), [], [b_vt])
                P.op("act", lambda e, esk=esk: e.activation(out=esk[:, 0:2], in_=pcol[0:64, PC_SINK:PC_SINK + 2], func=AF.Exp),
                     CB, [eskb])
                exprev = None
                ot = None
                for kt in range(NT):
                    nq = 2 if kt + 1 < NT else 1
                    pt, pb = getps()
                    for qh in range(nq):
                        for j in range(2):
                            col = (qh * 2 + j) * 128
                            kk_ = kn if j == 0 else kn2
                            P.op("pe", lambda e, pt=pt, col=col, kk_=kk_, kt=kt, qh=qh: e.matmul(
                                pt[:, col:col + 128], kk_[:, kt * 128:(kt + 1) * 128],
                                qn[:, (kt + qh) * 128:(kt + qh + 1) * 128], start=True, stop=False),
                                [b_kn, b_kn2, b_qn], [pb])
                            P.op("pe", lambda e, pt=pt, col=col, qh=qh, j=j: e.matmul(
                                pt[:, col:col + 128], cbf[:, B_ID:B_ID + 128],
                                cbf[:, B_SWAM + qh * 256 + j * 128:B_SWAM + qh * 256 + (j + 1) * 128], start=False, stop=True),
                                CB, [pb])
                    ex, exb = exp_.get()
                    P.op("act", lambda e, ex=ex, pt=pt, nq=nq: e.activation(out=ex[:, :nq * 256], in_=pt[:, :nq * 256], func=AF.Exp, scale=0.125),
                         [pb], [exb])
                    po, pob = getacc((kt % 2) * 2)
                    pd, pdb = getacc((kt % 2) * 2 + 1)
                    srcs = []
                    if kt > 0:
                        srcs.append((kt - 1, exprev[0], exprev[1], 256))
                    srcs.append((kt, ex, exb, 0))
                    for si, (ktile, ext, extb, c0) in enumerate(srcs):
                        last = si == len(srcs) - 1
                        P.op("pe", lambda e, po=po, ktile=ktile, ext=ext, c0=c0, si=si, last=last: e.matmul(
                            po[0:64, 0:256], vt[:, ktile * 64:(ktile + 1) * 64], ext[:, c0:c0 + 256], start=(si == 0), stop=last),
                            [b_vt, extb], [pob])
                        P.op("pe", lambda e, pd=pd, ext=ext, c0=c0, si=si, last=last: e.matmul(
                            pd[0:64, 0:256], cbf[:, B_ONES:B_ONES + 64], ext[:, c0:c0 + 256], start=(si == 0), stop=last),
                            [extb] + CB, [pdb])
                    exprev = (ex, exb)
                    dt_, dtb_ = sw_d.get()
                    for j in range(2):
                        P.op("dve", lambda e, dt_=dt_, pd=pd, j=j, esk=esk: e.tensor_scalar(
                            out=dt_[:, j * 128:(j + 1) * 128], in0=pd[0:64, j * 128:(j + 1) * 128],
                            scalar1=esk[:, j:j + 1], scalar2=None, op0=ALU.add), [pdb, eskb], [dtb_])
                    P.op("dve", lambda e, dt_=dt_: e.reciprocal(out=dt_[:, :], in_=dt_[:, :]), [dtb_], [dtb_])
                    kk = kt % 8
                    if kk == 0:
                        ot, otb = sw_o.get()
                    P.op("dve", lambda e, ot=ot, po=po, dt_=dt_, kk=kk: e.tensor_tensor(
                        out=ot[:, :, kk * 128:(kk + 1) * 128], in0=po[0:64, 0:256].rearrange("p (j q) -> p j q", j=2),
                        in1=dt_[:, :].rearrange("p (j q) -> p j q", j=2), op=ALU.mult), [pob, dtb_], [otb])
                    if kk == 7:
                        t0 = b * S + (kt - 7) * 128
                        for j in range(2):
                            P.dma("sp", lambda e, ot=ot, j=j, t0=t0: e.dma_start(
                                out=outB[3, j * 64:(j + 1) * 64, t0:t0 + SEG], in_=ot[:, j, :]), [otb], [])

        if "moba" in do:
            NBK = S // 256
            kmean = cm.sb([128, 32], F32, "kmean")
            b_kmean = Buf()
            P.op("pool", lambda e: e.memset(kmean[:, :], 0.0), [], [b_kmean])
            gm = Pool_("gm", 2, [128, 128], F32, cm)
            for t_, b__ in zip(gm.t, gm.b):
                P.op("pool", lambda e, t_=t_: e.memset(t_[:, :], 0.0), [], [b__])
            top8 = Pool_("top8", 2, [128, 8], F32, cm)
            selT = Pool_("selT", 2, [128, 512], BF16, cm)
            for t_, b__ in zip(selT.t, selT.b):
                P.op("pool", lambda e, t_=t_: e.memset(t_[:, :], 0.0), [], [b__])
            qseg32 = Pool_("qseg32", 2, [128, SEG], F32, cm)
            mo_o = Pool_("moo", 2, [128, 512], F32, cm)
            mo_r = Pool_("mor", 2, [128, 512], F32, cm)
            gate_sb = cm.sb([128, NT * 32], F32, "gate_sb")
            b_gate = [Buf() for _ in range(NT)]
            scale = 128 ** -0.5
            for b in range(NB):
                P.dma("pool", lambda e, b=b: e.dma_start(
                    out=vt[:, :].rearrange("p (n d) -> p n d", d=128),
                    in_=mv_tm[b * S:(b + 1) * S, :].rearrange("(n p) d -> p n d", p=128)), [], [b_vt])
                for sg in range(S // SEG):
                    t0 = b * S + sg * SEG
                    it, ib = segin.get()
                    P.dma("sp", lambda e, it=it, t0=t0: e.dma_start(out=it[:, :SEG], in_=fmT[F_MK, :, t0:t0 + SEG]), [], [ib])
                    kt32, kb32 = qseg32.get()
                    headnorm(it[:, :SEG], ib, SEG, ones_f, 1.0 / 128, kt32[:, :], kb32, pcol[:, PC_MKG:PC_MKG + 1])
                    P.op("act", lambda e, kt32=kt32, sg=sg: e.copy(out=kn[:, sg * SEG:(sg + 1) * SEG], in_=kt32[:, :]), [kb32], [b_kn])
                    nbs = SEG // 256
                    P.op("dve", lambda e, kt32=kt32, sg=sg: e.tensor_reduce(
                        out=kmean[:, sg * nbs:(sg + 1) * nbs], in_=kt32[:, :].rearrange("p (n k) -> p n k", k=256),
                        axis=mybir.AxisListType.X, op=ALU.add), [kb32], [b_kmean])
                P.op("dve", lambda e: e.tensor_scalar(out=kmean[:, :NBK], in0=kmean[:, :NBK], scalar1=1.0 / 256, scalar2=None,
                                                      op0=ALU.mult), [b_kmean], [b_kmean])
                for sg in range(S // SEG):
                    t0 = b * S + sg * SEG
                    it, ib = segin.get()
                    P.dma("sp", lambda e, it=it, t0=t0: e.dma_start(out=it[:, :SEG], in_=fmT[F_MQ, :, t0:t0 + SEG]), [], [ib])
                    qt32, qb32 = qseg32.get()
                    headnorm(it[:, :SEG], ib, SEG, ones_f, 1.0 / 128, qt32[:, :], qb32, pcol[:, PC_MQG:PC_MQG + 1])
                    P.op("act", lambda e, qt32=qt32, sg=sg: e.copy(out=qn[:, sg * SEG:(sg + 1) * SEG], in_=qt32[:, :]), [qb32], [b_qn])
                    for ti in range(SEG // 128):
                        qt_i = sg * (SEG // 128) + ti
                        pt, pb = getps()
                        P.op("pe", lambda e, pt=pt, qt32=qt32, ti=ti: e.matmul(
                            pt[:, 0:32], qt32[:, ti * 128:(ti + 1) * 128], kmean[:, 0:32], start=True, stop=True),
                            [qb32, b_kmean], [pb])
                        blk = qt_i // 2
                        P.op("dve", lambda e, pt=pt, qt_i=qt_i, blk=blk: e.tensor_tensor(
                            out=gate_sb[:, qt_i * 32:(qt_i + 1) * 32], in0=pt[:, 0:32], in1=negpast[:, blk * 32:(blk + 1) * 32],
                            op=ALU.add), [pb] + CB, [b_gate[qt_i]])
                for B in range(S // 256):
                    st_, stb = selT.get()
                    for ti in range(2):
                        qt_i = 2 * B + ti
                        g_ap = gate_sb[:, qt_i * 32:(qt_i + 1) * 32]
                        t8, t8b = top8.get()
                        P.op("dve", lambda e, t8=t8, g_ap=g_ap: e.max(out=t8[:, :], in_=g_ap), [b_gate[qt_i]], [t8b])
                        P.op("dve", lambda e, t8=t8: e.tensor_scalar(out=t8[:, 2:3], in0=t8[:, 2:3], scalar1=-1e29, scalar2=None,
                                                                     op0=ALU.max), [t8b], [t8b])
                        gt_, gtb = gm.get()
                        P.op("dve", lambda e, gt_=gt_, g_ap=g_ap, t8=t8: e.tensor_scalar(
                            out=gt_[:, 0:32], in0=g_ap, scalar1=t8[:, 2:3], scalar2=None, op0=ALU.is_ge), [b_gate[qt_i], t8b], [gtb])
                        P.op("dve", lambda e, gt_=gt_: e.tensor_scalar(
                            out=gt_[:, 0:32], in0=gt_[:, 0:32], scalar1=-NEG, scalar2=NEG, op0=ALU.mult, op1=ALU.add), [gtb], [gtb])
                        pt, pb = getps()
                        P.op("pe", lambda e, pt=pt, gt_=gt_: e.transpose(pt[:, 0:128], gt_[:, :], ident_f), [gtb] + CB, [pb])
                        P.op("act", lambda e, st_=st_, pt=pt, ti=ti: e.copy(out=st_[0:32, ti * 128:(ti + 1) * 128], in_=pt[0:32, 0:128]),
                             [pb], [stb])
                    po, pob = getacc((B % 2) * 2)
                    pd, pdb = getacc((B % 2) * 2 + 1)
                    nkt = 2 * B + 2
                    for kt in range(nkt):
                        n = kt // 2
                        pt, pb = getps()
                        P.op("pe", lambda e, pt=pt, kt=kt, B=B: e.matmul(
                            pt[:, 0:256], kn[:, kt * 128:(kt + 1) * 128], qn[:, B * 256:(B + 1) * 256], start=True, stop=False),
                            [b_kn, b_qn], [pb])
                        if n < B:
                            P.op("pe", lambda e, pt=pt, n=n, st_=st_: e.matmul(
                                pt[:, 0:256], cbf[:, B_ESEL + n * 128:B_ESEL + (n + 1) * 128], st_[:, 0:256], start=False, stop=True),
                                [stb] + CB, [pb])
                        else:
                            v = kt % 2
                            P.op("pe", lambda e, pt=pt, v=v: e.matmul(
                                pt[:, 0:256], cbf[:, B_ID:B_ID + 128], cbf[:, B_CM + v * 256:B_CM + (v + 1) * 256], start=False, stop=True),
                                CB, [pb])
                        ex, exb = exp_.get()
                        P.op("act", lambda e, ex=ex, pt=pt: e.activation(out=ex[:, 0:256], in_=pt[:, 0:256], func=AF.Exp, scale=scale),
                             [pb], [exb])
                        P.op("pe", lambda e, po=po, kt=kt, ex=ex, nkt=nkt: e.matmul(
                            po[:, 0:256], vt[:, kt * 128:(kt + 1) * 128], ex[:, 0:256], start=(kt == 0), stop=(kt == nkt - 1)),
                            [b_vt, exb], [pob])
                        P.op("pe", lambda e, pd=pd, ex=ex, kt=kt, nkt=nkt: e.matmul(
                            pd[:, 0:256], cbf[:, B_ONES:B_ONES + 128], ex[:, 0:256], start=(kt == 0), stop=(kt == nkt - 1)),
                            [exb] + CB, [pdb])
                    rt, rb = mo_r.get()
                    P.op("dve", lambda e, rt=rt, pd=pd: e.reciprocal(out=rt[:, 0:256], in_=pd[:, 0:256]), [pdb], [rb])
                    ot, otb = mo_o.get()
                    P.op("dve", lambda e, ot=ot, po=po, rt=rt: e.tensor_tensor(out=ot[:, 0:256], in0=po[:, 0:256], in1=rt[:, 0:256], op=ALU.mult),
                         [pob, rb], [otb])
                    t0 = b * S + B * 256
                    P.dma("sp", lambda e, ot=ot, t0=t0: e.dma_start(out=outB[0, :, t0:t0 + 256], in_=ot[:, 0:256]), [otb], [])

        P.barrier()
        P.emit(final=False)
        ms2.close()
        if "gdn" in do:
            ms3 = ExitStack()
            cm = Ctx(nc, ms3)
            NTT = TT // 128
            gab = cm.sb([128, 2, NTT], F32, "gab")
            gall = cm.sb([128, NTT], F32, "gall")
            ball = cm.sb([128, NTT], F32, "ball")
            nball = cm.sb([128, NTT], F32, "nball")
            negA = cm.sb([128, 1], F32, "negA")
            b_g = Buf()
            P.dma("sp", lambda e: e.dma_start(out=gab[:], in_=gab_tm[:, :, :]), [], [b_g])
            P.op("act", lambda e: e.activation(out=negA[:, :], in_=pcol[:, PC_ALOG:PC_ALOG + 1], func=AF.Exp), CB, [b_g])
            P.op("dve", lambda e: e.tensor_scalar(out=negA[:, :], in0=negA[:, :], scalar1=-1.0, scalar2=None, op0=ALU.mult), [b_g], [b_g])
            P.op("act", lambda e: e.activation(out=gall[:, :], in_=gab[:, 0, :], func=AF.Exp, bias=pcol[:, PC_DTB:PC_DTB + 1]),
                 [b_g] + CB, [b_g])
            P.op("act", lambda e: e.activation(out=gall[:, :], in_=gall[:, :], func=AF.Ln, bias=cst[:, C_ONES * 128:C_ONES * 128 + 1]),
                 [b_g] + CB, [b_g])
            P.op("dve", lambda e: e.tensor_scalar(out=gall[:, :], in0=gall[:, :], scalar1=negA[:, 0:1], scalar2=None, op0=ALU.mult),
                 [b_g], [b_g])
            P.op("act", lambda e: e.activation(out=ball[:, :], in_=gab[:, 1, :], func=AF.Sigmoid), [b_g], [b_g])
            P.op("dve", lambda e: e.tensor_scalar(out=nball[:, :], in0=ball[:, :], scalar1=-1.0, scalar2=None, op0=ALU.mult),
                 [b_g], [b_g])

            Sst = cm.sb([128, 128], F32, "Sst")
            b_S = Buf()
            cv = Pool_("cv", 2, [128, SEG], F32, cm)
            qs = Pool_("gqs", 2, [128, SEG], F32, cm)
            ks = Pool_("gks", 2, [128, SEG], F32, cm)
            vs = Pool_("gvs", 2, [128, SEG], F32, cm)
            zt = Pool_("gzt", 2, [128, SEG // 128, 128], F32, cm)
            og = Pool_("gog", 2, [128, SEG], F32, cm)
            m128 = {}
            for nm, n_ in (("gbc", 2), ("egb", 2), ("arg", 2), ("dst", 2), ("dti", 2), ("A", 2), ("aa", 4), ("yy", 4),
                           ("qk", 2), ("qd", 2), ("kd", 2), ("vb", 2), ("R", 2), ("vn", 2), ("ob", 2), ("junk", 2)):
                m128[nm] = Pool_("g_" + nm, n_, [128, 256 if nm == "aa" else 128], F32, cm)
            cols = Pool_("g_cols", 3, [128, 8], F32, cm)

            for b in range(NB):
                P.op("pool", lambda e: e.memset(Sst[:, :], 0.0), [], [b_S])
                for sg in range(S // SEG):
                    t0 = b * S + sg * SEG
                    segs = {}
                    for (nm, fi, pc, pool_) in (("q", F_GQ, PC_CQ, qs), ("k", F_GK, PC_CK, ks), ("v", F_GV, PC_CV, vs)):
                        it, ib = segin.get()
                        if sg == 0:
                            P.op("pool", lambda e, it=it: e.memset(it[:, 0:3], 0.0), [], [ib])
                            P.dma("sp", lambda e, it=it, fi=fi, t0=t0: e.dma_start(out=it[:, 3:], in_=fmT[fi, :, t0:t0 + SEG]), [], [ib])
                        else:
                            P.dma("sp", lambda e, it=it, fi=fi, t0=t0: e.dma_start(out=it[:, :], in_=fmT[fi, :, t0 - 3:t0 + SEG]), [], [ib])
                        ct_, cb2 = cv.get()
                        P.op("dve", lambda e, ct_=ct_, it=it, pc=pc: e.tensor_scalar(
                            out=ct_[:, :], in0=it[:, 3:SEG + 3], scalar1=pcol[:, pc + 3:pc + 4], scalar2=None, op0=ALU.mult),
                            [ib] + CB, [cb2])
                        for i in (2, 1, 0):
                            P.op("dve", lambda e, ct_=ct_, it=it, pc=pc, i=i: e.scalar_tensor_tensor(
                                out=ct_[:, :], in0=it[:, i:SEG + i], scalar=pcol[:, pc + i:pc + i + 1], in1=ct_[:, :],
                                op0=ALU.mult, op1=ALU.add), [ib, cb2] + CB, [cb2])
                        dt2, db2 = pool_.get()
                        if nm == "v":
                            P.op("act", lambda e, dt2=dt2, ct_=ct_: e.activation(out=dt2[:, :], in_=ct_[:, :], func=AF.Silu), [cb2], [db2])
                        else:
                            P.op("act", lambda e, ct_=ct_: e.activation(out=ct_[:, :], in_=ct_[:, :], func=AF.Silu), [cb2], [cb2])
                            headnorm(ct_[:, :], cb2, SEG, ones_f, 1.0, dt2[:, :], db2, None,
                                     post_scale=(128 ** -0.5 if nm == "q" else 1.0))
                        segs[nm] = (dt2, db2)
                    ztile, zb = zt.get()
                    P.dma("sp", lambda e, ztile=ztile, t0=t0: e.dma_start(
                        out=ztile[:, :, :], in_=gz_tm[t0:t0 + SEG, :].rearrange("(n p) d -> p n d", p=128)), [], [zb])
                    P.op("act", lambda e, ztile=ztile: e.activation(out=ztile[:, :, :], in_=ztile[:, :, :], func=AF.Silu), [zb], [zb])
                    ogt, ogb = og.get()
                    qT_, qb_ = segs["q"]
                    kT_, kb_ = segs["k"]
                    vT_, vb_ = segs["v"]
                    for ci in range(SEG // 128):
                        gi = (t0 // 128) + ci
                        cs = slice(ci * 128, (ci + 1) * 128)
                        gcol = gall[:, gi:gi + 1]
                        bcol = ball[:, gi:gi + 1]
                        nbcol = nball[:, gi:gi + 1]
                        gbc, gbcb = m128["gbc"].get()
                        P.op("dve", lambda e, gbc=gbc, gcol=gcol: e.tensor_scalar(out=gbc[:, :], in0=ones_f, scalar1=gcol, scalar2=None,
                                                                                 op0=ALU.mult), [b_g] + CB, [gbcb])
                        pG, pGb = getps()
                        P.op("pe", lambda e, pG=pG, gbc=gbc: e.matmul(pG[:, 0:128], gbc[:, :], cf(C_TRI), start=True, stop=True),
                             [gbcb] + CB, [pGb])
                        P.op("pe", lambda e, pG=pG, gcol=gcol: e.matmul(pG[:, 128:129], cf(C_TRI), gcol, start=True, stop=True),
                             [b_g] + CB, [pGb])
                        cl, clb = cols.get()
                        P.op("dve", lambda e, cl=cl, pG=pG: e.tensor_copy(out=cl[:, 0:1], in_=pG[:, 128:129]), [pGb], [clb])
                        P.op("dve", lambda e, cl=cl, pG=pG: e.tensor_scalar(out=cl[:, 1:2], in0=pG[:, 128:129], scalar1=-1.0, scalar2=None,
                                                                           op0=ALU.mult), [pGb], [clb])
                        P.op("dve", lambda e, cl=cl, pG=pG: e.tensor_copy(out=cl[:, 2:3], in_=pG[:, 127:128]), [pGb], [clb])
                        egb, egbb = m128["egb"].get()
                        P.op("act", lambda e, egb=egb, pG=pG: e.activation(out=egb[:, :], in_=pG[:, 0:128], func=AF.Exp), [pGb], [egbb])
                        arg, argb = m128["arg"].get()
                        P.op("dve", lambda e, arg=arg, pG=pG: e.scalar_tensor_tensor(
                            out=arg[:, :], in0=pG[:, 0:128], scalar=-1.0, in1=cf(C_NEGLS), op0=ALU.mult, op1=ALU.add), [pGb] + CB, [argb])
                        dst, dstb = m128["dst"].get()
                        P.op("act", lambda e, dst=dst, arg=arg, cl=cl: e.activation(out=dst[:, :], in_=arg[:, :], func=AF.Exp, bias=cl[:, 0:1]),
                             [argb, clb], [dstb])
                        arg2, arg2b = m128["arg"].get()
                        P.op("dve", lambda e, arg2=arg2, pG=pG: e.tensor_tensor(out=arg2[:, :], in0=pG[:, 0:128], in1=cf(C_NEGU), op=ALU.add),
                             [pGb] + CB, [arg2b])
                        dti, dtib = m128["dti"].get()
                        P.op("act", lambda e, dti=dti, arg2=arg2, cl=cl: e.activation(out=dti[:, :], in_=arg2[:, :], func=AF.Exp, bias=cl[:, 1:2]),
                             [arg2b, clb], [dtib])
                        P.op("act", lambda e, cl=cl: e.activation(out=cl[:, 3:4], in_=cl[:, 0:1], func=AF.Exp, scale=-1.0, bias=cl[:, 2:3]),
                             [clb], [clb])
                        P.op("act", lambda e, cl=cl: e.activation(out=cl[:, 4:5], in_=cl[:, 0:1], func=AF.Exp), [clb], [clb])
                        P.op("dve", lambda e, cl=cl, bcol=bcol: e.tensor_scalar(out=cl[:, 5:6], in0=cl[:, 4:5], scalar1=bcol, scalar2=-1.0,
                                                                               op0=ALU.mult, op1=ALU.mult), [clb, b_g], [clb])
                        pK, pKb = getps()
                        P.op("pe", lambda e, pK=pK, kT_=kT_, cs=cs: e.matmul(pK[:, 0:128], kT_[:, cs], kT_[:, cs], start=True, stop=True),
                             [kb_], [pKb])
                        aa, aab = m128["aa"].get()
                        P.op("dve", lambda e, aa=aa, pK=pK, nbcol=nbcol, dst=dst: e.scalar_tensor_tensor(
                            out=aa[:, 0:128], in0=pK[:, 0:128], scalar=nbcol, in1=dst[:, :], op0=ALU.mult, op1=ALU.mult),
                            [pKb, b_g, dstb], [aab])
                        P.op("pe", lambda e, pK=pK, aa=aa: e.transpose(pK[:, 128:256], aa[:, 0:128], ident_f), [aab] + CB, [pKb])
                        P.op("act", lambda e, aa=aa, pK=pK: e.copy(out=aa[:, 128:256], in_=pK[:, 128:256]), [pKb], [aab])
                        yy, yyb = m128["yy"].get()
                        P.op("dve", lambda e, yy=yy, aa=aa: e.tensor_tensor(out=yy[:, :], in0=aa[:, 128:256], in1=ident_f, op=ALU.add),
                             [aab] + CB, [yyb])
                        for s_ in range(0, 7):
                            pL, pLb = getps()
                            if s_ <= 5:
                                P.op("pe", lambda e, pL=pL, aa=aa: e.matmul(pL[:, 0:128], aa[:, 128:256], aa[:, 0:128], start=True, stop=True),
                                     [aab], [pLb])
                            if s_ <= 4:
                                P.op("pe", lambda e, pL=pL, aa=aa: e.matmul(pL[:, 128:256], aa[:, 0:128], aa[:, 128:256], start=True, stop=True),
                                     [aab], [pLb])
                            if s_ >= 1:
                                P.op("pe", lambda e, pL=pL, aa=aa, yy=yy: e.matmul(pL[:, 256:384], aa[:, 0:128], yy[:, :], start=True, stop=True),
                                     [aab, yyb], [pLb])
                                yy2, yy2b = m128["yy"].get()
                                P.op("dve", lambda e, yy2=yy2, yy=yy, pL=pL: e.tensor_tensor(out=yy2[:, :], in0=pL[:, 256:384], in1=yy[:, :],
                                                                                          op=ALU.add), [pLb, yyb], [yy2b])
                                yy, yyb = yy2, yy2b
                            if s_ <= 5:
                                aa2, aa2b = m128["aa"].get()
                                w_ = 256 if s_ <= 4 else 128
                                P.op("act", lambda e, aa2=aa2, pL=pL, w_=w_: e.copy(out=aa2[:, 0:w_], in_=pL[:, 0:w_]), [pLb], [aa2b])
                                aa, aab = aa2, aa2b
                        TTm, TTb = yy, yyb
                        pQ, pQb = getps()
                        P.op("pe", lambda e, pQ=pQ, kT_=kT_, qT_=qT_, cs=cs: e.matmul(pQ[:, 0:128], kT_[:, cs], qT_[:, cs], start=True, stop=True),
                             [kb_, qb_], [pQb])
                        P.op("pe", lambda e, pQ=pQ, kT_=kT_, cs=cs: e.transpose(pQ[:, 128:256], kT_[:, cs], ident_f), [kb_] + CB, [pQb])
                        P.op("pe", lambda e, pQ=pQ, vT_=vT_, cs=cs: e.transpose(pQ[:, 256:384], vT_[:, cs], ident_f), [vb_] + CB, [pQb])
                        qk, qkb = m128["qk"].get()
                        P.op("dve", lambda e, qk=qk, pQ=pQ, dti=dti: e.tensor_tensor(out=qk[:, :], in0=pQ[:, 0:128], in1=dti[:, :], op=ALU.mult),
                             [pQb, dtib], [qkb])
                        kd, kdb = m128["kd"].get()
                        P.op("dve", lambda e, kd=kd, pQ=pQ, cl=cl: e.tensor_scalar(out=kd[:, :], in0=pQ[:, 128:256], scalar1=cl[:, 3:4], scalar2=None,
                                                                                  op0=ALU.mult), [pQb, clb], [kdb])
                        vb2, vb2b = m128["vb"].get()
                        P.op("dve", lambda e, vb2=vb2, pQ=pQ, bcol=bcol: e.tensor_scalar(out=vb2[:, :], in0=pQ[:, 256:384], scalar1=bcol, scalar2=None,
                                                                                        op0=ALU.mult), [pQb, b_g], [vb2b])
                        qd, qdb = m128["qd"].get()
                        P.op("pool", lambda e, qd=qd, qT_=qT_, cs=cs, egb=egb: e.tensor_tensor(out=qd[:, :], in0=qT_[:, cs], in1=egb[:, :], op=ALU.mult),
                             [qb_, egbb], [qdb])
                        pS, pSb = getps()
                        pO, pOb = getps()
                        P.op("pe", lambda e, pS=pS, kT_=kT_, cs=cs: e.matmul(pS[:, 0:128], kT_[:, cs], Sst[:, :], start=True, stop=True),
                             [kb_, b_S], [pSb])
                        P.op("pe", lambda e, pO=pO, qd=qd: e.matmul(pO[:, 0:128], qd[:, :], Sst[:, :], start=True, stop=False),
                             [qdb, b_S], [pOb])
                        R, Rb = m128["R"].get()
                        P.op("dve", lambda e, R=R, pS=pS, cl=cl, vb2=vb2: e.scalar_tensor_tensor(
                            out=R[:, :], in0=pS[:, 0:128], scalar=cl[:, 5:6], in1=vb2[:, :], op0=ALU.mult, op1=ALU.add),
                            [pSb, clb, vb2b], [Rb])
                        P.op("pe", lambda e, pS=pS, TTm=TTm, R=R: e.matmul(pS[:, 128:256], TTm[:, :], R[:, :], start=True, stop=True),
                             [TTb, Rb], [pSb])
                        vn, vnb = m128["vn"].get()
                        P.op("act", lambda e, vn=vn, pS=pS: e.copy(out=vn[:, :], in_=pS[:, 128:256]), [pSb], [vnb])
                        P.op("pe", lambda e, pO=pO, qk=qk, vn=vn: e.matmul(pO[:, 0:128], qk[:, :], vn[:, :], start=False, stop=True),
                             [qkb, vnb], [pOb])
                        P.op("pe", lambda e, pS=pS, kd=kd, vn=vn: e.matmul(pS[:, 256:384], kd[:, :], vn[:, :], start=True, stop=True),
                             [kdb, vnb], [pSb])
                        P.op("dve", lambda e, pS=pS, egb=egb: e.scalar_tensor_tensor(
                            out=Sst[:, :], in0=Sst[:, :], scalar=egb[:, 127:128], in1=pS[:, 256:384], op0=ALU.mult, op1=ALU.add),
                            [pSb, egbb, b_S], [b_S])
                        jk, jkb = m128["junk"].get()
                        P.op("act", lambda e, jk=jk, pO=pO: e.activation(out=jk[:, :], in_=pO[:, 0:128], func=AF.Square), [pOb], [jkb])
                        P.op("dve", lambda e, cl=cl, jk=jk: e.tensor_reduce(out=cl[:, 6:7], in_=jk[:, :], axis=mybir.AxisListType.X, op=ALU.add),
                             [jkb], [clb])
                        P.op("dve", lambda e, cl=cl: e.tensor_scalar(out=cl[:, 6:7], in0=cl[:, 6:7], scalar1=1.0 / 128, scalar2=EPS,
                                                                     op0=ALU.mult, op1=ALU.add), [clb], [clb])
                        P.op("act", lambda e, cl=cl: e.activation(out=cl[:, 6:7], in_=cl[:, 6:7], func=AF.Sqrt), [clb], [clb])
                        P.op("dve", lambda e, cl=cl: e.reciprocal(out=cl[:, 6:7], in_=cl[:, 6:7]), [clb], [clb])
                        ob, obb = m128["ob"].get()
                        P.op("dve", lambda e, ob=ob, pO=pO, cl=cl: e.scalar_tensor_tensor(
                            out=ob[:, :], in0=pO[:, 0:128], scalar=cl[:, 6:7], in1=gnB[:, :], op0=ALU.mult, op1=ALU.mult),
                            [pOb, clb] + CB, [obb])
                        P.op("pool", lambda e, ob=ob, ztile=ztile, ci=ci: e.tensor_tensor(out=ob[:, :], in0=ob[:, :], in1=ztile[:, ci, :], op=ALU.mult),
                             [obb, zb], [obb])
                        P.op("pe", lambda e, pO=pO, ob=ob: e.transpose(pO[:, 128:256], ob[:, :], ident_f), [obb] + CB, [pOb])
                        P.op("act", lambda e, ogt=ogt, pO=pO, cs=cs: e.copy(out=ogt[:, cs], in_=pO[:, 128:256]), [pOb], [ogb])
                    P.dma("sp", lambda e, ogt=ogt, t0=t0: e.dma_start(out=outB[1, :, t0:t0 + SEG], in_=ogt[:, :]), [ogb], [])
            P.emit(final=True)
            ms3.close()
        else:
            P.emit(final=True)
    return nc
import numpy as np


def consts():
    i = np.arange(128)
    ident = np.eye(128, dtype=np.float32)
    ones = np.ones((128, 128), np.float32)
    tri = (i[:, None] <= i[None, :]).astype(np.float32)
    negls = np.where(i[None, :] < i[:, None], 0.0, NEG).astype(np.float32)
    negu = np.where(i[None, :] >= i[:, None], 0.0, NEG).astype(np.float32)
    blk64 = np.zeros((128, 128), np.float32)
    blk64[:64, :64] = 1
    blk64[64:, 64:] = 1
    cst = np.concatenate([ident, ones, tri, negls, negu, blk64], axis=1)
    negpast = np.zeros((128, 32, 32), np.float32)
    for blk in range(32):
        negpast[:, blk, blk:] = -1e30
    negpast = negpast.reshape(128, 1024)
    cbf = np.zeros((128, NCB16), np.float32)
    cbf[:, B_ID:B_ID + 128] = ident
    cbf[:, B_ONES:B_ONES + 128] = 1
    kl = i[:, None]
    ql = i[None, :]
    m0 = np.where(kl <= ql, 0.0, NEG)
    m1 = np.where(kl > ql, 0.0, NEG)
    cbf[:, B_SWAM:B_SWAM + 256] = np.concatenate([m0, m0], axis=1)
    cbf[:, B_SWAM + 256:B_SWAM + 512] = np.concatenate([m1, m1], axis=1)
    qib = np.arange(256)[None, :]
    for v in range(2):
        cbf[:, B_CM + v * 256:B_CM + (v + 1) * 256] = np.where(v * 128 + kl <= qib, 0.0, NEG)
    for n in range(32):
        cbf[n, B_ESEL + n * 128:B_ESEL + (n + 1) * 128] = 1
    return cst, negpast, cbf


def pack_b_inputs(proj, prm, c, S, NB):
    TT = NB * S
    sl = slice(c * 128, (c + 1) * 128)
    kvh = c // 4
    fm = np.zeros((NF, 128, TT), np.float32)
    fm[F_MQ] = proj["mq"][:, sl].T
    fm[F_MK] = proj["mk"][:, sl].T
    fm[F_GQ] = proj["gqkv"][:, c * 128:(c + 1) * 128].T
    fm[F_GK] = proj["gqkv"][:, 1024 + c * 128:1024 + (c + 1) * 128].T
    fm[F_GV] = proj["gqkv"][:, 2048 + c * 128:2048 + (c + 1) * 128].T
    fm[F_SCB] = proj["scb"][:, sl].T
    fm[F_SCC] = proj["scc"][:, sl].T
    fm[F_SCX] = proj["scx"][:, sl].T
    fm[F_SQ] = proj["sq"][:, sl].T
    skh = proj["sk"][:, kvh * 64:(kvh + 1) * 64].T
    fm[F_SK] = np.concatenate([skh, 0 * skh], axis=0)
    fm[F_SKB] = np.concatenate([0 * skh, skh], axis=0)
    pc = np.zeros((128, NPC), np.float32)
    pc[:, PC_MQG] = prm["moba_q_norm"]
    pc[:, PC_MKG] = prm["moba_k_norm"]
    gc = prm["gdn_conv"]
    pc[:, PC_CQ:PC_CQ + 4] = gc[:, c * 128:(c + 1) * 128].T
    pc[:, PC_CK:PC_CK + 4] = gc[:, 1024 + c * 128:1024 + (c + 1) * 128].T
    pc[:, PC_CV:PC_CV + 4] = gc[:, 2048 + c * 128:2048 + (c + 1) * 128].T
    pc[:, PC_ALOG] = prm["gdn_a_log"][c]
    pc[:, PC_DTB] = prm["gdn_dt_bias"][c]
    pc[:, PC_SC:PC_SC + 3] = prm["sc_conv"][:, sl].T
    pc[:, PC_SQG] = np.concatenate([prm["swa_q_norm"], prm["swa_q_norm"]])
    pc[:, PC_SKG] = np.concatenate([prm["swa_k_norm"], prm["swa_k_norm"]])
    pc[:, PC_SINK] = prm["swa_sinks"][2 * c]
    pc[:, PC_SINK + 1] = prm["swa_sinks"][2 * c + 1]
    gab = np.stack([proj["ga"][:, c].reshape(TT // 128, 128).T, proj["gb"][:, c].reshape(TT // 128, 128).T], axis=1)
    cst, negpast, cbf = consts()
    return {
        "fmT": fm,
        "mv_tm": np.ascontiguousarray(proj["mv"][:, sl]),
        "sv_tm": np.ascontiguousarray(proj["sv"][:, kvh * 64:(kvh + 1) * 64]),
        "gz_tm": np.ascontiguousarray(proj["gz"][:, sl]),
        "gab_tm": np.ascontiguousarray(gab),
        "pcol": pc,
        "gnB": np.broadcast_to(prm["gdn_out_norm"][None, :], (128, 128)).copy(),
        "cst": cst, "negpast": negpast, "cbf": cbf,
    }


N_CORES = 8
D_MODEL = 4096
SEQ = 8192
BATCH = 2
NTOK = BATCH * SEQ
TPC = NTOK // N_CORES
NCB_IN = 91
IN_PERM = np.concatenate([np.arange(0, 6144), np.arange(6160, 11536), np.arange(6144, 6160)])
R_MQ, R_MK, R_MV, R_GQ, R_GK, R_GV, R_GZ, R_SCB, R_SCC, R_SCX, R_SQ, R_SK, R_SV, R_GA, R_GB = (
    0, 1024, 2048, 3072, 4096, 5120, 6144, 7168, 8192, 9216, 10240, 11264, 11392, 11520, 11528)

_PROGS = {}


def _prog(name):
    if name not in _PROGS:
        if name == "A":
            _PROGS[name] = build_stage_a(TPC, NCB_IN)
        elif name == "B":
            _PROGS[name] = build_stage_b(SEQ, BATCH)
        else:
            _PROGS[name] = build_stage_c(TPC)
    return _PROGS[name]


def _shard_fm(aT, i):
    return np.ascontiguousarray(aT[:, i * TPC:(i + 1) * TPC]).reshape(KC, 128, TPC)


def _b_inputs(projT, l, c, P_):
    TT = NTOK
    kvh = c // 4
    r = lambda base: projT[base + c * 128: base + (c + 1) * 128]
    fm = np.empty((NF, 128, TT), np.float32)
    fm[F_MQ] = r(R_MQ)
    fm[F_MK] = r(R_MK)
    fm[F_GQ] = r(R_GQ)
    fm[F_GK] = r(R_GK)
    fm[F_GV] = r(R_GV)
    fm[F_SCB] = r(R_SCB)
    fm[F_SCC] = r(R_SCC)
    fm[F_SCX] = r(R_SCX)
    fm[F_SQ] = r(R_SQ)
    skh = projT[R_SK + kvh * 64: R_SK + (kvh + 1) * 64]
    fm[F_SK] = 0.0
    fm[F_SKB] = 0.0
    fm[F_SK, 0:64] = skh
    fm[F_SKB, 64:128] = skh
    pc = np.zeros((128, NPC), np.float32)
    pc[:, PC_MQG] = P_["moba_q_norm"][l]
    pc[:, PC_MKG] = P_["moba_k_norm"][l]
    gc = P_["gdn_conv"][l]
    pc[:, PC_CQ:PC_CQ + 4] = gc[:, c * 128:(c + 1) * 128].T
    pc[:, PC_CK:PC_CK + 4] = gc[:, 1024 + c * 128:1024 + (c + 1) * 128].T
    pc[:, PC_CV:PC_CV + 4] = gc[:, 2048 + c * 128:2048 + (c + 1) * 128].T
    pc[:, PC_ALOG] = P_["gdn_a_log"][l][c]
    pc[:, PC_DTB] = P_["gdn_dt_bias"][l][c]
    pc[:, PC_SC:PC_SC + 3] = P_["sc_conv"][l][:, c * 128:(c + 1) * 128].T
    pc[0:64, PC_SQG] = P_["swa_q_norm"][l]
    pc[64:128, PC_SQG] = P_["swa_q_norm"][l]
    pc[0:64, PC_SKG] = P_["swa_k_norm"][l]
    pc[64:128, PC_SKG] = P_["swa_k_norm"][l]
    pc[:, PC_SINK] = P_["swa_sinks"][l][2 * c]
    pc[:, PC_SINK + 1] = P_["swa_sinks"][l][2 * c + 1]
    gab = np.stack([projT[R_GA + c].reshape(TT // 128, 128).T, projT[R_GB + c].reshape(TT // 128, 128).T], axis=1)
    cst, negpast, cbf = consts()
    return {
        "fmT": fm,
        "mv_tm": np.ascontiguousarray(r(R_MV).T),
        "sv_tm": np.ascontiguousarray(projT[R_SV + kvh * 64: R_SV + (kvh + 1) * 64].T),
        "gz_tm": np.ascontiguousarray(r(R_GZ).T),
        "gab_tm": np.ascontiguousarray(gab),
        "pcol": pc,
        "gnB": np.ascontiguousarray(np.broadcast_to(P_["gdn_out_norm"][l][None, :], (128, 128))),
        "cst": cst, "negpast": negpast, "cbf": cbf,
    }


def kernel(x, norm_mix, w_in, moba_q_norm, moba_k_norm, gdn_conv, gdn_a_log, gdn_dt_bias,
           gdn_out_norm, sc_conv, swa_q_norm, swa_k_norm, swa_sinks, w_out, norm_ffn,
           w_gate, w_up, w_down):
    f32 = np.float32
    P_ = {k: np.asarray(v, f32) for k, v in dict(
        moba_q_norm=moba_q_norm, moba_k_norm=moba_k_norm, gdn_conv=gdn_conv, gdn_a_log=gdn_a_log,
        gdn_dt_bias=gdn_dt_bias, gdn_out_norm=gdn_out_norm, sc_conv=sc_conv, swa_q_norm=swa_q_norm,
        swa_k_norm=swa_k_norm, swa_sinks=swa_sinks).items()}
    x = np.asarray(x, f32)
    depth = w_in.shape[0]
    cores = list(range(N_CORES))
    ones = np.ones((128, 128), f32)
    xT = np.ascontiguousarray(x.reshape(NTOK, D_MODEL).T)
    for l in range(depth):
        wl = host_w_layout(np.asarray(w_in[l], f32), IN_PERM, NCB_IN)
        gcol = np.ascontiguousarray(np.asarray(norm_mix[l], f32).reshape(KC, 128).T)
        in_maps = [{"xT": _shard_fm(xT, i), "gcol": gcol, "w": wl, "ones": ones} for i in cores]
        res = run_bass_kernel_spmd(_prog("A"), in_maps, core_ids=cores)
        projT = np.concatenate([res.results[i]["projT"].reshape(NCB_IN * 128, TPC) for i in cores], axis=1)
        del wl, in_maps, res
        in_maps = [_b_inputs(projT, l, c, P_) for c in cores]
        res = run_bass_kernel_spmd(_prog("B"), in_maps, core_ids=cores)
        mixT = np.empty((D_MODEL, NTOK), f32)
        for c in cores:
            ob = res.results[c]["outB"]
            for m in range(4):
                mixT[m * 1024 + c * 128: m * 1024 + (c + 1) * 128] = ob[m]
        del in_maps, res, projT
        wo = lay_cols(np.asarray(w_out[l], f32), KC)
        wg = lay_cols(np.asarray(w_gate[l], f32), JB)
        wu = lay_cols(np.asarray(w_up[l], f32), JB)
        wd = lay_wd(np.asarray(w_down[l], f32), JB)
        gcol2 = np.ascontiguousarray(np.asarray(norm_ffn[l], f32).reshape(KC, 128).T)
        in_maps = [{"xT": _shard_fm(xT, i), "mixT": _shard_fm(mixT, i), "gcol": gcol2, "wo": wo, "wg": wg,
                    "wu": wu, "wd": wd, "ones": ones} for i in cores]
        res = run_bass_kernel_spmd(_prog("C"), in_maps, core_ids=cores)
        xT = np.concatenate([res.results[i]["x2T"].reshape(D_MODEL, TPC) for i in cores], axis=1)
        del wo, wg, wu, wd, in_maps, res, mixT
    return np.ascontiguousarray(xT.T).reshape(BATCH, SEQ, D_MODEL).astype(f32)
```

```python
import numpy as np
from contextlib import ExitStack
import concourse.bass as bass
import concourse.mybir as mybir
from concourse.bass_utils import run_bass_kernel_spmd

F32 = mybir.dt.float32
BF16 = mybir.dt.bfloat16
AF = mybir.ActivationFunctionType
ALU = mybir.AluOpType

ENGS = ("pe", "act", "dve", "pool", "sp")
NDMA = 8


class Buf:
    __slots__ = ("name", "w", "r", "excl")

    def __init__(self, name="", excl=False):
        self.name = name
        self.excl = excl
        self.w = None
        self.r = {}


class Prog:
    def __init__(self, nc, stack):
        self.nc = nc
        self.ops = {e: [] for e in ENGS}
        self.cnt = {e: 0 for e in ENGS}
        self.dcnt = {e: 0 for e in ENGS}
        self.sem = {e: stack.enter_context(nc.semaphore("s_" + e)) for e in ENGS}
        self.dsem = {e: [stack.enter_context(nc.semaphore("d_%s%d" % (e, i))) for i in range(NDMA)]
                     for e in ("sp", "pool", "act")}
        self.semname = {}
        for e in ENGS:
            self.semname[id(self.sem[e])] = e

    def _deps(self, reads, writes):
        toks = {}

        def add(t):
            if t is None:
                return
            s, v = t
            k = id(s)
            if k not in toks or toks[k][1] < v:
                toks[k] = (s, v)
        for b in reads:
            add(b.w)
        for b in writes:
            add(b.w)
            for t in b.r.values():
                add(t)
        return toks

    def _mark(self, tok, reads, writes):
        s, v = tok
        for b in reads:
            k = id(s)
            if k not in b.r or b.r[k][1] < v:
                b.r[k] = tok
        for b in writes:
            b.w = tok
            b.r = {}

    def capture(self):
        self._cap = []

    def end_capture(self):
        c, self._cap = self._cap, None
        return c

    def replay(self, caps):
        idx = [0] * len(caps)
        while any(idx[i] < len(caps[i]) for i in range(len(caps))):
            for i, cp in enumerate(caps):
                if idx[i] < len(cp):
                    kind, eng, fn, r, w = cp[idx[i]]
                    idx[i] += 1
                    (self.op if kind == "op" else self.dma)(eng, fn, r, w)

    def op(self, eng, fn, reads=(), writes=()):
        if getattr(self, "_cap", None) is not None:
            self._cap.append(("op", eng, fn, list(reads), list(writes)))
            return
        if any(b.excl for b in reads):
            writes = list(writes) + [b for b in reads if b.excl]
            reads = [b for b in reads if not b.excl]
        toks = self._deps(reads, writes)
        self.cnt[eng] += 1
        tok = (self.sem[eng], self.cnt[eng])
        self._mark(tok, reads, writes)
        self.ops[eng].append((toks, fn, self.sem[eng], 1))

    def dma(self, eng, fn, reads=(), writes=()):
        if getattr(self, "_cap", None) is not None:
            self._cap.append(("dma", eng, fn, list(reads), list(writes)))
            return
        toks = self._deps(reads, writes)
        i = self.dcnt[eng]
        self.dcnt[eng] += 1
        s = self.dsem[eng][i % NDMA]
        if i >= NDMA:
            k = id(s)
            v = 16 * (i // NDMA)
            if k not in toks or toks[k][1] < v:
                toks[k] = (s, v)
        tok = (s, 16 * (i // NDMA + 1))
        self._mark(tok, reads, writes)
        self.ops[eng].append((toks, fn, s, 16))

    def barrier(self):
        toks = {}
        for e in ENGS:
            if self.cnt[e]:
                toks[id(self.sem[e])] = (self.sem[e], self.cnt[e])
        for e in ("sp", "pool", "act"):
            n = self.dcnt[e]
            for j in range(min(n, NDMA)):
                cntj = (n - 1 - j) // NDMA + 1
                toks[id(self.dsem[e][j])] = (self.dsem[e][j], 16 * cntj)
        for e in ENGS:
            self.ops[e].append((dict(toks), None, None, 0))

    def emit(self, final=True):
        nc = self.nc
        if not hasattr(self, "known"):
            self.known = {e: {} for e in ENGS}
        if final:
            self.barrier()

        def run(engname, eng):
            known = self.known[engname]
            own = id(self.sem[engname])
            for toks, fn, s, inc in self.ops[engname]:
                for k, (ws, wv) in toks.items():
                    if engname == "pe" and k == own and fn is not None:
                        continue
                    if known.get(k, 0) >= wv:
                        continue
                    eng.wait_ge(ws, wv)
                    known[k] = wv
                if fn is not None:
                    fn(eng).then_inc(s, inc)
            self.ops[engname] = []

        with nc.Block() as block:
            @block.tensor
            def _(e):
                run("pe", e)

            @block.scalar
            def _(e):
                run("act", e)

            @block.vector
            def _(e):
                run("dve", e)

            @block.gpsimd
            def _(e):
                run("pool", e)

            @block.sync
            def _(e):
                run("sp", e)


class Ctx:
    def __init__(self, nc, stack):
        self.nc = nc
        self.stack = stack
        self.n = 0

    def sb(self, shape, dt, name=None):
        self.n += 1
        return self.stack.enter_context(self.nc.sbuf_tensor(name or ("t%d" % self.n), list(shape), dt))

    def ps(self, shape, dt, name=None):
        self.n += 1
        return self.stack.enter_context(self.nc.psum_tensor(name or ("p%d" % self.n), list(shape), dt))

    def din(self, name, shape, dt=F32):
        return self.nc.dram_tensor(name, list(shape), dt, kind="ExternalInput").ap()

    def dout(self, name, shape, dt=F32):
        return self.nc.dram_tensor(name, list(shape), dt, kind="ExternalOutput").ap()

EPS = 1e-6
GT = 512
KC = 32


def build_stage_a(T=2048, NCB=91):
    NG = T // GT
    nc = bass.Bass("TRN2", target_bir_lowering=False)
    with ExitStack() as st:
        c = Ctx(nc, st)
        xT = c.din("xT", [KC, 128, T])
        gcol = c.din("gcol", [128, KC])
        w = c.din("w", [NCB, 128, KC * 128])
        ones_d = c.din("ones", [128, 128])
        projT = c.dout("projT", [NCB, 128, T])

        P = Prog(nc, st)
        xg = c.sb([128, KC, GT], F32, "xg")
        hT = c.sb([128, KC, GT], BF16, "hT")
        NW = 3
        wt = [c.sb([128, KC * 128], BF16, "wt%d" % i) for i in range(NW)]
        sq = [c.sb([128, GT], F32, "sq%d" % i) for i in range(2)]
        sd = c.sb([128, GT], F32, "sd")
        rstd = c.sb([128, GT], F32, "rstd")
        gt = c.sb([128, KC], F32, "gt")
        ones = c.sb([128, 128], F32, "ones_sb")
        NO = 4
        ost = [c.sb([128, GT], F32, "ost%d" % i) for i in range(NO)]
        ps = [c.ps([128, GT], F32, "ps%d" % i) for i in range(NO)]
        pss = c.ps([128, GT], F32, "pss")

        b_xg = [Buf() for _ in range(4)]
        b_hT = [Buf() for _ in range(KC)]
        b_wt = [Buf() for _ in range(NW)]
        b_sq = [Buf() for _ in range(2)]
        b_sd, b_rstd, b_gt, b_ones, b_pss = Buf(), Buf(), Buf(), Buf(), Buf()
        b_ost = [Buf() for _ in range(NO)]
        b_ps = [Buf() for _ in range(NO)]

        P.dma("sp", lambda e: e.dma_start(out=gt[:], in_=gcol[:, :]), [], [b_gt])
        P.dma("sp", lambda e: e.dma_start(out=ones[:], in_=ones_d[:, :]), [], [b_ones])

        wi = 0
        oi = 0
        for g in range(NG):
            ts = slice(g * GT, (g + 1) * GT)
            for q in range(4):
                P.dma("sp", lambda e, q=q, ts=ts: e.dma_start(
                    out=xg[:, q * 8:(q + 1) * 8, :], in_=xT[q * 8:(q + 1) * 8, :, ts].rearrange("k p t -> p k t")),
                    [], [b_xg[q]])
            for dc in range(KC):
                s = dc % 2
                P.op("act", lambda e, dc=dc, s=s: e.activation(out=sq[s][:], in_=xg[:, dc, :], func=AF.Square),
                     [b_xg[dc // 8]], [b_sq[s]])
                P.op("pe", lambda e, dc=dc, s=s: e.matmul(pss[:], ones[:], sq[s][:], start=(dc == 0), stop=(dc == KC - 1)),
                     [b_sq[s], b_ones], [b_pss])
            P.op("dve", lambda e: e.tensor_scalar(out=sd[:], in0=pss[:], scalar1=1.0 / 4096, scalar2=EPS,
                                                  op0=ALU.mult, op1=ALU.add), [b_pss], [b_sd])
            P.op("act", lambda e: e.activation(out=sd[:], in_=sd[:], func=AF.Sqrt), [b_sd], [b_sd])
            P.op("dve", lambda e: e.reciprocal(out=rstd[:], in_=sd[:]), [b_sd], [b_rstd])
            for dc in range(KC):
                P.op("dve", lambda e, dc=dc: e.scalar_tensor_tensor(
                    out=hT[:, dc, :], in0=xg[:, dc, :], scalar=gt[:, dc:dc + 1], in1=rstd[:],
                    op0=ALU.mult, op1=ALU.mult), [b_xg[dc // 8], b_gt, b_rstd], [b_hT[dc]])
            for cb in range(NCB):
                wb = wi % NW
                wi += 1
                P.dma("pool", lambda e, cb=cb, wb=wb: e.dma_start(out=wt[wb][:], in_=w[cb, :, :]), [], [b_wt[wb]])
                ob = oi % NO
                oi += 1
                for kc in range(KC):
                    P.op("pe", lambda e, kc=kc, wb=wb, ob=ob: e.matmul(
                        ps[ob][:], wt[wb][:, kc * 128:(kc + 1) * 128], hT[:, kc, :],
                        start=(kc == 0), stop=(kc == KC - 1)), [b_wt[wb], b_hT[kc]], [b_ps[ob]])
                if cb % 2 == 0:
                    P.op("act", lambda e, ob=ob: e.copy(out=ost[ob][:], in_=ps[ob][:]), [b_ps[ob]], [b_ost[ob]])
                else:
                    P.op("dve", lambda e, ob=ob: e.tensor_copy(out=ost[ob][:], in_=ps[ob][:]), [b_ps[ob]], [b_ost[ob]])
                P.dma("sp", lambda e, cb=cb, ob=ob, ts=ts: e.dma_start(out=projT[cb, :, ts], in_=ost[ob][:]),
                      [b_ost[ob]], [])
        P.emit()
    return nc


def host_w_layout(w_in_l, perm, NCB):
    wp = w_in_l[:, perm]
    pad = NCB * 128 - wp.shape[1]
    if pad:
        wp = np.concatenate([wp, np.zeros((4096, pad), np.float32)], axis=1)
    a = wp.reshape(KC, 128, NCB, 128).transpose(2, 1, 0, 3)
    return np.ascontiguousarray(a).reshape(NCB, 128, KC * 128)


JB = 86
JH = 43


def build_stage_c(T=2048, NJB=JB, NNB=KC):
    NG = T // GT
    JHALF = NJB // 2
    nc = bass.Bass("TRN2", target_bir_lowering=False)
    with ExitStack() as st:
        c = Ctx(nc, st)
        xT = c.din("xT", [KC, 128, T])
        mixT = c.din("mixT", [KC, 128, T])
        gcol = c.din("gcol", [128, KC])
        wo = c.din("wo", [KC, 128, KC * 128])
        wg = c.din("wg", [NJB, 128, KC * 128])
        wu = c.din("wu", [NJB, 128, KC * 128])
        wd = c.din("wd", [KC, 2, 128, JHALF * 128])
        ones_d = c.din("ones", [128, 128])
        x1T = c.dout("x1T", [KC, 128, T])
        x2T = c.dout("x2T", [KC, 128, T])

        P = Prog(nc, st)
        aT = c.sb([128, NJB, GT], BF16, "aT")
        h2T = c.sb([128, KC, GT], BF16, "h2T")
        NW = 3
        WSZ = max(JHALF * 128, KC * 128)
        wt = [c.sb([128, WSZ], BF16, "wt%d" % i) for i in range(NW)]
        xin = [c.sb([128, GT], F32, "xin%d" % i) for i in range(2)]
        blk = [c.sb([128, GT], F32, "blk%d" % i) for i in range(2)]
        tmp = [c.sb([128, GT], F32, "tmp%d" % i) for i in range(2)]
        sq = c.sb([128, GT], F32, "sq")
        sd = c.sb([128, GT], F32, "sd")
        rstd = c.sb([128, GT], F32, "rstd")
        gt = c.sb([128, KC], F32, "gt")
        ones = c.sb([128, 128], F32, "ones_sb")
        ps = [c.ps([128, GT], F32, "ps%d" % i) for i in range(6)]
        pss = c.ps([128, GT], F32, "pss")

        b_aT = [Buf() for _ in range(NJB)]
        b_h2T = [Buf() for _ in range(KC)]
        b_wt = [Buf() for _ in range(NW)]
        b_xin = [Buf() for _ in range(2)]
        b_blk = [Buf() for _ in range(2)]
        b_tmp = [Buf() for _ in range(2)]
        b_sq, b_sd, b_rstd, b_gt, b_ones, b_pss = Buf(), Buf(), Buf(), Buf(), Buf(), Buf()
        b_ps = [Buf() for _ in range(6)]
        b_x1d = [Buf() for _ in range(KC)]

        P.dma("sp", lambda e: e.dma_start(out=gt[:], in_=gcol[:, :]), [], [b_gt])
        P.dma("sp", lambda e: e.dma_start(out=ones[:], in_=ones_d[:, :]), [], [b_ones])

        cnt = {"w": 0, "p": 0, "x": 0, "b": 0, "t": 0}

        def nxt(k, n):
            v = cnt[k] % n
            cnt[k] += 1
            return v

        for g in range(NG):
            ts = slice(g * GT, (g + 1) * GT)
            for q in range(4):
                P.dma("pool", lambda e, q=q, ts=ts: e.dma_start(
                    out=aT[:, q * 8:(q + 1) * 8, :], in_=mixT[q * 8:(q + 1) * 8, :, ts].rearrange("k p t -> p k t")),
                    [], b_aT[q * 8:(q + 1) * 8])
            for nb in range(NNB):
                wb = nxt("w", NW)
                P.dma("pool", lambda e, nb=nb, wb=wb: e.dma_start(out=wt[wb][:, 0:KC * 128], in_=wo[nb, :, :]),
                      [], [b_wt[wb]])
                xb = nxt("x", 2)
                P.dma("sp", lambda e, nb=nb, xb=xb, ts=ts: e.dma_start(out=xin[xb][:], in_=xT[nb, :, ts]),
                      [], [b_xin[xb]])
                pb = nxt("p", 6)
                for fc in range(KC):
                    P.op("pe", lambda e, fc=fc, wb=wb, pb=pb: e.matmul(
                        ps[pb][:], wt[wb][:, fc * 128:(fc + 1) * 128], aT[:, fc, :],
                        start=(fc == 0), stop=(fc == KC - 1)), [b_wt[wb], b_aT[fc]], [b_ps[pb]])
                bb = nxt("b", 2)
                P.op("dve", lambda e, pb=pb, xb=xb, bb=bb: e.tensor_tensor(
                    out=blk[bb][:], in0=ps[pb][:], in1=xin[xb][:], op=ALU.add),
                    [b_ps[pb], b_xin[xb]], [b_blk[bb]])
                P.dma("sp", lambda e, nb=nb, bb=bb, ts=ts: e.dma_start(out=x1T[nb, :, ts], in_=blk[bb][:]),
                      [b_blk[bb]], [b_x1d[nb]])
                P.op("act", lambda e, bb=bb: e.activation(out=sq[:], in_=blk[bb][:], func=AF.Square),
                     [b_blk[bb]], [b_sq])
                P.op("pe", lambda e, nb=nb: e.matmul(pss[:], ones[:], sq[:], start=(nb == 0), stop=(nb == NNB - 1)),
                     [b_sq, b_ones], [b_pss])
            P.op("dve", lambda e: e.tensor_scalar(out=sd[:], in0=pss[:], scalar1=1.0 / (NNB * 128), scalar2=EPS,
                                                  op0=ALU.mult, op1=ALU.add), [b_pss], [b_sd])
            P.op("act", lambda e: e.activation(out=sd[:], in_=sd[:], func=AF.Sqrt), [b_sd], [b_sd])
            P.op("dve", lambda e: e.reciprocal(out=rstd[:], in_=sd[:]), [b_sd], [b_rstd])
            for kc in range(NNB):
                xb = nxt("x", 2)
                P.dma("sp", lambda e, kc=kc, xb=xb, ts=ts: e.dma_start(out=xin[xb][:], in_=x1T[kc, :, ts]),
                      [b_x1d[kc]], [b_xin[xb]])
                P.op("dve", lambda e, kc=kc, xb=xb: e.scalar_tensor_tensor(
                    out=h2T[:, kc, :], in0=xin[xb][:], scalar=gt[:, kc:kc + 1], in1=rstd[:],
                    op0=ALU.mult, op1=ALU.mult), [b_xin[xb], b_gt, b_rstd], [b_h2T[kc]])
            for jb in range(NJB):
                wbg = nxt("w", NW)
                P.dma("pool", lambda e, jb=jb, wb=wbg: e.dma_start(out=wt[wb][:, 0:KC * 128], in_=wg[jb, :, :]),
                      [], [b_wt[wbg]])
                pg = nxt("p", 6)
                for kc in range(NNB):
                    P.op("pe", lambda e, kc=kc, wb=wbg, pb=pg: e.matmul(
                        ps[pb][:], wt[wb][:, kc * 128:(kc + 1) * 128], h2T[:, kc, :],
                        start=(kc == 0), stop=(kc == NNB - 1)), [b_wt[wbg], b_h2T[kc]], [b_ps[pg]])
                wbu = nxt("w", NW)
                P.dma("pool", lambda e, jb=jb, wb=wbu: e.dma_start(out=wt[wb][:, 0:KC * 128], in_=wu[jb, :, :]),
                      [], [b_wt[wbu]])
                pu = nxt("p", 6)
                for kc in range(NNB):
                    P.op("pe", lambda e, kc=kc, wb=wbu, pb=pu: e.matmul(
                        ps[pb][:], wt[wb][:, kc * 128:(kc + 1) * 128], h2T[:, kc, :],
                        start=(kc == 0), stop=(kc == NNB - 1)), [b_wt[wbu], b_h2T[kc]], [b_ps[pu]])
                tb = nxt("t", 2)
                P.op("act", lambda e, pb=pg, tb=tb: e.activation(out=tmp[tb][:], in_=ps[pb][:], func=AF.Silu),
                     [b_ps[pg]], [b_tmp[tb]])
                P.op("dve", lambda e, jb=jb, pb=pu, tb=tb: e.tensor_tensor(
                    out=aT[:, jb, :], in0=ps[pb][:], in1=tmp[tb][:], op=ALU.mult),
                    [b_ps[pu], b_tmp[tb]], [b_aT[jb]])
            for nb in range(NNB):
                pb = nxt("p", 6)
                for hf in range(2):
                    wb = nxt("w", NW)
                    P.dma("pool", lambda e, nb=nb, hf=hf, wb=wb: e.dma_start(
                        out=wt[wb][:, 0:JHALF * 128], in_=wd[nb, hf, :, :]), [], [b_wt[wb]])
                    for jj in range(JHALF):
                        jc = hf * JHALF + jj
                        P.op("pe", lambda e, jj=jj, jc=jc, wb=wb, pb=pb: e.matmul(
                            ps[pb][:], wt[wb][:, jj * 128:(jj + 1) * 128], aT[:, jc, :],
                            start=(jc == 0), stop=(jc == NJB - 1)), [b_wt[wb], b_aT[jc]], [b_ps[pb]])
                xb = nxt("x", 2)
                P.dma("sp", lambda e, nb=nb, xb=xb, ts=ts: e.dma_start(out=xin[xb][:], in_=x1T[nb, :, ts]),
                      [b_x1d[nb]], [b_xin[xb]])
                bb = nxt("b", 2)
                P.op("dve", lambda e, pb=pb, xb=xb, bb=bb: e.tensor_tensor(
                    out=blk[bb][:], in0=ps[pb][:], in1=xin[xb][:], op=ALU.add),
                    [b_ps[pb], b_xin[xb]], [b_blk[bb]])
                P.dma("sp", lambda e, nb=nb, bb=bb, ts=ts: e.dma_start(out=x2T[nb, :, ts], in_=blk[bb][:]),
                      [b_blk[bb]], [])
        P.emit()
    return nc


def lay_cols(wm, nblk):
    K = wm.shape[0]
    a = wm.reshape(K // 128, 128, nblk, 128).transpose(2, 1, 0, 3)
    return np.ascontiguousarray(a).reshape(nblk, 128, (K // 128) * 128)


def lay_wd(wdm, njb):
    jh = njb // 2
    a = wdm.reshape(2, jh, 128, KC, 128).transpose(3, 0, 2, 1, 4)
    return np.ascontiguousarray(a).reshape(KC, 2, 128, jh * 128)


def fm(a, T):
    return np.ascontiguousarray(a.T).reshape(KC, 128, T)


NEG = -30000.0
F_MQ, F_MK, F_GQ, F_GK, F_GV, F_SCB, F_SCC, F_SCX, F_SQ, F_SK, F_SKB = range(11)
NF = 11
PC_MQG, PC_MKG = 0, 1
PC_CQ, PC_CK, PC_CV = 2, 6, 10
PC_ALOG, PC_DTB = 14, 15
PC_SC = 16
PC_SQG, PC_SKG = 19, 20
PC_SINK = 21
NPC = 23
C_ID, C_ONES, C_TRI, C_NEGLS, C_NEGU, C_BLK64 = range(6)
NCF = 6
B_ID, B_ONES, B_SWAM, B_CM, B_ESEL = 0, 128, 256, 768, 1280
NCB16 = 1280 + 32 * 128


def build_stage_b(S=8192, NB=2, do=("sc", "swa", "moba", "gdn")):
    TT = NB * S
    NT = S // 128
    SEG = min(1024, S)
    nc = bass.Bass("TRN2", target_bir_lowering=False)
    with ExitStack() as st:
        c = Ctx(nc, st)
        fmT = c.din("fmT", [NF, 128, TT])
        mv_tm = c.din("mv_tm", [TT, 128])
        sv_tm = c.din("sv_tm", [TT, 64])
        gz_tm = c.din("gz_tm", [TT, 128])
        gab_tm = c.din("gab_tm", [128, 2, TT // 128])
        pcol_d = c.din("pcol", [128, NPC])
        gnB_d = c.din("gnB", [128, 128])
        cst_d = c.din("cst", [128, NCF * 128])
        negpast_d = c.din("negpast", [128, 32 * 32])
        cbf_d = c.din("cbf", [128, NCB16])
        outB = c.dout("outB", [4, 128, TT])

        P = Prog(nc, st)
        pcol = c.sb([128, NPC], F32, "pcol_sb")
        gnB = c.sb([128, 128], F32, "gnB_sb")
        cst = c.sb([128, NCF * 128], F32, "cst_sb")
        negpast = c.sb([128, 32 * 32], F32, "negpast_sb")
        cbf = c.sb([128, NCB16], BF16, "cbf_sb")
        b_const = Buf()
        P.dma("sp", lambda e: e.dma_start(out=pcol[:], in_=pcol_d[:, :]), [], [b_const])
        P.dma("sp", lambda e: e.dma_start(out=gnB[:], in_=gnB_d[:, :]), [], [b_const])
        P.dma("sp", lambda e: e.dma_start(out=cst[:], in_=cst_d[:, :]), [], [b_const])
        P.dma("sp", lambda e: e.dma_start(out=negpast[:], in_=negpast_d[:, :]), [], [b_const])
        P.dma("pool", lambda e: e.dma_start(out=cbf[:], in_=cbf_d[:, :]), [], [b_const])

        def cf(i):
            return cst[:, i * 128:(i + 1) * 128]

        NPS = 8
        ps = [c.ps([128, 512], F32, "ps%d" % i) for i in range(NPS)]
        b_ps = [Buf(excl=True) for _ in range(NPS)]
        cnt = {}

        def nxt(k, n):
            v = cnt.get(k, 0)
            cnt[k] = v + 1
            return v % n

        class Pool_:
            def __init__(self, name, n, shape, dt, cx=None):
                cx = cx or c
                self.t = [cx.sb(shape, dt, "%s%d" % (name, i)) for i in range(n)]
                self.b = [Buf() for _ in range(n)]
                self.n = n
                self.i = 0

            def get(self):
                k = self.i % self.n
                self.i += 1
                return self.t[k], self.b[k]

        def getps():
            k = 4 + nxt("ps", NPS - 4)
            return ps[k], b_ps[k]
        getps_default = getps

        def getacc(i):
            return ps[i], b_ps[i]

        CB = [b_const]
        ones_f = cf(C_ONES)
        ident_f = cf(C_ID)

        if "sc" in do:
          with ExitStack() as ms:
            cm = Ctx(nc, ms)
            SCS = 1024
            sc_in = Pool_("scin", 4, [128, SCS + 2], F32, cm)
            sc_b = Pool_("scb", 2, [128, SCS], F32, cm)
            sc_u = Pool_("scu", 2, [128, SCS + 2], F32, cm)
            sc_y = Pool_("scy", 2, [128, SCS], F32, cm)
            for b in range(NB):
                for sg in range(S // SCS):
                    t0 = b * S + sg * SCS
                    ct, cb_ = sc_in.get()
                    xt, xb_ = sc_in.get()
                    bt, bb_ = sc_b.get()
                    if sg == 0:
                        P.op("pool", lambda e, ct=ct: e.memset(ct[:, 0:2], 0.0), [], [cb_])
                        P.op("pool", lambda e, xt=xt: e.memset(xt[:, 0:2], 0.0), [], [xb_])
                        P.dma("sp", lambda e, ct=ct, t0=t0: e.dma_start(out=ct[:, 2:], in_=fmT[F_SCC, :, t0:t0 + SCS]), [], [cb_])
                        P.dma("sp", lambda e, xt=xt, t0=t0: e.dma_start(out=xt[:, 2:], in_=fmT[F_SCX, :, t0:t0 + SCS]), [], [xb_])
                    else:
                        P.dma("sp", lambda e, ct=ct, t0=t0: e.dma_start(out=ct[:, :], in_=fmT[F_SCC, :, t0 - 2:t0 + SCS]), [], [cb_])
                        P.dma("sp", lambda e, xt=xt, t0=t0: e.dma_start(out=xt[:, :], in_=fmT[F_SCX, :, t0 - 2:t0 + SCS]), [], [xb_])
                    P.dma("sp", lambda e, bt=bt, t0=t0: e.dma_start(out=bt[:, :], in_=fmT[F_SCB, :, t0:t0 + SCS]), [], [bb_])
                    ut, ub_ = sc_u.get()
                    yt, yb_ = sc_y.get()
                    P.op("pool", lambda e, ut=ut, ct=ct, xt=xt: e.tensor_tensor(out=ut[:], in0=ct[:], in1=xt[:], op=ALU.mult),
                         [cb_, xb_], [ub_])
                    P.op("dve", lambda e, yt=yt, ut=ut: e.tensor_scalar(
                        out=yt[:], in0=ut[:, 2:SCS + 2], scalar1=pcol[:, PC_SC + 2:PC_SC + 3], scalar2=None, op0=ALU.mult),
                        [ub_] + CB, [yb_])
                    for i in (1, 0):
                        P.op("dve", lambda e, yt=yt, ut=ut, i=i: e.scalar_tensor_tensor(
                            out=yt[:], in0=ut[:, i:SCS + i], scalar=pcol[:, PC_SC + i:PC_SC + i + 1], in1=yt[:],
                            op0=ALU.mult, op1=ALU.add), [ub_, yb_] + CB, [yb_])
                    P.op("pool", lambda e, yt=yt, bt=bt: e.tensor_tensor(out=yt[:], in0=yt[:], in1=bt[:], op=ALU.mult),
                         [yb_, bb_], [yb_])
                    P.dma("sp", lambda e, yt=yt, t0=t0: e.dma_start(out=outB[2, :, t0:t0 + SCS], in_=yt[:]), [yb_], [])
            P.barrier()
            P.emit(final=False)

        tmpn = Pool_("tmpn", 2, [128, SEG], F32)
        rsn = Pool_("rsn", 2, [128, SEG], F32)

        def headnorm(src, srcb, n, onesmat, inv_d, out, outb, gain_ap, post_scale=1.0, tmpn=tmpn, rsn=rsn, getps=None):
            getps = getps or getps_default
            sqt, sqb = tmpn.get()
            P.op("act", lambda e: e.activation(out=sqt[:, :n], in_=src, func=AF.Square), [srcb], [sqb])
            rt, rb = rsn.get()
            for h0 in range(0, n, 512):
                w_ = min(512, n - h0)
                pt, pb = getps()
                P.op("pe", lambda e, pt=pt, h0=h0, w_=w_: e.matmul(pt[:, :w_], onesmat, sqt[:, h0:h0 + w_], start=True, stop=True),
                     [sqb] + CB, [pb])
                P.op("dve", lambda e, pt=pt, h0=h0, w_=w_: e.tensor_scalar(
                    out=rt[:, h0:h0 + w_], in0=pt[:, :w_], scalar1=inv_d, scalar2=EPS, op0=ALU.mult, op1=ALU.add),
                    [pb], [rb])
            P.op("act", lambda e: e.activation(out=rt[:, :n], in_=rt[:, :n], func=AF.Sqrt), [rb], [rb])
            P.op("dve", lambda e: e.reciprocal(out=rt[:, :n], in_=rt[:, :n]), [rb], [rb])
            if gain_ap is None:
                P.op("dve", lambda e: e.scalar_tensor_tensor(out=out, in0=src, scalar=post_scale, in1=rt[:, :n],
                                                             op0=ALU.mult, op1=ALU.mult), [srcb, rb], [outb])
            else:
                P.op("dve", lambda e: e.scalar_tensor_tensor(out=out, in0=src, scalar=gain_ap, in1=rt[:, :n],
                                                             op0=ALU.mult, op1=ALU.mult), [srcb, rb] + CB, [outb])

        segin = Pool_("segin", 3, [128, SEG + 3], F32)
        ms2 = ExitStack()
        cm = Ctx(nc, ms2)
        qn = cm.sb([128, S], BF16, "qn")
        kn = cm.sb([128, S], BF16, "kn")
        kn2 = cm.sb([128, S], BF16, "kn2")
        b_kn2 = Buf()
        vt = cm.sb([128, NT * 128], BF16, "vt")
        b_qn, b_kn, b_vt = Buf(), Buf(), Buf()
        exp_ = Pool_("ex", 6, [128, 512], BF16, cm)

        if "swa" in do:
            sw_o = Pool_("swo", 2, [64, 2, SEG], F32, cm)
            sw_d = Pool_("swd", 2, [64, 256], F32, cm)
            esk = cm.sb([64, 2], F32, "esk")
            eskb = Buf()
            for b in range(NB):
                for sg in range(S // SEG):
                    t0 = b * S + sg * SEG
                    for (fi, dst, dstb, gcolid) in ((F_SQ, qn, b_qn, PC_SQG), (F_SK, kn, b_kn, PC_SKG), (F_SKB, kn2, b_kn2, PC_SKG)):
                        it, ib = segin.get()
                        P.dma("sp", lambda e, it=it, fi=fi, t0=t0: e.dma_start(out=it[:, :SEG], in_=fmT[fi, :, t0:t0 + SEG]), [], [ib])
                        headnorm(it[:, :SEG], ib, SEG, cf(C_BLK64), 1.0 / 64, dst[:, sg * SEG:(sg + 1) * SEG], dstb,
                                 pcol[:, gcolid:gcolid + 1])
                P.dma("pool", lambda e, b=b: e.dma_start(
                    out=vt[:, 0:NT * 64].rearrange("p (n d) -> p n d", d=64),
                    in_=sv_tm[b * S:(b + 1) * S, :].rearrange("(n p) d -> p n d", p=128)), [], [b_vt])
                P.op("act", lambda e, esk=esk: e.activation(out=esk[:, 0:2], in_=pcol[0:64, PC_SINK:PC_SINK + 2], func=AF.Exp),
                     CB, [eskb])
                exs = {}

                def swa_front(kt):
                    nq = 2 if kt + 1 < NT else 1
                    pt, pb = getps()
                    for qh in range(nq):
                        for j in range(2):
                            col = (qh * 2 + j) * 128
                            kk_ = kn if j == 0 else kn2
                            P.op("pe", lambda e, pt=pt, col=col, kk_=kk_, kt=kt, qh=qh: e.matmul(
                                pt[:, col:col + 128], kk_[:, kt * 128:(kt + 1) * 128],
                                qn[:, (kt + qh) * 128:(kt + qh + 1) * 128], start=True, stop=False),
                                [b_kn, b_kn2, b_qn], [pb])
                            P.op("pe", lambda e, pt=pt, col=col, qh=qh, j=j: e.matmul(
                                pt[:, col:col + 128], cbf[:, B_ID:B_ID + 128],
                                cbf[:, B_SWAM + qh * 256 + j * 128:B_SWAM + qh * 256 + (j + 1) * 128], start=False, stop=True),
                                CB, [pb])
                    ex, exb = exp_.get()
                    P.op("act", lambda e, ex=ex, pt=pt, nq=nq: e.activation(out=ex[:, :nq * 256], in_=pt[:, :nq * 256], func=AF.Exp, scale=0.125),
                         [pb], [exb])
                    exs[kt] = (ex, exb)

                swst = {"ot": None, "otb": None}

                def swa_back(kt, b=b):
                    ex, exb = exs[kt]
                    po, pob = getacc((kt % 2) * 2)
                    pd, pdb = getacc((kt % 2) * 2 + 1)
                    srcs = []
                    if kt > 0:
                        srcs.append((kt - 1, exs[kt - 1][0], exs[kt - 1][1], 256))
                    srcs.append((kt, ex, exb, 0))
                    for si, (ktile, ext, extb, c0) in enumerate(srcs):
                        last = si == len(srcs) - 1
                        P.op("pe", lambda e, po=po, ktile=ktile, ext=ext, c0=c0, si=si, last=last: e.matmul(
                            po[0:64, 0:256], vt[:, ktile * 64:(ktile + 1) * 64], ext[:, c0:c0 + 256], start=(si == 0), stop=last),
                            [b_vt, extb], [pob])
                    for si, (ktile, ext, extb, c0) in enumerate(srcs):
                        last = si == len(srcs) - 1
                        P.op("pe", lambda e, pd=pd, ext=ext, c0=c0, si=si, last=last: e.matmul(
                            pd[0:64, 0:256], cbf[:, B_ONES:B_ONES + 64], ext[:, c0:c0 + 256], start=(si == 0), stop=last),
                            [extb] + CB, [pdb])
                    if kt > 0:
                        del exs[kt - 1]
                    dt_, dtb_ = sw_d.get()
                    for j in range(2):
                        P.op("dve", lambda e, dt_=dt_, pd=pd, j=j, esk=esk: e.tensor_scalar(
                            out=dt_[:, j * 128:(j + 1) * 128], in0=pd[0:64, j * 128:(j + 1) * 128],
                            scalar1=esk[:, j:j + 1], scalar2=None, op0=ALU.add), [pdb, eskb], [dtb_])
                    P.op("dve", lambda e, dt_=dt_: e.reciprocal(out=dt_[:, :], in_=dt_[:, :]), [dtb_], [dtb_])
                    kk = kt % 8
                    if kk == 0:
                        swst["ot"], swst["otb"] = sw_o.get()
                    ot, otb = swst["ot"], swst["otb"]
                    P.op("dve", lambda e, ot=ot, po=po, dt_=dt_, kk=kk: e.tensor_tensor(
                        out=ot[:, :, kk * 128:(kk + 1) * 128], in0=po[0:64, 0:256].rearrange("p (j q) -> p j q", j=2),
                        in1=dt_[:, :].rearrange("p (j q) -> p j q", j=2), op=ALU.mult), [pob, dtb_], [otb])
                    if kk == 7:
                        t0 = b * S + (kt - 7) * 128
                        for j in range(2):
                            P.dma("sp", lambda e, ot=ot, j=j, t0=t0: e.dma_start(
                                out=outB[3, j * 64:(j + 1) * 64, t0:t0 + SEG], in_=ot[:, j, :]), [otb], [])

                LA = 2
                for kt in range(min(LA, NT)):
                    swa_front(kt)
                for kt in range(NT):
                    if kt + LA < NT:
                        swa_front(kt + LA)
                    swa_back(kt)

        if "moba" in do:
            NBK = S // 256
            kmean = cm.sb([128, 32], F32, "kmean")
            b_kmean = Buf()
            P.op("pool", lambda e: e.memset(kmean[:, :], 0.0), [], [b_kmean])
            gm = Pool_("gm", 2, [128, 128], F32, cm)
            for t_, b__ in zip(gm.t, gm.b):
                P.op("pool", lambda e, t_=t_: e.memset(t_[:, :], 0.0), [], [b__])
            top8 = Pool_("top8", 2, [128, 8], F32, cm)
            selT_all = cm.sb([128, S], BF16, "selT_all")
            b_sel = [Buf() for _ in range(S // 256)]
            P.op("pool", lambda e: e.memset(selT_all[:, :], 0.0), [], b_sel)
            qseg32 = Pool_("qseg32", 2, [128, SEG], F32, cm)
            mo_o = Pool_("moo", 2, [128, 512], F32, cm)
            mo_r = Pool_("mor", 2, [128, 512], F32, cm)
            gate_sb = cm.sb([128, NT * 32], F32, "gate_sb")
            b_gate = [Buf() for _ in range(NT)]
            scale = 128 ** -0.5
            for b in range(NB):
                P.dma("pool", lambda e, b=b: e.dma_start(
                    out=vt[:, :].rearrange("p (n d) -> p n d", d=128),
                    in_=mv_tm[b * S:(b + 1) * S, :].rearrange("(n p) d -> p n d", p=128)), [], [b_vt])
                for sg in range(S // SEG):
                    t0 = b * S + sg * SEG
                    it, ib = segin.get()
                    P.dma("sp", lambda e, it=it, t0=t0: e.dma_start(out=it[:, :SEG], in_=fmT[F_MK, :, t0:t0 + SEG]), [], [ib])
                    kt32, kb32 = qseg32.get()
                    headnorm(it[:, :SEG], ib, SEG, ones_f, 1.0 / 128, kt32[:, :], kb32, pcol[:, PC_MKG:PC_MKG + 1])
                    P.op("act", lambda e, kt32=kt32, sg=sg: e.copy(out=kn[:, sg * SEG:(sg + 1) * SEG], in_=kt32[:, :]), [kb32], [b_kn])
                    nbs = SEG // 256
                    P.op("dve", lambda e, kt32=kt32, sg=sg: e.tensor_reduce(
                        out=kmean[:, sg * nbs:(sg + 1) * nbs], in_=kt32[:, :].rearrange("p (n k) -> p n k", k=256),
                        axis=mybir.AxisListType.X, op=ALU.add), [kb32], [b_kmean])
                P.op("dve", lambda e: e.tensor_scalar(out=kmean[:, :NBK], in0=kmean[:, :NBK], scalar1=1.0 / 256, scalar2=None,
                                                      op0=ALU.mult), [b_kmean], [b_kmean])
                for sg in range(S // SEG):
                    t0 = b * S + sg * SEG
                    it, ib = segin.get()
                    P.dma("sp", lambda e, it=it, t0=t0: e.dma_start(out=it[:, :SEG], in_=fmT[F_MQ, :, t0:t0 + SEG]), [], [ib])
                    qt32, qb32 = qseg32.get()
                    headnorm(it[:, :SEG], ib, SEG, ones_f, 1.0 / 128, qt32[:, :], qb32, pcol[:, PC_MQG:PC_MQG + 1])
                    P.op("act", lambda e, qt32=qt32, sg=sg: e.copy(out=qn[:, sg * SEG:(sg + 1) * SEG], in_=qt32[:, :]), [qb32], [b_qn])
                    for ti in range(SEG // 128):
                        qt_i = sg * (SEG // 128) + ti
                        pt, pb = getps()
                        P.op("pe", lambda e, pt=pt, qt32=qt32, ti=ti: e.matmul(
                            pt[:, 0:32], qt32[:, ti * 128:(ti + 1) * 128], kmean[:, 0:32], start=True, stop=True),
                            [qb32, b_kmean], [pb])
                        blk = qt_i // 2
                        P.op("dve", lambda e, pt=pt, qt_i=qt_i, blk=blk: e.tensor_tensor(
                            out=gate_sb[:, qt_i * 32:(qt_i + 1) * 32], in0=pt[:, 0:32], in1=negpast[:, blk * 32:(blk + 1) * 32],
                            op=ALU.add), [pb] + CB, [b_gate[qt_i]])
                for qt_i in range(NT):
                    g_ap = gate_sb[:, qt_i * 32:(qt_i + 1) * 32]
                    t8, t8b = top8.get()
                    P.op("dve", lambda e, t8=t8, g_ap=g_ap: e.max(out=t8[:, :], in_=g_ap), [b_gate[qt_i]], [t8b])
                    P.op("dve", lambda e, t8=t8: e.tensor_scalar(out=t8[:, 2:3], in0=t8[:, 2:3], scalar1=-1e29, scalar2=None,
                                                                 op0=ALU.max), [t8b], [t8b])
                    gt_, gtb = gm.get()
                    P.op("dve", lambda e, gt_=gt_, g_ap=g_ap, t8=t8: e.tensor_scalar(
                        out=gt_[:, 0:32], in0=g_ap, scalar1=t8[:, 2:3], scalar2=None, op0=ALU.is_ge), [b_gate[qt_i], t8b], [gtb])
                    P.op("dve", lambda e, gt_=gt_: e.tensor_scalar(
                        out=gt_[:, 0:32], in0=gt_[:, 0:32], scalar1=-NEG, scalar2=NEG, op0=ALU.mult, op1=ALU.add), [gtb], [gtb])
                    pt, pb = getps()
                    P.op("pe", lambda e, pt=pt, gt_=gt_: e.transpose(pt[:, 0:128], gt_[:, :], ident_f), [gtb] + CB, [pb])
                    P.op("act", lambda e, pt=pt, qt_i=qt_i: e.copy(out=selT_all[0:32, qt_i * 128:(qt_i + 1) * 128], in_=pt[0:32, 0:128]),
                         [pb], [b_sel[qt_i // 2]])
                items = [(B, kt) for B in range(S // 256) for kt in range(2 * B + 2)]
                mex = {}

                def mo_front(i):
                    B, kt = items[i]
                    n = kt // 2
                    pt, pb = getps()
                    P.op("pe", lambda e, pt=pt, kt=kt, B=B: e.matmul(
                        pt[:, 0:256], kn[:, kt * 128:(kt + 1) * 128], qn[:, B * 256:(B + 1) * 256], start=True, stop=False),
                        [b_kn, b_qn], [pb])
                    if n < B:
                        P.op("pe", lambda e, pt=pt, n=n, B=B: e.matmul(
                            pt[:, 0:256], cbf[:, B_ESEL + n * 128:B_ESEL + (n + 1) * 128], selT_all[:, B * 256:(B + 1) * 256],
                            start=False, stop=True), [b_sel[B]] + CB, [pb])
                    else:
                        v = kt % 2
                        P.op("pe", lambda e, pt=pt, v=v: e.matmul(
                            pt[:, 0:256], cbf[:, B_ID:B_ID + 128], cbf[:, B_CM + v * 256:B_CM + (v + 1) * 256], start=False, stop=True),
                            CB, [pb])
                    ex, exb = exp_.get()
                    P.op("act", lambda e, ex=ex, pt=pt: e.activation(out=ex[:, 0:256], in_=pt[:, 0:256], func=AF.Exp, scale=scale),
                         [pb], [exb])
                    mex[i] = (ex, exb)

                def mo_back(i, b=b):
                    B, kt = items[i]
                    nkt = 2 * B + 2
                    ex, exb = mex.pop(i)
                    po, pob = getacc((B % 2) * 2)
                    pd, pdb = getacc((B % 2) * 2 + 1)
                    P.op("pe", lambda e, po=po, kt=kt, ex=ex, nkt=nkt: e.matmul(
                        po[:, 0:256], vt[:, kt * 128:(kt + 1) * 128], ex[:, 0:256], start=(kt == 0), stop=(kt == nkt - 1)),
                        [b_vt, exb], [pob])
                    P.op("pe", lambda e, pd=pd, ex=ex, kt=kt, nkt=nkt: e.matmul(
                        pd[:, 0:256], cbf[:, B_ONES:B_ONES + 128], ex[:, 0:256], start=(kt == 0), stop=(kt == nkt - 1)),
                        [exb] + CB, [pdb])
                    if kt == nkt - 1:
                        rt, rb = mo_r.get()
                        P.op("dve", lambda e, rt=rt, pd=pd: e.reciprocal(out=rt[:, 0:256], in_=pd[:, 0:256]), [pdb], [rb])
                        ot, otb = mo_o.get()
                        P.op("dve", lambda e, ot=ot, po=po, rt=rt: e.tensor_tensor(out=ot[:, 0:256], in0=po[:, 0:256], in1=rt[:, 0:256], op=ALU.mult),
                             [pob, rb], [otb])
                        t0 = b * S + B * 256
                        P.dma("sp", lambda e, ot=ot, t0=t0: e.dma_start(out=outB[0, :, t0:t0 + 256], in_=ot[:, 0:256]), [otb], [])

                LA = 2
                for i in range(min(LA, len(items))):
                    mo_front(i)
                for i in range(len(items)):
                    if i + LA < len(items):
                        mo_front(i + LA)
                    mo_back(i)

        P.barrier()
        P.emit(final=False)
        ms2.close()
        if "gdn" in do:
            ms3 = ExitStack()
            cm = Ctx(nc, ms3)
            NTT = TT // 128
            gab = cm.sb([128, 2, NTT], F32, "gab")
            gall = cm.sb([128, NTT], F32, "gall")
            ball = cm.sb([128, NTT], F32, "ball")
            nball = cm.sb([128, NTT], F32, "nball")
            negA = cm.sb([128, 1], F32, "negA")
            b_g = Buf()
            P.dma("sp", lambda e: e.dma_start(out=gab[:], in_=gab_tm[:, :, :]), [], [b_g])
            P.op("act", lambda e: e.activation(out=negA[:, :], in_=pcol[:, PC_ALOG:PC_ALOG + 1], func=AF.Exp), CB, [b_g])
            P.op("dve", lambda e: e.tensor_scalar(out=negA[:, :], in0=negA[:, :], scalar1=-1.0, scalar2=None, op0=ALU.mult), [b_g], [b_g])
            P.op("act", lambda e: e.activation(out=gall[:, :], in_=gab[:, 0, :], func=AF.Exp, bias=pcol[:, PC_DTB:PC_DTB + 1]),
                 [b_g] + CB, [b_g])
            P.op("act", lambda e: e.activation(out=gall[:, :], in_=gall[:, :], func=AF.Ln, bias=cst[:, C_ONES * 128:C_ONES * 128 + 1]),
                 [b_g] + CB, [b_g])
            P.op("dve", lambda e: e.tensor_scalar(out=gall[:, :], in0=gall[:, :], scalar1=negA[:, 0:1], scalar2=None, op0=ALU.mult),
                 [b_g], [b_g])
            P.op("act", lambda e: e.activation(out=ball[:, :], in_=gab[:, 1, :], func=AF.Sigmoid), [b_g], [b_g])
            P.op("dve", lambda e: e.tensor_scalar(out=nball[:, :], in0=ball[:, :], scalar1=-1.0, scalar2=None, op0=ALU.mult),
                 [b_g], [b_g])

            lanes = []
            for ln in range(NB):
                L = {}
                L["Sst"] = cm.sb([128, 128], F32, "Sst%d" % ln)
                L["b_S"] = Buf()
                L["cv"] = Pool_("cv%d_" % ln, 1, [128, SEG], F32, cm)
                L["qs"] = Pool_("gqs%d_" % ln, 1, [128, SEG], F32, cm)
                L["ks"] = Pool_("gks%d_" % ln, 1, [128, SEG], F32, cm)
                L["vs"] = Pool_("gvs%d_" % ln, 1, [128, SEG], F32, cm)
                L["zt"] = Pool_("gzt%d_" % ln, 1, [128, SEG // 128, 128], F32, cm)
                L["og"] = Pool_("gog%d_" % ln, 1, [128, SEG], F32, cm)
                L["segin"] = Pool_("gsegin%d_" % ln, 2, [128, SEG + 3], F32, cm)
                L["tmpn"] = Pool_("gtmpn%d_" % ln, 1, [128, SEG], F32, cm)
                L["rsn"] = Pool_("grsn%d_" % ln, 1, [128, SEG], F32, cm)
                m128 = {}
                for nm, n_ in (("gbc", 2), ("egb", 2), ("arg", 2), ("dst", 2), ("dti", 2), ("A", 2), ("aa", 4), ("yy", 4),
                               ("qk", 2), ("qd", 2), ("kd", 2), ("vb", 2), ("R", 2), ("vn", 2), ("ob", 2), ("junk", 2)):
                    m128[nm] = Pool_("g%d_%s" % (ln, nm), n_, [128, 256 if nm == "aa" else 128], F32, cm)
                L["m128"] = m128
                L["cols"] = Pool_("g%d_cols" % ln, 3, [128, 8], F32, cm)
                nb_ = NPS // NB

                def getps_l(ln=ln, nb_=nb_, st_=[0]):
                    k = ln * nb_ + (st_[0] % nb_)
                    st_[0] += 1
                    return ps[k], b_ps[k]
                L["getps"] = getps_l
                lanes.append(L)

            def gdn_lane(b, L):
                Sst, b_S, cv, qs, ks, vs, zt, og, m128, cols = (L[k] for k in ("Sst", "b_S", "cv", "qs", "ks", "vs", "zt", "og", "m128", "cols"))
                segin_l, tmpn_l, rsn_l, getps = L["segin"], L["tmpn"], L["rsn"], L["getps"]
                P.op("pool", lambda e: e.memset(Sst[:, :], 0.0), [], [b_S])
                for sg in range(S // SEG):
                    t0 = b * S + sg * SEG
                    segs = {}
                    for (nm, fi, pc, pool_) in (("q", F_GQ, PC_CQ, qs), ("k", F_GK, PC_CK, ks), ("v", F_GV, PC_CV, vs)):
                        it, ib = segin_l.get()
                        if sg == 0:
                            P.op("pool", lambda e, it=it: e.memset(it[:, 0:3], 0.0), [], [ib])
                            P.dma("sp", lambda e, it=it, fi=fi, t0=t0: e.dma_start(out=it[:, 3:], in_=fmT[fi, :, t0:t0 + SEG]), [], [ib])
                        else:
                            P.dma("sp", lambda e, it=it, fi=fi, t0=t0: e.dma_start(out=it[:, :], in_=fmT[fi, :, t0 - 3:t0 + SEG]), [], [ib])
                        ct_, cb2 = cv.get()
                        P.op("dve", lambda e, ct_=ct_, it=it, pc=pc: e.tensor_scalar(
                            out=ct_[:, :], in0=it[:, 3:SEG + 3], scalar1=pcol[:, pc + 3:pc + 4], scalar2=None, op0=ALU.mult),
                            [ib] + CB, [cb2])
                        for i in (2, 1, 0):
                            P.op("dve", lambda e, ct_=ct_, it=it, pc=pc, i=i: e.scalar_tensor_tensor(
                                out=ct_[:, :], in0=it[:, i:SEG + i], scalar=pcol[:, pc + i:pc + i + 1], in1=ct_[:, :],
                                op0=ALU.mult, op1=ALU.add), [ib, cb2] + CB, [cb2])
                        dt2, db2 = pool_.get()
                        if nm == "v":
                            P.op("act", lambda e, dt2=dt2, ct_=ct_: e.activation(out=dt2[:, :], in_=ct_[:, :], func=AF.Silu), [cb2], [db2])
                        else:
                            P.op("act", lambda e, ct_=ct_: e.activation(out=ct_[:, :], in_=ct_[:, :], func=AF.Silu), [cb2], [cb2])
                            headnorm(ct_[:, :], cb2, SEG, ones_f, 1.0, dt2[:, :], db2, None,
                                     post_scale=(128 ** -0.5 if nm == "q" else 1.0), tmpn=tmpn_l, rsn=rsn_l, getps=getps)
                        segs[nm] = (dt2, db2)
                    ztile, zb = zt.get()
                    P.dma("sp", lambda e, ztile=ztile, t0=t0: e.dma_start(
                        out=ztile[:, :, :], in_=gz_tm[t0:t0 + SEG, :].rearrange("(n p) d -> p n d", p=128)), [], [zb])
                    P.op("act", lambda e, ztile=ztile: e.activation(out=ztile[:, :, :], in_=ztile[:, :, :], func=AF.Silu), [zb], [zb])
                    ogt, ogb = og.get()
                    qT_, qb_ = segs["q"]
                    kT_, kb_ = segs["k"]
                    vT_, vb_ = segs["v"]
                    for ci in range(SEG // 128):
                        gi = (t0 // 128) + ci
                        cs = slice(ci * 128, (ci + 1) * 128)
                        gcol = gall[:, gi:gi + 1]
                        bcol = ball[:, gi:gi + 1]
                        nbcol = nball[:, gi:gi + 1]
                        gbc, gbcb = m128["gbc"].get()
                        P.op("dve", lambda e, gbc=gbc, gcol=gcol: e.tensor_scalar(out=gbc[:, :], in0=ones_f, scalar1=gcol, scalar2=None,
                                                                                 op0=ALU.mult), [b_g] + CB, [gbcb])
                        pG, pGb = getps()
                        P.op("pe", lambda e, pG=pG, gbc=gbc: e.matmul(pG[:, 0:128], gbc[:, :], cf(C_TRI), start=True, stop=True),
                             [gbcb] + CB, [pGb])
                        P.op("pe", lambda e, pG=pG, gcol=gcol: e.matmul(pG[:, 128:129], cf(C_TRI), gcol, start=True, stop=True),
                             [b_g] + CB, [pGb])
                        cl, clb = cols.get()
                        P.op("dve", lambda e, cl=cl, pG=pG: e.tensor_copy(out=cl[:, 0:1], in_=pG[:, 128:129]), [pGb], [clb])
                        P.op("dve", lambda e, cl=cl, pG=pG: e.tensor_scalar(out=cl[:, 1:2], in0=pG[:, 128:129], scalar1=-1.0, scalar2=None,
                                                                           op0=ALU.mult), [pGb], [clb])
                        P.op("dve", lambda e, cl=cl, pG=pG: e.tensor_copy(out=cl[:, 2:3], in_=pG[:, 127:128]), [pGb], [clb])
                        egb, egbb = m128["egb"].get()
                        P.op("act", lambda e, egb=egb, pG=pG: e.activation(out=egb[:, :], in_=pG[:, 0:128], func=AF.Exp), [pGb], [egbb])
                        arg, argb = m128["arg"].get()
                        P.op("dve", lambda e, arg=arg, pG=pG: e.scalar_tensor_tensor(
                            out=arg[:, :], in0=pG[:, 0:128], scalar=-1.0, in1=cf(C_NEGLS), op0=ALU.mult, op1=ALU.add), [pGb] + CB, [argb])
                        dst, dstb = m128["dst"].get()
                        P.op("act", lambda e, dst=dst, arg=arg, cl=cl: e.activation(out=dst[:, :], in_=arg[:, :], func=AF.Exp, bias=cl[:, 0:1]),
                             [argb, clb], [dstb])
                        arg2, arg2b = m128["arg"].get()
                        P.op("dve", lambda e, arg2=arg2, pG=pG: e.tensor_tensor(out=arg2[:, :], in0=pG[:, 0:128], in1=cf(C_NEGU), op=ALU.add),
                             [pGb] + CB, [arg2b])
                        dti, dtib = m128["dti"].get()
                        P.op("act", lambda e, dti=dti, arg2=arg2, cl=cl: e.activation(out=dti[:, :], in_=arg2[:, :], func=AF.Exp, bias=cl[:, 1:2]),
                             [arg2b, clb], [dtib])
                        P.op("act", lambda e, cl=cl: e.activation(out=cl[:, 3:4], in_=cl[:, 0:1], func=AF.Exp, scale=-1.0, bias=cl[:, 2:3]),
                             [clb], [clb])
                        P.op("act", lambda e, cl=cl: e.activation(out=cl[:, 4:5], in_=cl[:, 0:1], func=AF.Exp), [clb], [clb])
                        P.op("dve", lambda e, cl=cl, bcol=bcol: e.tensor_scalar(out=cl[:, 5:6], in0=cl[:, 4:5], scalar1=bcol, scalar2=-1.0,
                                                                               op0=ALU.mult, op1=ALU.mult), [clb, b_g], [clb])
                        pK, pKb = getps()
                        P.op("pe", lambda e, pK=pK, kT_=kT_, cs=cs: e.matmul(pK[:, 0:128], kT_[:, cs], kT_[:, cs], start=True, stop=True),
                             [kb_], [pKb])
                        aa, aab = m128["aa"].get()
                        P.op("dve", lambda e, aa=aa, pK=pK, nbcol=nbcol, dst=dst: e.scalar_tensor_tensor(
                            out=aa[:, 0:128], in0=pK[:, 0:128], scalar=nbcol, in1=dst[:, :], op0=ALU.mult, op1=ALU.mult),
                            [pKb, b_g, dstb], [aab])
                        P.op("pe", lambda e, pK=pK, aa=aa: e.transpose(pK[:, 128:256], aa[:, 0:128], ident_f), [aab] + CB, [pKb])
                        P.op("act", lambda e, aa=aa, pK=pK: e.copy(out=aa[:, 128:256], in_=pK[:, 128:256]), [pKb], [aab])
                        yy, yyb = m128["yy"].get()
                        P.op("dve", lambda e, yy=yy, aa=aa: e.tensor_tensor(out=yy[:, :], in0=aa[:, 128:256], in1=ident_f, op=ALU.add),
                             [aab] + CB, [yyb])
                        for s_ in range(0, 7):
                            pL, pLb = getps()
                            if s_ <= 5:
                                P.op("pe", lambda e, pL=pL, aa=aa: e.matmul(pL[:, 0:128], aa[:, 128:256], aa[:, 0:128], start=True, stop=True),
                                     [aab], [pLb])
                            if s_ <= 4:
                                P.op("pe", lambda e, pL=pL, aa=aa: e.matmul(pL[:, 128:256], aa[:, 0:128], aa[:, 128:256], start=True, stop=True),
                                     [aab], [pLb])
                            if s_ >= 1:
                                P.op("pe", lambda e, pL=pL, aa=aa, yy=yy: e.matmul(pL[:, 256:384], aa[:, 0:128], yy[:, :], start=True, stop=True),
                                     [aab, yyb], [pLb])
                                yy2, yy2b = m128["yy"].get()
                                P.op("dve", lambda e, yy2=yy2, yy=yy, pL=pL: e.tensor_tensor(out=yy2[:, :], in0=pL[:, 256:384], in1=yy[:, :],
                                                                                          op=ALU.add), [pLb, yyb], [yy2b])
                                yy, yyb = yy2, yy2b
                            if s_ <= 5:
                                aa2, aa2b = m128["aa"].get()
                                w_ = 256 if s_ <= 4 else 128
                                P.op("act", lambda e, aa2=aa2, pL=pL, w_=w_: e.copy(out=aa2[:, 0:w_], in_=pL[:, 0:w_]), [pLb], [aa2b])
                                aa, aab = aa2, aa2b
                        TTm, TTb = yy, yyb
                        pQ, pQb = getps()
                        P.op("pe", lambda e, pQ=pQ, kT_=kT_, qT_=qT_, cs=cs: e.matmul(pQ[:, 0:128], kT_[:, cs], qT_[:, cs], start=True, stop=True),
                             [kb_, qb_], [pQb])
                        P.op("pe", lambda e, pQ=pQ, kT_=kT_, cs=cs: e.transpose(pQ[:, 128:256], kT_[:, cs], ident_f), [kb_] + CB, [pQb])
                        P.op("pe", lambda e, pQ=pQ, vT_=vT_, cs=cs: e.transpose(pQ[:, 256:384], vT_[:, cs], ident_f), [vb_] + CB, [pQb])
                        qk, qkb = m128["qk"].get()
                        P.op("dve", lambda e, qk=qk, pQ=pQ, dti=dti: e.tensor_tensor(out=qk[:, :], in0=pQ[:, 0:128], in1=dti[:, :], op=ALU.mult),
                             [pQb, dtib], [qkb])
                        kd, kdb = m128["kd"].get()
                        P.op("dve", lambda e, kd=kd, pQ=pQ, cl=cl: e.tensor_scalar(out=kd[:, :], in0=pQ[:, 128:256], scalar1=cl[:, 3:4], scalar2=None,
                                                                                  op0=ALU.mult), [pQb, clb], [kdb])
                        vb2, vb2b = m128["vb"].get()
                        P.op("dve", lambda e, vb2=vb2, pQ=pQ, bcol=bcol: e.tensor_scalar(out=vb2[:, :], in0=pQ[:, 256:384], scalar1=bcol, scalar2=None,
                                                                                        op0=ALU.mult), [pQb, b_g], [vb2b])
                        qd, qdb = m128["qd"].get()
                        P.op("pool", lambda e, qd=qd, qT_=qT_, cs=cs, egb=egb: e.tensor_tensor(out=qd[:, :], in0=qT_[:, cs], in1=egb[:, :], op=ALU.mult),
                             [qb_, egbb], [qdb])
                        pS, pSb = getps()
                        pO, pOb = getps()
                        P.op("pe", lambda e, pS=pS, kT_=kT_, cs=cs: e.matmul(pS[:, 0:128], kT_[:, cs], Sst[:, :], start=True, stop=True),
                             [kb_, b_S], [pSb])
                        P.op("pe", lambda e, pO=pO, qd=qd: e.matmul(pO[:, 0:128], qd[:, :], Sst[:, :], start=True, stop=False),
                             [qdb, b_S], [pOb])
                        R, Rb = m128["R"].get()
                        P.op("dve", lambda e, R=R, pS=pS, cl=cl, vb2=vb2: e.scalar_tensor_tensor(
                            out=R[:, :], in0=pS[:, 0:128], scalar=cl[:, 5:6], in1=vb2[:, :], op0=ALU.mult, op1=ALU.add),
                            [pSb, clb, vb2b], [Rb])
                        P.op("pe", lambda e, pS=pS, TTm=TTm, R=R: e.matmul(pS[:, 128:256], TTm[:, :], R[:, :], start=True, stop=True),
                             [TTb, Rb], [pSb])
                        vn, vnb = m128["vn"].get()
                        P.op("act", lambda e, vn=vn, pS=pS: e.copy(out=vn[:, :], in_=pS[:, 128:256]), [pSb], [vnb])
                        P.op("pe", lambda e, pO=pO, qk=qk, vn=vn: e.matmul(pO[:, 0:128], qk[:, :], vn[:, :], start=False, stop=True),
                             [qkb, vnb], [pOb])
                        P.op("pe", lambda e, pS=pS, kd=kd, vn=vn: e.matmul(pS[:, 256:384], kd[:, :], vn[:, :], start=True, stop=True),
                             [kdb, vnb], [pSb])
                        P.op("dve", lambda e, pS=pS, egb=egb: e.scalar_tensor_tensor(
                            out=Sst[:, :], in0=Sst[:, :], scalar=egb[:, 127:128], in1=pS[:, 256:384], op0=ALU.mult, op1=ALU.add),
                            [pSb, egbb, b_S], [b_S])
                        jk, jkb = m128["junk"].get()
                        P.op("act", lambda e, jk=jk, pO=pO: e.activation(out=jk[:, :], in_=pO[:, 0:128], func=AF.Square), [pOb], [jkb])
                        P.op("dve", lambda e, cl=cl, jk=jk: e.tensor_reduce(out=cl[:, 6:7], in_=jk[:, :], axis=mybir.AxisListType.X, op=ALU.add),
                             [jkb], [clb])
                        P.op("dve", lambda e, cl=cl: e.tensor_scalar(out=cl[:, 6:7], in0=cl[:, 6:7], scalar1=1.0 / 128, scalar2=EPS,
                                                                     op0=ALU.mult, op1=ALU.add), [clb], [clb])
                        P.op("act", lambda e, cl=cl: e.activation(out=cl[:, 6:7], in_=cl[:, 6:7], func=AF.Sqrt), [clb], [clb])
                        P.op("dve", lambda e, cl=cl: e.reciprocal(out=cl[:, 6:7], in_=cl[:, 6:7]), [clb], [clb])
                        ob, obb = m128["ob"].get()
                        P.op("dve", lambda e, ob=ob, pO=pO, cl=cl: e.scalar_tensor_tensor(
                            out=ob[:, :], in0=pO[:, 0:128], scalar=cl[:, 6:7], in1=gnB[:, :], op0=ALU.mult, op1=ALU.mult),
                            [pOb, clb] + CB, [obb])
                        P.op("pool", lambda e, ob=ob, ztile=ztile, ci=ci: e.tensor_tensor(out=ob[:, :], in0=ob[:, :], in1=ztile[:, ci, :], op=ALU.mult),
                             [obb, zb], [obb])
                        P.op("pe", lambda e, pO=pO, ob=ob: e.transpose(pO[:, 128:256], ob[:, :], ident_f), [obb] + CB, [pOb])
                        P.op("act", lambda e, ogt=ogt, pO=pO, cs=cs: e.copy(out=ogt[:, cs], in_=pO[:, 128:256]), [pOb], [ogb])
                    P.dma("sp", lambda e, ogt=ogt, t0=t0: e.dma_start(out=outB[1, :, t0:t0 + SEG], in_=ogt[:, :]), [ogb], [])
            caps = []
            for b in range(NB):
                P.capture()
                gdn_lane(b, lanes[b])
                caps.append(P.end_capture())
            P.replay(caps)
            P.emit(final=True)
            ms3.close()
        else:
            P.emit(final=True)
    return nc
import numpy as np


def consts():
    i = np.arange(128)
    ident = np.eye(128, dtype=np.float32)
    ones = np.ones((128, 128), np.float32)
    tri = (i[:, None] <= i[None, :]).astype(np.float32)
    negls = np.where(i[None, :] < i[:, None], 0.0, NEG).astype(np.float32)
    negu = np.where(i[None, :] >= i[:, None], 0.0, NEG).astype(np.float32)
    blk64 = np.zeros((128, 128), np.float32)
    blk64[:64, :64] = 1
    blk64[64:, 64:] = 1
    cst = np.concatenate([ident, ones, tri, negls, negu, blk64], axis=1)
    negpast = np.zeros((128, 32, 32), np.float32)
    for blk in range(32):
        negpast[:, blk, blk:] = -1e30
    negpast = negpast.reshape(128, 1024)
    cbf = np.zeros((128, NCB16), np.float32)
    cbf[:, B_ID:B_ID + 128] = ident
    cbf[:, B_ONES:B_ONES + 128] = 1
    kl = i[:, None]
    ql = i[None, :]
    m0 = np.where(kl <= ql, 0.0, NEG)
    m1 = np.where(kl > ql, 0.0, NEG)
    cbf[:, B_SWAM:B_SWAM + 256] = np.concatenate([m0, m0], axis=1)
    cbf[:, B_SWAM + 256:B_SWAM + 512] = np.concatenate([m1, m1], axis=1)
    qib = np.arange(256)[None, :]
    for v in range(2):
        cbf[:, B_CM + v * 256:B_CM + (v + 1) * 256] = np.where(v * 128 + kl <= qib, 0.0, NEG)
    for n in range(32):
        cbf[n, B_ESEL + n * 128:B_ESEL + (n + 1) * 128] = 1
    return cst, negpast, cbf


def pack_b_inputs(proj, prm, c, S, NB):
    TT = NB * S
    sl = slice(c * 128, (c + 1) * 128)
    kvh = c // 4
    fm = np.zeros((NF, 128, TT), np.float32)
    fm[F_MQ] = proj["mq"][:, sl].T
    fm[F_MK] = proj["mk"][:, sl].T
    fm[F_GQ] = proj["gqkv"][:, c * 128:(c + 1) * 128].T
    fm[F_GK] = proj["gqkv"][:, 1024 + c * 128:1024 + (c + 1) * 128].T
    fm[F_GV] = proj["gqkv"][:, 2048 + c * 128:2048 + (c + 1) * 128].T
    fm[F_SCB] = proj["scb"][:, sl].T
    fm[F_SCC] = proj["scc"][:, sl].T
    fm[F_SCX] = proj["scx"][:, sl].T
    fm[F_SQ] = proj["sq"][:, sl].T
    skh = proj["sk"][:, kvh * 64:(kvh + 1) * 64].T
    fm[F_SK] = np.concatenate([skh, 0 * skh], axis=0)
    fm[F_SKB] = np.concatenate([0 * skh, skh], axis=0)
    pc = np.zeros((128, NPC), np.float32)
    pc[:, PC_MQG] = prm["moba_q_norm"]
    pc[:, PC_MKG] = prm["moba_k_norm"]
    gc = prm["gdn_conv"]
    pc[:, PC_CQ:PC_CQ + 4] = gc[:, c * 128:(c + 1) * 128].T
    pc[:, PC_CK:PC_CK + 4] = gc[:, 1024 + c * 128:1024 + (c + 1) * 128].T
    pc[:, PC_CV:PC_CV + 4] = gc[:, 2048 + c * 128:2048 + (c + 1) * 128].T
    pc[:, PC_ALOG] = prm["gdn_a_log"][c]
    pc[:, PC_DTB] = prm["gdn_dt_bias"][c]
    pc[:, PC_SC:PC_SC + 3] = prm["sc_conv"][:, sl].T
    pc[:, PC_SQG] = np.concatenate([prm["swa_q_norm"], prm["swa_q_norm"]])
    pc[:, PC_SKG] = np.concatenate([prm["swa_k_norm"], prm["swa_k_norm"]])
    pc[:, PC_SINK] = prm["swa_sinks"][2 * c]
    pc[:, PC_SINK + 1] = prm["swa_sinks"][2 * c + 1]
    gab = np.stack([proj["ga"][:, c].reshape(TT // 128, 128).T, proj["gb"][:, c].reshape(TT // 128, 128).T], axis=1)
    cst, negpast, cbf = consts()
    return {
        "fmT": fm,
        "mv_tm": np.ascontiguousarray(proj["mv"][:, sl]),
        "sv_tm": np.ascontiguousarray(proj["sv"][:, kvh * 64:(kvh + 1) * 64]),
        "gz_tm": np.ascontiguousarray(proj["gz"][:, sl]),
        "gab_tm": np.ascontiguousarray(gab),
        "pcol": pc,
        "gnB": np.broadcast_to(prm["gdn_out_norm"][None, :], (128, 128)).copy(),
        "cst": cst, "negpast": negpast, "cbf": cbf,
    }


N_CORES = 8
D_MODEL = 4096
SEQ = 8192
BATCH = 2
NTOK = BATCH * SEQ
TPC = NTOK // N_CORES
NCB_IN = 91
IN_PERM = np.concatenate([np.arange(0, 6144), np.arange(6160, 11536), np.arange(6144, 6160)])
R_MQ, R_MK, R_MV, R_GQ, R_GK, R_GV, R_GZ, R_SCB, R_SCC, R_SCX, R_SQ, R_SK, R_SV, R_GA, R_GB = (
    0, 1024, 2048, 3072, 4096, 5120, 6144, 7168, 8192, 9216, 10240, 11264, 11392, 11520, 11528)

_PROGS = {}


def _prog(name):
    if name not in _PROGS:
        if name == "A":
            _PROGS[name] = build_stage_a(TPC, NCB_IN)
        elif name == "B":
            _PROGS[name] = build_stage_b(SEQ, BATCH)
        else:
            _PROGS[name] = build_stage_c(TPC)
    return _PROGS[name]


def _shard_fm(aT, i):
    return np.ascontiguousarray(aT[:, i * TPC:(i + 1) * TPC]).reshape(KC, 128, TPC)


def _b_inputs(projT, l, c, P_):
    TT = NTOK
    kvh = c // 4
    r = lambda base: projT[base + c * 128: base + (c + 1) * 128]
    fm = np.empty((NF, 128, TT), np.float32)
    fm[F_MQ] = r(R_MQ)
    fm[F_MK] = r(R_MK)
    fm[F_GQ] = r(R_GQ)
    fm[F_GK] = r(R_GK)
    fm[F_GV] = r(R_GV)
    fm[F_SCB] = r(R_SCB)
    fm[F_SCC] = r(R_SCC)
    fm[F_SCX] = r(R_SCX)
    fm[F_SQ] = r(R_SQ)
    skh = projT[R_SK + kvh * 64: R_SK + (kvh + 1) * 64]
    fm[F_SK] = 0.0
    fm[F_SKB] = 0.0
    fm[F_SK, 0:64] = skh
    fm[F_SKB, 64:128] = skh
    pc = np.zeros((128, NPC), np.float32)
    pc[:, PC_MQG] = P_["moba_q_norm"][l]
    pc[:, PC_MKG] = P_["moba_k_norm"][l]
    gc = P_["gdn_conv"][l]
    pc[:, PC_CQ:PC_CQ + 4] = gc[:, c * 128:(c + 1) * 128].T
    pc[:, PC_CK:PC_CK + 4] = gc[:, 1024 + c * 128:1024 + (c + 1) * 128].T
    pc[:, PC_CV:PC_CV + 4] = gc[:, 2048 + c * 128:2048 + (c + 1) * 128].T
    pc[:, PC_ALOG] = P_["gdn_a_log"][l][c]
    pc[:, PC_DTB] = P_["gdn_dt_bias"][l][c]
    pc[:, PC_SC:PC_SC + 3] = P_["sc_conv"][l][:, c * 128:(c + 1) * 128].T
    pc[0:64, PC_SQG] = P_["swa_q_norm"][l]
    pc[64:128, PC_SQG] = P_["swa_q_norm"][l]
    pc[0:64, PC_SKG] = P_["swa_k_norm"][l]
    pc[64:128, PC_SKG] = P_["swa_k_norm"][l]
    pc[:, PC_SINK] = P_["swa_sinks"][l][2 * c]
    pc[:, PC_SINK + 1] = P_["swa_sinks"][l][2 * c + 1]
    gab = np.stack([projT[R_GA + c].reshape(TT // 128, 128).T, projT[R_GB + c].reshape(TT // 128, 128).T], axis=1)
    cst, negpast, cbf = consts()
    return {
        "fmT": fm,
        "mv_tm": np.ascontiguousarray(r(R_MV).T),
        "sv_tm": np.ascontiguousarray(projT[R_SV + kvh * 64: R_SV + (kvh + 1) * 64].T),
        "gz_tm": np.ascontiguousarray(r(R_GZ).T),
        "gab_tm": np.ascontiguousarray(gab),
        "pcol": pc,
        "gnB": np.ascontiguousarray(np.broadcast_to(P_["gdn_out_norm"][l][None, :], (128, 128))),
        "cst": cst, "negpast": negpast, "cbf": cbf,
    }


def kernel(x, norm_mix, w_in, moba_q_norm, moba_k_norm, gdn_conv, gdn_a_log, gdn_dt_bias,
           gdn_out_norm, sc_conv, swa_q_norm, swa_k_norm, swa_sinks, w_out, norm_ffn,
           w_gate, w_up, w_down):
    f32 = np.float32
    P_ = {k: np.asarray(v, f32) for k, v in dict(
        moba_q_norm=moba_q_norm, moba_k_norm=moba_k_norm, gdn_conv=gdn_conv, gdn_a_log=gdn_a_log,
        gdn_dt_bias=gdn_dt_bias, gdn_out_norm=gdn_out_norm, sc_conv=sc_conv, swa_q_norm=swa_q_norm,
        swa_k_norm=swa_k_norm, swa_sinks=swa_sinks).items()}
    x = np.asarray(x, f32)
    depth = w_in.shape[0]
    cores = list(range(N_CORES))
    ones = np.ones((128, 128), f32)
    xT = np.ascontiguousarray(x.reshape(NTOK, D_MODEL).T)
    for l in range(depth):
        wl = host_w_layout(np.asarray(w_in[l], f32), IN_PERM, NCB_IN)
        gcol = np.ascontiguousarray(np.asarray(norm_mix[l], f32).reshape(KC, 128).T)
        in_maps = [{"xT": _shard_fm(xT, i), "gcol": gcol, "w": wl, "ones": ones} for i in cores]
        res = run_bass_kernel_spmd(_prog("A"), in_maps, core_ids=cores)
        projT = np.concatenate([res.results[i]["projT"].reshape(NCB_IN * 128, TPC) for i in cores], axis=1)
        del wl, in_maps, res
        in_maps = [_b_inputs(projT, l, c, P_) for c in cores]
        res = run_bass_kernel_spmd(_prog("B"), in_maps, core_ids=cores)
        mixT = np.empty((D_MODEL, NTOK), f32)
        for c in cores:
            ob = res.results[c]["outB"]
            for m in range(4):
                mixT[m * 1024 + c * 128: m * 1024 + (c + 1) * 128] = ob[m]
        del in_maps, res, projT
        wo = lay_cols(np.asarray(w_out[l], f32), KC)
        wg = lay_cols(np.asarray(w_gate[l], f32), JB)
        wu = lay_cols(np.asarray(w_up[l], f32), JB)
        wd = lay_wd(np.asarray(w_down[l], f32), JB)
        gcol2 = np.ascontiguousarray(np.asarray(norm_ffn[l], f32).reshape(KC, 128).T)
        in_maps = [{"xT": _shard_fm(xT, i), "mixT": _shard_fm(mixT, i), "gcol": gcol2, "wo": wo, "wg": wg,
                    "wu": wu, "wd": wd, "ones": ones} for i in cores]
        res = run_bass_kernel_spmd(_prog("C"), in_maps, core_ids=cores)
        xT = np.concatenate([res.results[i]["x2T"].reshape(D_MODEL, TPC) for i in cores], axis=1)
        del wo, wg, wu, wd, in_maps, res, mixT
    return np.ascontiguousarray(xT.T).reshape(BATCH, SEQ, D_MODEL).astype(f32)
```

```python
import numpy as np
from contextlib import ExitStack
import concourse.bass as bass
import concourse.mybir as mybir
from concourse.bass_utils import run_bass_kernel_spmd

F32 = mybir.dt.float32
BF16 = mybir.dt.bfloat16
AF = mybir.ActivationFunctionType
ALU = mybir.AluOpType

ENGS = ("pe", "act", "dve", "pool", "sp")
NDMA = 8


class Buf:
    __slots__ = ("name", "w", "r", "excl")

    def __init__(self, name="", excl=False):
        self.name = name
        self.excl = excl
        self.w = None
        self.r = {}


class Prog:
    def __init__(self, nc, stack):
        self.nc = nc
        self.ops = {e: [] for e in ENGS}
        self.cnt = {e: 0 for e in ENGS}
        self.dcnt = {e: 0 for e in ENGS}
        self.sem = {e: stack.enter_context(nc.semaphore("s_" + e)) for e in ENGS}
        self.dsem = {e: [stack.enter_context(nc.semaphore("d_%s%d" % (e, i))) for i in range(NDMA)]
                     for e in ("sp", "pool", "act")}
        self.semname = {}
        for e in ENGS:
            self.semname[id(self.sem[e])] = e

    def _deps(self, reads, writes):
        toks = {}

        def add(t):
            if t is None:
                return
            s, v = t
            k = id(s)
            if k not in toks or toks[k][1] < v:
                toks[k] = (s, v)
        for b in reads:
            add(b.w)
        for b in writes:
            add(b.w)
            for t in b.r.values():
                add(t)
        return toks

    def _mark(self, tok, reads, writes):
        s, v = tok
        for b in reads:
            k = id(s)
            if k not in b.r or b.r[k][1] < v:
                b.r[k] = tok
        for b in writes:
            b.w = tok
            b.r = {}

    def capture(self):
        self._cap = []

    def end_capture(self):
        c, self._cap = self._cap, None
        return c

    def replay(self, caps):
        idx = [0] * len(caps)
        while any(idx[i] < len(caps[i]) for i in range(len(caps))):
            for i, cp in enumerate(caps):
                if idx[i] < len(cp):
                    kind, eng, fn, r, w = cp[idx[i]]
                    idx[i] += 1
                    (self.op if kind == "op" else self.dma)(eng, fn, r, w)

    def op(self, eng, fn, reads=(), writes=()):
        if getattr(self, "_cap", None) is not None:
            self._cap.append(("op", eng, fn, list(reads), list(writes)))
            return
        if any(b.excl for b in reads):
            writes = list(writes) + [b for b in reads if b.excl]
            reads = [b for b in reads if not b.excl]
        toks = self._deps(reads, writes)
        self.cnt[eng] += 1
        tok = (self.sem[eng], self.cnt[eng])
        self._mark(tok, reads, writes)
        self.ops[eng].append((toks, fn, self.sem[eng], 1))

    def dma(self, eng, fn, reads=(), writes=()):
        if getattr(self, "_cap", None) is not None:
            self._cap.append(("dma", eng, fn, list(reads), list(writes)))
            return
        toks = self._deps(reads, writes)
        i = self.dcnt[eng]
        self.dcnt[eng] += 1
        s = self.dsem[eng][i % NDMA]
        if i >= NDMA:
            k = id(s)
            v = 16 * (i // NDMA)
            if k not in toks or toks[k][1] < v:
                toks[k] = (s, v)
        tok = (s, 16 * (i // NDMA + 1))
        self._mark(tok, reads, writes)
        self.ops[eng].append((toks, fn, s, 16))

    def barrier(self):
        toks = {}
        for e in ENGS:
            if self.cnt[e]:
                toks[id(self.sem[e])] = (self.sem[e], self.cnt[e])
        for e in ("sp", "pool", "act"):
            n = self.dcnt[e]
            for j in range(min(n, NDMA)):
                cntj = (n - 1 - j) // NDMA + 1
                toks[id(self.dsem[e][j])] = (self.dsem[e][j], 16 * cntj)
        for e in ENGS:
            self.ops[e].append((dict(toks), None, None, 0))

    def emit(self, final=True):
        nc = self.nc
        if not hasattr(self, "known"):
            self.known = {e: {} for e in ENGS}
        if final:
            self.barrier()

        def run(engname, eng):
            known = self.known[engname]
            own = id(self.sem[engname])
            for toks, fn, s, inc in self.ops[engname]:
                for k, (ws, wv) in toks.items():
                    if engname == "pe" and k == own and fn is not None:
                        continue
                    if known.get(k, 0) >= wv:
                        continue
                    eng.wait_ge(ws, wv)
                    known[k] = wv
                if fn is not None:
                    fn(eng).then_inc(s, inc)
            self.ops[engname] = []

        with nc.Block() as block:
            @block.tensor
            def _(e):
                run("pe", e)

            @block.scalar
            def _(e):
                run("act", e)

            @block.vector
            def _(e):
                run("dve", e)

            @block.gpsimd
            def _(e):
                run("pool", e)

            @block.sync
            def _(e):
                run("sp", e)


class Ctx:
    def __init__(self, nc, stack):
        self.nc = nc
        self.stack = stack
        self.n = 0

    def sb(self, shape, dt, name=None):
        self.n += 1
        return self.stack.enter_context(self.nc.sbuf_tensor(name or ("t%d" % self.n), list(shape), dt))

    def ps(self, shape, dt, name=None):
        self.n += 1
        return self.stack.enter_context(self.nc.psum_tensor(name or ("p%d" % self.n), list(shape), dt))

    def din(self, name, shape, dt=F32):
        return self.nc.dram_tensor(name, list(shape), dt, kind="ExternalInput").ap()

    def dout(self, name, shape, dt=F32):
        return self.nc.dram_tensor(name, list(shape), dt, kind="ExternalOutput").ap()

EPS = 1e-6
GT = 512
KC = 32


def build_stage_a(T=2048, NCB=91):
    NG = T // GT
    nc = bass.Bass("TRN2", target_bir_lowering=False)
    with ExitStack() as st:
        c = Ctx(nc, st)
        xT = c.din("xT", [KC, 128, T])
        gcol = c.din("gcol", [128, KC])
        w = c.din("w", [NCB, 128, KC * 128])
        ones_d = c.din("ones", [128, 128])
        projT = c.dout("projT", [NCB, 128, T])

        P = Prog(nc, st)
        xg = c.sb([128, KC, GT], F32, "xg")
        hT = c.sb([128, KC, GT], BF16, "hT")
        NW = 3
        wt = [c.sb([128, KC * 128], BF16, "wt%d" % i) for i in range(NW)]
        sq = [c.sb([128, GT], F32, "sq%d" % i) for i in range(2)]
        sd = c.sb([128, GT], F32, "sd")
        rstd = c.sb([128, GT], F32, "rstd")
        gt = c.sb([128, KC], F32, "gt")
        ones = c.sb([128, 128], F32, "ones_sb")
        NO = 4
        ost = [c.sb([128, GT], F32, "ost%d" % i) for i in range(NO)]
        ps = [c.ps([128, GT], F32, "ps%d" % i) for i in range(NO)]
        pss = c.ps([128, GT], F32, "pss")

        b_xg = [Buf() for _ in range(4)]
        b_hT = [Buf() for _ in range(KC)]
        b_wt = [Buf() for _ in range(NW)]
        b_sq = [Buf() for _ in range(2)]
        b_sd, b_rstd, b_gt, b_ones, b_pss = Buf(), Buf(), Buf(), Buf(), Buf()
        b_ost = [Buf() for _ in range(NO)]
        b_ps = [Buf() for _ in range(NO)]

        P.dma("sp", lambda e: e.dma_start(out=gt[:], in_=gcol[:, :]), [], [b_gt])
        P.dma("sp", lambda e: e.dma_start(out=ones[:], in_=ones_d[:, :]), [], [b_ones])

        wi = 0
        oi = 0
        for g in range(NG):
            ts = slice(g * GT, (g + 1) * GT)
            for q in range(4):
                P.dma("sp", lambda e, q=q, ts=ts: e.dma_start(
                    out=xg[:, q * 8:(q + 1) * 8, :], in_=xT[q * 8:(q + 1) * 8, :, ts].rearrange("k p t -> p k t")),
                    [], [b_xg[q]])
            for dc in range(KC):
                if dc == 0:
                    P.op("act", lambda e: e.activation(out=sd[:], in_=xg[:, 0, :], func=AF.Square), [b_xg[0]], [b_sd])
                else:
                    s = dc % 2
                    P.op("act", lambda e, dc=dc, s=s: e.activation(out=sq[s][:], in_=xg[:, dc, :], func=AF.Square),
                         [b_xg[dc // 8]], [b_sq[s]])
                    P.op("dve", lambda e, s=s: e.tensor_tensor(out=sd[:], in0=sd[:], in1=sq[s][:], op=ALU.add),
                         [b_sq[s], b_sd], [b_sd])
            P.op("pe", lambda e: e.matmul(pss[:], ones[:], sd[:], start=True, stop=True), [b_sd, b_ones], [b_pss])
            P.op("dve", lambda e: e.tensor_scalar(out=sd[:], in0=pss[:], scalar1=1.0 / 4096, scalar2=EPS,
                                                  op0=ALU.mult, op1=ALU.add), [b_pss], [b_sd])
            P.op("act", lambda e: e.activation(out=sd[:], in_=sd[:], func=AF.Sqrt), [b_sd], [b_sd])
            P.op("dve", lambda e: e.reciprocal(out=rstd[:], in_=sd[:]), [b_sd], [b_rstd])
            for dc in range(KC):
                P.op("dve", lambda e, dc=dc: e.scalar_tensor_tensor(
                    out=hT[:, dc, :], in0=xg[:, dc, :], scalar=gt[:, dc:dc + 1], in1=rstd[:],
                    op0=ALU.mult, op1=ALU.mult), [b_xg[dc // 8], b_gt, b_rstd], [b_hT[dc]])
            for cb in range(NCB):
                wb = wi % NW
                wi += 1
                P.dma("pool", lambda e, cb=cb, wb=wb: e.dma_start(out=wt[wb][:], in_=w[cb, :, :]), [], [b_wt[wb]])
                ob = oi % NO
                oi += 1
                for kc in range(KC):
                    P.op("pe", lambda e, kc=kc, wb=wb, ob=ob: e.matmul(
                        ps[ob][:], wt[wb][:, kc * 128:(kc + 1) * 128], hT[:, kc, :],
                        start=(kc == 0), stop=(kc == KC - 1)), [b_wt[wb], b_hT[kc]], [b_ps[ob]])
                if cb % 2 == 0:
                    P.op("act", lambda e, ob=ob: e.copy(out=ost[ob][:], in_=ps[ob][:]), [b_ps[ob]], [b_ost[ob]])
                else:
                    P.op("dve", lambda e, ob=ob: e.tensor_copy(out=ost[ob][:], in_=ps[ob][:]), [b_ps[ob]], [b_ost[ob]])
                P.dma("sp", lambda e, cb=cb, ob=ob, ts=ts: e.dma_start(out=projT[cb, :, ts], in_=ost[ob][:]),
                      [b_ost[ob]], [])
        P.emit()
    return nc


def host_w_layout(w_in_l, perm, NCB):
    wp = w_in_l[:, perm]
    pad = NCB * 128 - wp.shape[1]
    if pad:
        wp = np.concatenate([wp, np.zeros((4096, pad), np.float32)], axis=1)
    a = wp.reshape(KC, 128, NCB, 128).transpose(2, 1, 0, 3)
    return np.ascontiguousarray(a).reshape(NCB, 128, KC * 128)


JB = 86
JH = 43


def build_stage_c(T=2048, NJB=JB, NNB=KC):
    NG = T // GT
    JHALF = NJB // 2
    nc = bass.Bass("TRN2", target_bir_lowering=False)
    with ExitStack() as st:
        c = Ctx(nc, st)
        xT = c.din("xT", [KC, 128, T])
        mixT = c.din("mixT", [KC, 128, T])
        gcol = c.din("gcol", [128, KC])
        wo = c.din("wo", [KC, 128, KC * 128])
        wg = c.din("wg", [NJB, 128, KC * 128])
        wu = c.din("wu", [NJB, 128, KC * 128])
        wd = c.din("wd", [KC, 2, 128, JHALF * 128])
        ones_d = c.din("ones", [128, 128])
        x1T = c.dout("x1T", [KC, 128, T])
        x2T = c.dout("x2T", [KC, 128, T])

        P = Prog(nc, st)
        aT = c.sb([128, NJB, GT], BF16, "aT")
        h2T = c.sb([128, KC, GT], BF16, "h2T")
        NW = 3
        WSZ = max(JHALF * 128, KC * 128)
        wt = [c.sb([128, WSZ], BF16, "wt%d" % i) for i in range(NW)]
        xin = [c.sb([128, GT], F32, "xin%d" % i) for i in range(2)]
        blk = [c.sb([128, GT], F32, "blk%d" % i) for i in range(2)]
        tmp = [c.sb([128, GT], F32, "tmp%d" % i) for i in range(2)]
        sq = c.sb([128, GT], F32, "sq")
        sd = c.sb([128, GT], F32, "sd")
        rstd = c.sb([128, GT], F32, "rstd")
        gt = c.sb([128, KC], F32, "gt")
        ones = c.sb([128, 128], F32, "ones_sb")
        ps = [c.ps([128, GT], F32, "ps%d" % i) for i in range(6)]
        pss = c.ps([128, GT], F32, "pss")

        b_aT = [Buf() for _ in range(NJB)]
        b_h2T = [Buf() for _ in range(KC)]
        b_wt = [Buf() for _ in range(NW)]
        b_xin = [Buf() for _ in range(2)]
        b_blk = [Buf() for _ in range(2)]
        b_tmp = [Buf() for _ in range(2)]
        b_sq, b_sd, b_rstd, b_gt, b_ones, b_pss = Buf(), Buf(), Buf(), Buf(), Buf(), Buf()
        b_ps = [Buf() for _ in range(6)]
        b_x1d = [Buf() for _ in range(KC)]

        P.dma("sp", lambda e: e.dma_start(out=gt[:], in_=gcol[:, :]), [], [b_gt])
        P.dma("sp", lambda e: e.dma_start(out=ones[:], in_=ones_d[:, :]), [], [b_ones])

        cnt = {"w": 0, "p": 0, "x": 0, "b": 0, "t": 0}

        def nxt(k, n):
            v = cnt[k] % n
            cnt[k] += 1
            return v

        for g in range(NG):
            ts = slice(g * GT, (g + 1) * GT)
            for q in range(4):
                P.dma("pool", lambda e, q=q, ts=ts: e.dma_start(
                    out=aT[:, q * 8:(q + 1) * 8, :], in_=mixT[q * 8:(q + 1) * 8, :, ts].rearrange("k p t -> p k t")),
                    [], b_aT[q * 8:(q + 1) * 8])
            for nb in range(NNB):
                wb = nxt("w", NW)
                P.dma("pool", lambda e, nb=nb, wb=wb: e.dma_start(out=wt[wb][:, 0:KC * 128], in_=wo[nb, :, :]),
                      [], [b_wt[wb]])
                xb = nxt("x", 2)
                P.dma("sp", lambda e, nb=nb, xb=xb, ts=ts: e.dma_start(out=xin[xb][:], in_=xT[nb, :, ts]),
                      [], [b_xin[xb]])
                pb = nxt("p", 6)
                for fc in range(KC):
                    P.op("pe", lambda e, fc=fc, wb=wb, pb=pb: e.matmul(
                        ps[pb][:], wt[wb][:, fc * 128:(fc + 1) * 128], aT[:, fc, :],
                        start=(fc == 0), stop=(fc == KC - 1)), [b_wt[wb], b_aT[fc]], [b_ps[pb]])
                bb = nxt("b", 2)
                P.op("dve", lambda e, pb=pb, xb=xb, bb=bb: e.tensor_tensor(
                    out=blk[bb][:], in0=ps[pb][:], in1=xin[xb][:], op=ALU.add),
                    [b_ps[pb], b_xin[xb]], [b_blk[bb]])
                P.dma("sp", lambda e, nb=nb, bb=bb, ts=ts: e.dma_start(out=x1T[nb, :, ts], in_=blk[bb][:]),
                      [b_blk[bb]], [b_x1d[nb]])
                if nb == 0:
                    P.op("act", lambda e, bb=bb: e.activation(out=sd[:], in_=blk[bb][:], func=AF.Square),
                         [b_blk[bb]], [b_sd])
                else:
                    P.op("act", lambda e, bb=bb: e.activation(out=sq[:], in_=blk[bb][:], func=AF.Square),
                         [b_blk[bb]], [b_sq])
                    P.op("dve", lambda e: e.tensor_tensor(out=sd[:], in0=sd[:], in1=sq[:], op=ALU.add), [b_sq, b_sd], [b_sd])
                if nb == NNB - 1:
                    P.op("pe", lambda e: e.matmul(pss[:], ones[:], sd[:], start=True, stop=True), [b_sd, b_ones], [b_pss])
            P.op("dve", lambda e: e.tensor_scalar(out=sd[:], in0=pss[:], scalar1=1.0 / (NNB * 128), scalar2=EPS,
                                                  op0=ALU.mult, op1=ALU.add), [b_pss], [b_sd])
            P.op("act", lambda e: e.activation(out=sd[:], in_=sd[:], func=AF.Sqrt), [b_sd], [b_sd])
            P.op("dve", lambda e: e.reciprocal(out=rstd[:], in_=sd[:]), [b_sd], [b_rstd])
            for kc in range(NNB):
                xb = nxt("x", 2)
                P.dma("sp", lambda e, kc=kc, xb=xb, ts=ts: e.dma_start(out=xin[xb][:], in_=x1T[kc, :, ts]),
                      [b_x1d[kc]], [b_xin[xb]])
                P.op("dve", lambda e, kc=kc, xb=xb: e.scalar_tensor_tensor(
                    out=h2T[:, kc, :], in0=xin[xb][:], scalar=gt[:, kc:kc + 1], in1=rstd[:],
                    op0=ALU.mult, op1=ALU.mult), [b_xin[xb], b_gt, b_rstd], [b_h2T[kc]])
            for jb in range(NJB):
                wbg = nxt("w", NW)
                P.dma("pool", lambda e, jb=jb, wb=wbg: e.dma_start(out=wt[wb][:, 0:KC * 128], in_=wg[jb, :, :]),
                      [], [b_wt[wbg]])
                pg = nxt("p", 6)
                for kc in range(NNB):
                    P.op("pe", lambda e, kc=kc, wb=wbg, pb=pg: e.matmul(
                        ps[pb][:], wt[wb][:, kc * 128:(kc + 1) * 128], h2T[:, kc, :],
                        start=(kc == 0), stop=(kc == NNB - 1)), [b_wt[wbg], b_h2T[kc]], [b_ps[pg]])
                wbu = nxt("w", NW)
                P.dma("pool", lambda e, jb=jb, wb=wbu: e.dma_start(out=wt[wb][:, 0:KC * 128], in_=wu[jb, :, :]),
                      [], [b_wt[wbu]])
                pu = nxt("p", 6)
                for kc in range(NNB):
                    P.op("pe", lambda e, kc=kc, wb=wbu, pb=pu: e.matmul(
                        ps[pb][:], wt[wb][:, kc * 128:(kc + 1) * 128], h2T[:, kc, :],
                        start=(kc == 0), stop=(kc == NNB - 1)), [b_wt[wbu], b_h2T[kc]], [b_ps[pu]])
                tb = nxt("t", 2)
                P.op("act", lambda e, pb=pg, tb=tb: e.activation(out=tmp[tb][:], in_=ps[pb][:], func=AF.Silu),
                     [b_ps[pg]], [b_tmp[tb]])
                P.op("dve", lambda e, jb=jb, pb=pu, tb=tb: e.tensor_tensor(
                    out=aT[:, jb, :], in0=ps[pb][:], in1=tmp[tb][:], op=ALU.mult),
                    [b_ps[pu], b_tmp[tb]], [b_aT[jb]])
            for nb in range(NNB):
                pb = nxt("p", 6)
                for hf in range(2):
                    wb = nxt("w", NW)
                    P.dma("pool", lambda e, nb=nb, hf=hf, wb=wb: e.dma_start(
                        out=wt[wb][:, 0:JHALF * 128], in_=wd[nb, hf, :, :]), [], [b_wt[wb]])
                    for jj in range(JHALF):
                        jc = hf * JHALF + jj
                        P.op("pe", lambda e, jj=jj, jc=jc, wb=wb, pb=pb: e.matmul(
                            ps[pb][:], wt[wb][:, jj * 128:(jj + 1) * 128], aT[:, jc, :],
                            start=(jc == 0), stop=(jc == NJB - 1)), [b_wt[wb], b_aT[jc]], [b_ps[pb]])
                xb = nxt("x", 2)
                P.dma("sp", lambda e, nb=nb, xb=xb, ts=ts: e.dma_start(out=xin[xb][:], in_=x1T[nb, :, ts]),
                      [b_x1d[nb]], [b_xin[xb]])
                bb = nxt("b", 2)
                P.op("dve", lambda e, pb=pb, xb=xb, bb=bb: e.tensor_tensor(
                    out=blk[bb][:], in0=ps[pb][:], in1=xin[xb][:], op=ALU.add),
                    [b_ps[pb], b_xin[xb]], [b_blk[bb]])
                P.dma("sp", lambda e, nb=nb, bb=bb, ts=ts: e.dma_start(out=x2T[nb, :, ts], in_=blk[bb][:]),
                      [b_blk[bb]], [])
        P.emit()
    return nc


def lay_cols(wm, nblk):
    K = wm.shape[0]
    a = wm.reshape(K // 128, 128, nblk, 128).transpose(2, 1, 0, 3)
    return np.ascontiguousarray(a).reshape(nblk, 128, (K // 128) * 128)


def lay_wd(wdm, njb):
    jh = njb // 2
    a = wdm.reshape(2, jh, 128, KC, 128).transpose(3, 0, 2, 1, 4)
    return np.ascontiguousarray(a).reshape(KC, 2, 128, jh * 128)


def fm(a, T):
    return np.ascontiguousarray(a.T).reshape(KC, 128, T)


NEG = -30000.0
F_MQ, F_MK, F_GQ, F_GK, F_GV, F_SCB, F_SCC, F_SCX, F_SQ, F_SK, F_SKB = range(11)
NF = 11
PC_MQG, PC_MKG = 0, 1
PC_CQ, PC_CK, PC_CV = 2, 6, 10
PC_ALOG, PC_DTB = 14, 15
PC_SC = 16
PC_SQG, PC_SKG = 19, 20
PC_SINK = 21
NPC = 23
C_ID, C_ONES, C_TRI, C_NEGLS, C_NEGU, C_BLK64 = range(6)
NCF = 6
B_ID, B_ONES, B_SWAM, B_CM, B_ESEL = 0, 128, 256, 768, 1280
NCB16 = 1280 + 32 * 128


def build_stage_b(S=8192, NB=2, do=("sc", "swa", "moba", "gdn")):
    TT = NB * S
    NT = S // 128
    SEG = min(1024, S)
    nc = bass.Bass("TRN2", target_bir_lowering=False)
    with ExitStack() as st:
        c = Ctx(nc, st)
        fmT = c.din("fmT", [NF, 128, TT])
        mv_tm = c.din("mv_tm", [TT, 128])
        sv_tm = c.din("sv_tm", [TT, 64])
        gz_tm = c.din("gz_tm", [TT, 128])
        gab_tm = c.din("gab_tm", [128, 2, TT // 128])
        pcol_d = c.din("pcol", [128, NPC])
        gnB_d = c.din("gnB", [128, 128])
        cst_d = c.din("cst", [128, NCF * 128])
        negpast_d = c.din("negpast", [128, 32 * 32])
        cbf_d = c.din("cbf", [128, NCB16])
        outB = c.dout("outB", [4, 128, TT])

        P = Prog(nc, st)
        pcol = c.sb([128, NPC], F32, "pcol_sb")
        gnB = c.sb([128, 128], F32, "gnB_sb")
        cst = c.sb([128, NCF * 128], F32, "cst_sb")
        negpast = c.sb([128, 32 * 32], F32, "negpast_sb")
        cbf = c.sb([128, NCB16], BF16, "cbf_sb")
        b_const = Buf()
        P.dma("sp", lambda e: e.dma_start(out=pcol[:], in_=pcol_d[:, :]), [], [b_const])
        P.dma("sp", lambda e: e.dma_start(out=gnB[:], in_=gnB_d[:, :]), [], [b_const])
        P.dma("sp", lambda e: e.dma_start(out=cst[:], in_=cst_d[:, :]), [], [b_const])
        P.dma("sp", lambda e: e.dma_start(out=negpast[:], in_=negpast_d[:, :]), [], [b_const])
        P.dma("pool", lambda e: e.dma_start(out=cbf[:], in_=cbf_d[:, :]), [], [b_const])

        def cf(i):
            return cst[:, i * 128:(i + 1) * 128]

        NPS = 8
        ps = [c.ps([128, 512], F32, "ps%d" % i) for i in range(NPS)]
        b_ps = [Buf(excl=True) for _ in range(NPS)]
        cnt = {}

        def nxt(k, n):
            v = cnt.get(k, 0)
            cnt[k] = v + 1
            return v % n

        class Pool_:
            def __init__(self, name, n, shape, dt, cx=None):
                cx = cx or c
                self.t = [cx.sb(shape, dt, "%s%d" % (name, i)) for i in range(n)]
                self.b = [Buf() for _ in range(n)]
                self.n = n
                self.i = 0

            def get(self):
                k = self.i % self.n
                self.i += 1
                return self.t[k], self.b[k]

        def getps():
            k = 4 + nxt("ps", NPS - 4)
            return ps[k], b_ps[k]
        getps_default = getps

        def getacc(i):
            return ps[i], b_ps[i]

        CB = [b_const]
        ones_f = cf(C_ONES)
        ident_f = cf(C_ID)

        if "sc" in do:
          with ExitStack() as ms:
            cm = Ctx(nc, ms)
            SCS = 1024
            sc_in = Pool_("scin", 4, [128, SCS + 2], F32, cm)
            sc_b = Pool_("scb", 2, [128, SCS], F32, cm)
            sc_u = Pool_("scu", 2, [128, SCS + 2], F32, cm)
            sc_y = Pool_("scy", 2, [128, SCS], F32, cm)
            for b in range(NB):
                for sg in range(S // SCS):
                    t0 = b * S + sg * SCS
                    ct, cb_ = sc_in.get()
                    xt, xb_ = sc_in.get()
                    bt, bb_ = sc_b.get()
                    if sg == 0:
                        P.op("pool", lambda e, ct=ct: e.memset(ct[:, 0:2], 0.0), [], [cb_])
                        P.op("pool", lambda e, xt=xt: e.memset(xt[:, 0:2], 0.0), [], [xb_])
                        P.dma("sp", lambda e, ct=ct, t0=t0: e.dma_start(out=ct[:, 2:], in_=fmT[F_SCC, :, t0:t0 + SCS]), [], [cb_])
                        P.dma("sp", lambda e, xt=xt, t0=t0: e.dma_start(out=xt[:, 2:], in_=fmT[F_SCX, :, t0:t0 + SCS]), [], [xb_])
                    else:
                        P.dma("sp", lambda e, ct=ct, t0=t0: e.dma_start(out=ct[:, :], in_=fmT[F_SCC, :, t0 - 2:t0 + SCS]), [], [cb_])
                        P.dma("sp", lambda e, xt=xt, t0=t0: e.dma_start(out=xt[:, :], in_=fmT[F_SCX, :, t0 - 2:t0 + SCS]), [], [xb_])
                    P.dma("sp", lambda e, bt=bt, t0=t0: e.dma_start(out=bt[:, :], in_=fmT[F_SCB, :, t0:t0 + SCS]), [], [bb_])
                    ut, ub_ = sc_u.get()
                    yt, yb_ = sc_y.get()
                    P.op("pool", lambda e, ut=ut, ct=ct, xt=xt: e.tensor_tensor(out=ut[:], in0=ct[:], in1=xt[:], op=ALU.mult),
                         [cb_, xb_], [ub_])
                    P.op("dve", lambda e, yt=yt, ut=ut: e.tensor_scalar(
                        out=yt[:], in0=ut[:, 2:SCS + 2], scalar1=pcol[:, PC_SC + 2:PC_SC + 3], scalar2=None, op0=ALU.mult),
                        [ub_] + CB, [yb_])
                    for i in (1, 0):
                        P.op("dve", lambda e, yt=yt, ut=ut, i=i: e.scalar_tensor_tensor(
                            out=yt[:], in0=ut[:, i:SCS + i], scalar=pcol[:, PC_SC + i:PC_SC + i + 1], in1=yt[:],
                            op0=ALU.mult, op1=ALU.add), [ub_, yb_] + CB, [yb_])
                    P.op("pool", lambda e, yt=yt, bt=bt: e.tensor_tensor(out=yt[:], in0=yt[:], in1=bt[:], op=ALU.mult),
                         [yb_, bb_], [yb_])
                    P.dma("sp", lambda e, yt=yt, t0=t0: e.dma_start(out=outB[2, :, t0:t0 + SCS], in_=yt[:]), [yb_], [])
            P.barrier()
            P.emit(final=False)

        tmpn = Pool_("tmpn", 2, [128, SEG], F32)
        rsn = Pool_("rsn", 2, [128, SEG], F32)

        def headnorm(src, srcb, n, onesmat, inv_d, out, outb, gain_ap, post_scale=1.0, tmpn=tmpn, rsn=rsn, getps=None):
            getps = getps or getps_default
            sqt, sqb = tmpn.get()
            P.op("act", lambda e: e.activation(out=sqt[:, :n], in_=src, func=AF.Square), [srcb], [sqb])
            rt, rb = rsn.get()
            for h0 in range(0, n, 512):
                w_ = min(512, n - h0)
                pt, pb = getps()
                P.op("pe", lambda e, pt=pt, h0=h0, w_=w_: e.matmul(pt[:, :w_], onesmat, sqt[:, h0:h0 + w_], start=True, stop=True),
                     [sqb] + CB, [pb])
                P.op("dve", lambda e, pt=pt, h0=h0, w_=w_: e.tensor_scalar(
                    out=rt[:, h0:h0 + w_], in0=pt[:, :w_], scalar1=inv_d, scalar2=EPS, op0=ALU.mult, op1=ALU.add),
                    [pb], [rb])
            P.op("act", lambda e: e.activation(out=rt[:, :n], in_=rt[:, :n], func=AF.Ln), [rb], [rb])
            P.op("act", lambda e: e.activation(out=rt[:, :n], in_=rt[:, :n], func=AF.Exp, scale=-0.5), [rb], [rb])
            if gain_ap is None:
                P.op("dve", lambda e: e.scalar_tensor_tensor(out=out, in0=src, scalar=post_scale, in1=rt[:, :n],
                                                             op0=ALU.mult, op1=ALU.mult), [srcb, rb], [outb])
            else:
                P.op("dve", lambda e: e.scalar_tensor_tensor(out=out, in0=src, scalar=gain_ap, in1=rt[:, :n],
                                                             op0=ALU.mult, op1=ALU.mult), [srcb, rb] + CB, [outb])

        segin = Pool_("segin", 3, [128, SEG + 3], F32)
        ms2 = ExitStack()
        cm = Ctx(nc, ms2)
        qn = cm.sb([128, S], BF16, "qn")
        kn = cm.sb([128, S], BF16, "kn")
        kn2 = cm.sb([128, S], BF16, "kn2")
        b_kn2 = Buf()
        vt = cm.sb([128, NT * 128], BF16, "vt")
        b_qn, b_kn, b_vt = Buf(), Buf(), Buf()
        exp_ = Pool_("ex", 6, [128, 512], BF16, cm)

        if "swa" in do:
            sw_o = Pool_("swo", 2, [64, 2, SEG], F32, cm)
            sw_d = Pool_("swd", 2, [64, 256], F32, cm)
            esk = cm.sb([64, 2], F32, "esk")
            eskb = Buf()
            for b in range(NB):
                for sg in range(S // SEG):
                    t0 = b * S + sg * SEG
                    for (fi, dst, dstb, gcolid) in ((F_SQ, qn, b_qn, PC_SQG), (F_SK, kn, b_kn, PC_SKG), (F_SKB, kn2, b_kn2, PC_SKG)):
                        it, ib = segin.get()
                        P.dma("sp", lambda e, it=it, fi=fi, t0=t0: e.dma_start(out=it[:, :SEG], in_=fmT[fi, :, t0:t0 + SEG]), [], [ib])
                        headnorm(it[:, :SEG], ib, SEG, cf(C_BLK64), 1.0 / 64, dst[:, sg * SEG:(sg + 1) * SEG], dstb,
                                 pcol[:, gcolid:gcolid + 1])
                P.dma("pool", lambda e, b=b: e.dma_start(
                    out=vt[:, 0:NT * 64].rearrange("p (n d) -> p n d", d=64),
                    in_=sv_tm[b * S:(b + 1) * S, :].rearrange("(n p) d -> p n d", p=128)), [], [b_vt])
                P.op("act", lambda e, esk=esk: e.activation(out=esk[:, 0:2], in_=pcol[0:64, PC_SINK:PC_SINK + 2], func=AF.Exp),
                     CB, [eskb])
                exs = {}

                def swa_front(kt):
                    nq = 2 if kt + 1 < NT else 1
                    pt, pb = getps()
                    for qh in range(nq):
                        for j in range(2):
                            col = (qh * 2 + j) * 128
                            kk_ = kn if j == 0 else kn2
                            P.op("pe", lambda e, pt=pt, col=col, kk_=kk_, kt=kt, qh=qh: e.matmul(
                                pt[:, col:col + 128], kk_[:, kt * 128:(kt + 1) * 128],
                                qn[:, (kt + qh) * 128:(kt + qh + 1) * 128], start=True, stop=False),
                                [b_kn, b_kn2, b_qn], [pb])
                            P.op("pe", lambda e, pt=pt, col=col, qh=qh, j=j: e.matmul(
                                pt[:, col:col + 128], cbf[:, B_ID:B_ID + 128],
                                cbf[:, B_SWAM + qh * 256 + j * 128:B_SWAM + qh * 256 + (j + 1) * 128], start=False, stop=True),
                                CB, [pb])
                    ex, exb = exp_.get()
                    P.op("act", lambda e, ex=ex, pt=pt, nq=nq: e.activation(out=ex[:, :nq * 256], in_=pt[:, :nq * 256], func=AF.Exp, scale=0.125),
                         [pb], [exb])
                    exs[kt] = (ex, exb)

                swst = {"ot": None, "otb": None}

                def swa_back(kt, b=b):
                    ex, exb = exs[kt]
                    po, pob = getacc((kt % 2) * 2)
                    pd, pdb = getacc((kt % 2) * 2 + 1)
                    srcs = []
                    if kt > 0:
                        srcs.append((kt - 1, exs[kt - 1][0], exs[kt - 1][1], 256))
                    srcs.append((kt, ex, exb, 0))
                    for si, (ktile, ext, extb, c0) in enumerate(srcs):
                        last = si == len(srcs) - 1
                        P.op("pe", lambda e, po=po, ktile=ktile, ext=ext, c0=c0, si=si, last=last: e.matmul(
                            po[0:64, 0:256], vt[:, ktile * 64:(ktile + 1) * 64], ext[:, c0:c0 + 256], start=(si == 0), stop=last),
                            [b_vt, extb], [pob])
                    for si, (ktile, ext, extb, c0) in enumerate(srcs):
                        last = si == len(srcs) - 1
                        P.op("pe", lambda e, pd=pd, ext=ext, c0=c0, si=si, last=last: e.matmul(
                            pd[0:64, 0:256], cbf[:, B_ONES:B_ONES + 64], ext[:, c0:c0 + 256], start=(si == 0), stop=last),
                            [extb] + CB, [pdb])
                    if kt > 0:
                        del exs[kt - 1]
                    dt_, dtb_ = sw_d.get()
                    for j in range(2):
                        P.op("dve", lambda e, dt_=dt_, pd=pd, j=j, esk=esk: e.tensor_scalar(
                            out=dt_[:, j * 128:(j + 1) * 128], in0=pd[0:64, j * 128:(j + 1) * 128],
                            scalar1=esk[:, j:j + 1], scalar2=None, op0=ALU.add), [pdb, eskb], [dtb_])
                    P.op("act", lambda e, dt_=dt_: e.activation(out=dt_[:, :], in_=dt_[:, :], func=AF.Ln), [dtb_], [dtb_])
                    P.op("act", lambda e, dt_=dt_: e.activation(out=dt_[:, :], in_=dt_[:, :], func=AF.Exp, scale=-1.0), [dtb_], [dtb_])
                    kk = kt % 8
                    if kk == 0:
                        swst["ot"], swst["otb"] = sw_o.get()
                    ot, otb = swst["ot"], swst["otb"]
                    P.op("dve", lambda e, ot=ot, po=po, dt_=dt_, kk=kk: e.tensor_tensor(
                        out=ot[:, :, kk * 128:(kk + 1) * 128], in0=po[0:64, 0:256].rearrange("p (j q) -> p j q", j=2),
                        in1=dt_[:, :].rearrange("p (j q) -> p j q", j=2), op=ALU.mult), [pob, dtb_], [otb])
                    if kk == 7:
                        t0 = b * S + (kt - 7) * 128
                        for j in range(2):
                            P.dma("sp", lambda e, ot=ot, j=j, t0=t0: e.dma_start(
                                out=outB[3, j * 64:(j + 1) * 64, t0:t0 + SEG], in_=ot[:, j, :]), [otb], [])

                LA = 2
                for kt in range(min(LA, NT)):
                    swa_front(kt)
                for kt in range(NT):
                    if kt + LA < NT:
                        swa_front(kt + LA)
                    swa_back(kt)

        if "moba" in do:
            NBK = S // 256
            kmean = cm.sb([128, 32], F32, "kmean")
            b_kmean = Buf()
            P.op("pool", lambda e: e.memset(kmean[:, :], 0.0), [], [b_kmean])
            gm = Pool_("gm", 2, [128, 128], F32, cm)
            for t_, b__ in zip(gm.t, gm.b):
                P.op("pool", lambda e, t_=t_: e.memset(t_[:, :], 0.0), [], [b__])
            top8 = Pool_("top8", 2, [128, 8], F32, cm)
            selT_all = cm.sb([128, S], BF16, "selT_all")
            b_sel = [Buf() for _ in range(S // 256)]
            P.op("pool", lambda e: e.memset(selT_all[:, :], 0.0), [], b_sel)
            qseg32 = Pool_("qseg32", 2, [128, SEG], F32, cm)
            mo_o = Pool_("moo", 2, [128, 512], F32, cm)
            mo_r = Pool_("mor", 2, [128, 512], F32, cm)
            gate_sb = cm.sb([128, NT * 32], F32, "gate_sb")
            b_gate = [Buf() for _ in range(NT)]
            scale = 128 ** -0.5
            for b in range(NB):
                P.dma("pool", lambda e, b=b: e.dma_start(
                    out=vt[:, :].rearrange("p (n d) -> p n d", d=128),
                    in_=mv_tm[b * S:(b + 1) * S, :].rearrange("(n p) d -> p n d", p=128)), [], [b_vt])
                for sg in range(S // SEG):
                    t0 = b * S + sg * SEG
                    it, ib = segin.get()
                    P.dma("sp", lambda e, it=it, t0=t0: e.dma_start(out=it[:, :SEG], in_=fmT[F_MK, :, t0:t0 + SEG]), [], [ib])
                    kt32, kb32 = qseg32.get()
                    headnorm(it[:, :SEG], ib, SEG, ones_f, 1.0 / 128, kt32[:, :], kb32, pcol[:, PC_MKG:PC_MKG + 1])
                    P.op("act", lambda e, kt32=kt32, sg=sg: e.copy(out=kn[:, sg * SEG:(sg + 1) * SEG], in_=kt32[:, :]), [kb32], [b_kn])
                    nbs = SEG // 256
                    P.op("dve", lambda e, kt32=kt32, sg=sg: e.tensor_reduce(
                        out=kmean[:, sg * nbs:(sg + 1) * nbs], in_=kt32[:, :].rearrange("p (n k) -> p n k", k=256),
                        axis=mybir.AxisListType.X, op=ALU.add), [kb32], [b_kmean])
                P.op("dve", lambda e: e.tensor_scalar(out=kmean[:, :NBK], in0=kmean[:, :NBK], scalar1=1.0 / 256, scalar2=None,
                                                      op0=ALU.mult), [b_kmean], [b_kmean])
                for sg in range(S // SEG):
                    t0 = b * S + sg * SEG
                    it, ib = segin.get()
                    P.dma("sp", lambda e, it=it, t0=t0: e.dma_start(out=it[:, :SEG], in_=fmT[F_MQ, :, t0:t0 + SEG]), [], [ib])
                    qt32, qb32 = qseg32.get()
                    headnorm(it[:, :SEG], ib, SEG, ones_f, 1.0 / 128, qt32[:, :], qb32, pcol[:, PC_MQG:PC_MQG + 1])
                    P.op("act", lambda e, qt32=qt32, sg=sg: e.copy(out=qn[:, sg * SEG:(sg + 1) * SEG], in_=qt32[:, :]), [qb32], [b_qn])
                    for ti in range(SEG // 128):
                        qt_i = sg * (SEG // 128) + ti
                        pt, pb = getps()
                        P.op("pe", lambda e, pt=pt, qt32=qt32, ti=ti: e.matmul(
                            pt[:, 0:32], qt32[:, ti * 128:(ti + 1) * 128], kmean[:, 0:32], start=True, stop=True),
                            [qb32, b_kmean], [pb])
                        blk = qt_i // 2
                        P.op("dve", lambda e, pt=pt, qt_i=qt_i, blk=blk: e.tensor_tensor(
                            out=gate_sb[:, qt_i * 32:(qt_i + 1) * 32], in0=pt[:, 0:32], in1=negpast[:, blk * 32:(blk + 1) * 32],
                            op=ALU.add), [pb] + CB, [b_gate[qt_i]])
                for qt_i in range(NT):
                    g_ap = gate_sb[:, qt_i * 32:(qt_i + 1) * 32]
                    t8, t8b = top8.get()
                    P.op("dve", lambda e, t8=t8, g_ap=g_ap: e.max(out=t8[:, :], in_=g_ap), [b_gate[qt_i]], [t8b])
                    P.op("dve", lambda e, t8=t8: e.tensor_scalar(out=t8[:, 2:3], in0=t8[:, 2:3], scalar1=-1e29, scalar2=None,
                                                                 op0=ALU.max), [t8b], [t8b])
                    gt_, gtb = gm.get()
                    P.op("dve", lambda e, gt_=gt_, g_ap=g_ap, t8=t8: e.tensor_scalar(
                        out=gt_[:, 0:32], in0=g_ap, scalar1=t8[:, 2:3], scalar2=None, op0=ALU.is_ge), [b_gate[qt_i], t8b], [gtb])
                    P.op("dve", lambda e, gt_=gt_: e.tensor_scalar(
                        out=gt_[:, 0:32], in0=gt_[:, 0:32], scalar1=-NEG, scalar2=NEG, op0=ALU.mult, op1=ALU.add), [gtb], [gtb])
                    pt, pb = getps()
                    P.op("pe", lambda e, pt=pt, gt_=gt_: e.transpose(pt[:, 0:128], gt_[:, :], ident_f), [gtb] + CB, [pb])
                    P.op("act", lambda e, pt=pt, qt_i=qt_i: e.copy(out=selT_all[0:32, qt_i * 128:(qt_i + 1) * 128], in_=pt[0:32, 0:128]),
                         [pb], [b_sel[qt_i // 2]])
                items = [(B, kt) for B in range(S // 256) for kt in range(2 * B + 2)]
                mex = {}

                def mo_front(i):
                    B, kt = items[i]
                    n = kt // 2
                    pt, pb = getps()
                    P.op("pe", lambda e, pt=pt, kt=kt, B=B: e.matmul(
                        pt[:, 0:256], kn[:, kt * 128:(kt + 1) * 128], qn[:, B * 256:(B + 1) * 256], start=True, stop=False),
                        [b_kn, b_qn], [pb])
                    if n < B:
                        P.op("pe", lambda e, pt=pt, n=n, B=B: e.matmul(
                            pt[:, 0:256], cbf[:, B_ESEL + n * 128:B_ESEL + (n + 1) * 128], selT_all[:, B * 256:(B + 1) * 256],
                            start=False, stop=True), [b_sel[B]] + CB, [pb])
                    else:
                        v = kt % 2
                        P.op("pe", lambda e, pt=pt, v=v: e.matmul(
                            pt[:, 0:256], cbf[:, B_ID:B_ID + 128], cbf[:, B_CM + v * 256:B_CM + (v + 1) * 256], start=False, stop=True),
                            CB, [pb])
                    ex, exb = exp_.get()
                    P.op("act", lambda e, ex=ex, pt=pt: e.activation(out=ex[:, 0:256], in_=pt[:, 0:256], func=AF.Exp, scale=scale),
                         [pb], [exb])
                    mex[i] = (ex, exb)

                def mo_back(i, b=b):
                    B, kt = items[i]
                    nkt = 2 * B + 2
                    ex, exb = mex.pop(i)
                    po, pob = getacc((B % 2) * 2)
                    pd, pdb = getacc((B % 2) * 2 + 1)
                    P.op("pe", lambda e, po=po, kt=kt, ex=ex, nkt=nkt: e.matmul(
                        po[:, 0:256], vt[:, kt * 128:(kt + 1) * 128], ex[:, 0:256], start=(kt == 0), stop=(kt == nkt - 1)),
                        [b_vt, exb], [pob])
                    P.op("pe", lambda e, pd=pd, ex=ex, kt=kt, nkt=nkt: e.matmul(
                        pd[:, 0:256], cbf[:, B_ONES:B_ONES + 128], ex[:, 0:256], start=(kt == 0), stop=(kt == nkt - 1)),
                        [exb] + CB, [pdb])
                    if kt == nkt - 1:
                        rt, rb = mo_r.get()
                        P.op("dve", lambda e, rt=rt, pd=pd: e.reciprocal(out=rt[:, 0:256], in_=pd[:, 0:256]), [pdb], [rb])
                        ot, otb = mo_o.get()
                        P.op("dve", lambda e, ot=ot, po=po, rt=rt: e.tensor_tensor(out=ot[:, 0:256], in0=po[:, 0:256], in1=rt[:, 0:256], op=ALU.mult),
                             [pob, rb], [otb])
                        t0 = b * S + B * 256
                        P.dma("sp", lambda e, ot=ot, t0=t0: e.dma_start(out=outB[0, :, t0:t0 + 256], in_=ot[:, 0:256]), [otb], [])

                LA = 2
                for i in range(min(LA, len(items))):
                    mo_front(i)
                for i in range(len(items)):
                    if i + LA < len(items):
                        mo_front(i + LA)
                    mo_back(i)

        P.barrier()
        P.emit(final=False)
        ms2.close()
        if "gdn" in do:
            ms3 = ExitStack()
            cm = Ctx(nc, ms3)
            NTT = TT // 128
            gab = cm.sb([128, 2, NTT], F32, "gab")
            gall = cm.sb([128, NTT], F32, "gall")
            ball = cm.sb([128, NTT], F32, "ball")
            nball = cm.sb([128, NTT], F32, "nball")
            negA = cm.sb([128, 1], F32, "negA")
            b_g = Buf()
            P.dma("sp", lambda e: e.dma_start(out=gab[:], in_=gab_tm[:, :, :]), [], [b_g])
            P.op("act", lambda e: e.activation(out=negA[:, :], in_=pcol[:, PC_ALOG:PC_ALOG + 1], func=AF.Exp), CB, [b_g])
            P.op("dve", lambda e: e.tensor_scalar(out=negA[:, :], in0=negA[:, :], scalar1=-1.0, scalar2=None, op0=ALU.mult), [b_g], [b_g])
            P.op("act", lambda e: e.activation(out=gall[:, :], in_=gab[:, 0, :], func=AF.Exp, bias=pcol[:, PC_DTB:PC_DTB + 1]),
                 [b_g] + CB, [b_g])
            P.op("act", lambda e: e.activation(out=gall[:, :], in_=gall[:, :], func=AF.Ln, bias=cst[:, C_ONES * 128:C_ONES * 128 + 1]),
                 [b_g] + CB, [b_g])
            P.op("dve", lambda e: e.tensor_scalar(out=gall[:, :], in0=gall[:, :], scalar1=negA[:, 0:1], scalar2=None, op0=ALU.mult),
                 [b_g], [b_g])
            P.op("act", lambda e: e.activation(out=ball[:, :], in_=gab[:, 1, :], func=AF.Sigmoid), [b_g], [b_g])
            P.op("dve", lambda e: e.tensor_scalar(out=nball[:, :], in0=ball[:, :], scalar1=-1.0, scalar2=None, op0=ALU.mult),
                 [b_g], [b_g])

            lanes = []
            for ln in range(NB):
                L = {}
                L["Sst"] = cm.sb([128, 128], F32, "Sst%d" % ln)
                L["b_S"] = Buf()
                L["cv"] = Pool_("cv%d_" % ln, 1, [128, SEG], F32, cm)
                L["qs"] = Pool_("gqs%d_" % ln, 1, [128, SEG], F32, cm)
                L["ks"] = Pool_("gks%d_" % ln, 1, [128, SEG], F32, cm)
                L["vs"] = Pool_("gvs%d_" % ln, 1, [128, SEG], F32, cm)
                L["zt"] = Pool_("gzt%d_" % ln, 1, [128, SEG // 128, 128], F32, cm)
                L["og"] = Pool_("gog%d_" % ln, 1, [128, SEG], F32, cm)
                L["segin"] = Pool_("gsegin%d_" % ln, 2, [128, SEG + 3], F32, cm)
                L["tmpn"] = Pool_("gtmpn%d_" % ln, 1, [128, SEG], F32, cm)
                L["rsn"] = Pool_("grsn%d_" % ln, 1, [128, SEG], F32, cm)
                m128 = {}
                for nm, n_ in (("gbc", 2), ("egb", 2), ("arg", 2), ("dst", 2), ("dti", 2), ("A", 2), ("aa", 4), ("yy", 4),
                               ("qk", 2), ("qd", 2), ("kd", 2), ("vb", 2), ("R", 2), ("vn", 2), ("ob", 2), ("junk", 2)):
                    m128[nm] = Pool_("g%d_%s" % (ln, nm), n_, [128, 256 if nm == "aa" else 128], F32, cm)
                L["m128"] = m128
                L["cols"] = Pool_("g%d_cols" % ln, 3, [128, 8], F32, cm)
                nb_ = NPS // NB

                def getps_l(ln=ln, nb_=nb_, st_=[0]):
                    k = ln * nb_ + (st_[0] % nb_)
                    st_[0] += 1
                    return ps[k], b_ps[k]
                L["getps"] = getps_l
                lanes.append(L)

            def gdn_lane(b, L):
                Sst, b_S, cv, qs, ks, vs, zt, og, m128, cols = (L[k] for k in ("Sst", "b_S", "cv", "qs", "ks", "vs", "zt", "og", "m128", "cols"))
                segin_l, tmpn_l, rsn_l, getps = L["segin"], L["tmpn"], L["rsn"], L["getps"]
                P.op("pool", lambda e: e.memset(Sst[:, :], 0.0), [], [b_S])
                for sg in range(S // SEG):
                    t0 = b * S + sg * SEG
                    segs = {}
                    for (nm, fi, pc, pool_) in (("q", F_GQ, PC_CQ, qs), ("k", F_GK, PC_CK, ks), ("v", F_GV, PC_CV, vs)):
                        it, ib = segin_l.get()
                        if sg == 0:
                            P.op("pool", lambda e, it=it: e.memset(it[:, 0:3], 0.0), [], [ib])
                            P.dma("sp", lambda e, it=it, fi=fi, t0=t0: e.dma_start(out=it[:, 3:], in_=fmT[fi, :, t0:t0 + SEG]), [], [ib])
                        else:
                            P.dma("sp", lambda e, it=it, fi=fi, t0=t0: e.dma_start(out=it[:, :], in_=fmT[fi, :, t0 - 3:t0 + SEG]), [], [ib])
                        ct_, cb2 = cv.get()
                        P.op("dve", lambda e, ct_=ct_, it=it, pc=pc: e.tensor_scalar(
                            out=ct_[:, :], in0=it[:, 3:SEG + 3], scalar1=pcol[:, pc + 3:pc + 4], scalar2=None, op0=ALU.mult),
                            [ib] + CB, [cb2])
                        for i in (2, 1, 0):
                            P.op("dve", lambda e, ct_=ct_, it=it, pc=pc, i=i: e.scalar_tensor_tensor(
                                out=ct_[:, :], in0=it[:, i:SEG + i], scalar=pcol[:, pc + i:pc + i + 1], in1=ct_[:, :],
                                op0=ALU.mult, op1=ALU.add), [ib, cb2] + CB, [cb2])
                        dt2, db2 = pool_.get()
                        if nm == "v":
                            P.op("act", lambda e, dt2=dt2, ct_=ct_: e.activation(out=dt2[:, :], in_=ct_[:, :], func=AF.Silu), [cb2], [db2])
                        else:
                            P.op("act", lambda e, ct_=ct_: e.activation(out=ct_[:, :], in_=ct_[:, :], func=AF.Silu), [cb2], [cb2])
                            headnorm(ct_[:, :], cb2, SEG, ones_f, 1.0, dt2[:, :], db2, None,
                                     post_scale=(128 ** -0.5 if nm == "q" else 1.0), tmpn=tmpn_l, rsn=rsn_l, getps=getps)
                        segs[nm] = (dt2, db2)
                    ztile, zb = zt.get()
                    P.dma("sp", lambda e, ztile=ztile, t0=t0: e.dma_start(
                        out=ztile[:, :, :], in_=gz_tm[t0:t0 + SEG, :].rearrange("(n p) d -> p n d", p=128)), [], [zb])
                    P.op("act", lambda e, ztile=ztile: e.activation(out=ztile[:, :, :], in_=ztile[:, :, :], func=AF.Silu), [zb], [zb])
                    ogt, ogb = og.get()
                    qT_, qb_ = segs["q"]
                    kT_, kb_ = segs["k"]
                    vT_, vb_ = segs["v"]
                    for ci in range(SEG // 128):
                        gi = (t0 // 128) + ci
                        cs = slice(ci * 128, (ci + 1) * 128)
                        gcol = gall[:, gi:gi + 1]
                        bcol = ball[:, gi:gi + 1]
                        nbcol = nball[:, gi:gi + 1]
                        gbc, gbcb = m128["gbc"].get()
                        P.op("dve", lambda e, gbc=gbc, gcol=gcol: e.tensor_scalar(out=gbc[:, :], in0=ones_f, scalar1=gcol, scalar2=None,
                                                                                 op0=ALU.mult), [b_g] + CB, [gbcb])
                        pG, pGb = getps()
                        P.op("pe", lambda e, pG=pG, gbc=gbc: e.matmul(pG[:, 0:128], gbc[:, :], cf(C_TRI), start=True, stop=True),
                             [gbcb] + CB, [pGb])
                        P.op("pe", lambda e, pG=pG, gcol=gcol: e.matmul(pG[:, 128:129], cf(C_TRI), gcol, start=True, stop=True),
                             [b_g] + CB, [pGb])
                        cl, clb = cols.get()
                        P.op("dve", lambda e, cl=cl, pG=pG: e.tensor_copy(out=cl[:, 0:1], in_=pG[:, 128:129]), [pGb], [clb])
                        P.op("dve", lambda e, cl=cl, pG=pG: e.tensor_scalar(out=cl[:, 1:2], in0=pG[:, 128:129], scalar1=-1.0, scalar2=None,
                                                                           op0=ALU.mult), [pGb], [clb])
                        P.op("dve", lambda e, cl=cl, pG=pG: e.tensor_copy(out=cl[:, 2:3], in_=pG[:, 127:128]), [pGb], [clb])
                        egb, egbb = m128["egb"].get()
                        P.op("act", lambda e, egb=egb, pG=pG: e.activation(out=egb[:, :], in_=pG[:, 0:128], func=AF.Exp), [pGb], [egbb])
                        arg, argb = m128["arg"].get()
                        P.op("dve", lambda e, arg=arg, pG=pG: e.scalar_tensor_tensor(
                            out=arg[:, :], in0=pG[:, 0:128], scalar=-1.0, in1=cf(C_NEGLS), op0=ALU.mult, op1=ALU.add), [pGb] + CB, [argb])
                        dst, dstb = m128["dst"].get()
                        P.op("act", lambda e, dst=dst, arg=arg, cl=cl: e.activation(out=dst[:, :], in_=arg[:, :], func=AF.Exp, bias=cl[:, 0:1]),
                             [argb, clb], [dstb])
                        arg2, arg2b = m128["arg"].get()
                        P.op("dve", lambda e, arg2=arg2, pG=pG: e.tensor_tensor(out=arg2[:, :], in0=pG[:, 0:128], in1=cf(C_NEGU), op=ALU.add),
                             [pGb] + CB, [arg2b])
                        dti, dtib = m128["dti"].get()
                        P.op("act", lambda e, dti=dti, arg2=arg2, cl=cl: e.activation(out=dti[:, :], in_=arg2[:, :], func=AF.Exp, bias=cl[:, 1:2]),
                             [arg2b, clb], [dtib])
                        P.op("act", lambda e, cl=cl: e.activation(out=cl[:, 3:4], in_=cl[:, 0:1], func=AF.Exp, scale=-1.0, bias=cl[:, 2:3]),
                             [clb], [clb])
                        P.op("act", lambda e, cl=cl: e.activation(out=cl[:, 4:5], in_=cl[:, 0:1], func=AF.Exp), [clb], [clb])
                        P.op("dve", lambda e, cl=cl, bcol=bcol: e.tensor_scalar(out=cl[:, 5:6], in0=cl[:, 4:5], scalar1=bcol, scalar2=-1.0,
                                                                               op0=ALU.mult, op1=ALU.mult), [clb, b_g], [clb])
                        pK, pKb = getps()
                        P.op("pe", lambda e, pK=pK, kT_=kT_, cs=cs: e.matmul(pK[:, 0:128], kT_[:, cs], kT_[:, cs], start=True, stop=True),
                             [kb_], [pKb])
                        aa, aab = m128["aa"].get()
                        P.op("dve", lambda e, aa=aa, pK=pK, nbcol=nbcol, dst=dst: e.scalar_tensor_tensor(
                            out=aa[:, 0:128], in0=pK[:, 0:128], scalar=nbcol, in1=dst[:, :], op0=ALU.mult, op1=ALU.mult),
                            [pKb, b_g, dstb], [aab])
                        P.op("pe", lambda e, pK=pK, aa=aa: e.transpose(pK[:, 128:256], aa[:, 0:128], ident_f), [aab] + CB, [pKb])
                        P.op("act", lambda e, aa=aa, pK=pK: e.copy(out=aa[:, 128:256], in_=pK[:, 128:256]), [pKb], [aab])
                        yy, yyb = m128["yy"].get()
                        P.op("dve", lambda e, yy=yy, aa=aa: e.tensor_tensor(out=yy[:, :], in0=aa[:, 128:256], in1=ident_f, op=ALU.add),
                             [aab] + CB, [yyb])
                        for s_ in range(0, 7):
                            pL, pLb = getps()
                            if s_ <= 5:
                                P.op("pe", lambda e, pL=pL, aa=aa: e.matmul(pL[:, 0:128], aa[:, 128:256], aa[:, 0:128], start=True, stop=True),
                                     [aab], [pLb])
                            if s_ <= 4:
                                P.op("pe", lambda e, pL=pL, aa=aa: e.matmul(pL[:, 128:256], aa[:, 0:128], aa[:, 128:256], start=True, stop=True),
                                     [aab], [pLb])
                            if s_ >= 1:
                                pY, pYb = getps()
                                P.op("pe", lambda e, pY=pY, aa=aa, yy=yy: e.matmul(pY[:, 0:128], aa[:, 0:128], yy[:, :], start=True, stop=True),
                                     [aab, yyb], [pYb])
                                yy2, yy2b = m128["yy"].get()
                                P.op("dve", lambda e, yy2=yy2, yy=yy, pY=pY: e.tensor_tensor(out=yy2[:, :], in0=pY[:, 0:128], in1=yy[:, :],
                                                                                          op=ALU.add), [pYb, yyb], [yy2b])
                                yy, yyb = yy2, yy2b
                            if s_ <= 5:
                                aa2, aa2b = m128["aa"].get()
                                w_ = 256 if s_ <= 4 else 128
                                P.op("act", lambda e, aa2=aa2, pL=pL, w_=w_: e.copy(out=aa2[:, 0:w_], in_=pL[:, 0:w_]), [pLb], [aa2b])
                                aa, aab = aa2, aa2b
                        TTm, TTb = yy, yyb
                        pQ, pQb = getps()
                        P.op("pe", lambda e, pQ=pQ, kT_=kT_, qT_=qT_, cs=cs: e.matmul(pQ[:, 0:128], kT_[:, cs], qT_[:, cs], start=True, stop=True),
                             [kb_, qb_], [pQb])
                        P.op("pe", lambda e, pQ=pQ, kT_=kT_, cs=cs: e.transpose(pQ[:, 128:256], kT_[:, cs], ident_f), [kb_] + CB, [pQb])
                        P.op("pe", lambda e, pQ=pQ, vT_=vT_, cs=cs: e.transpose(pQ[:, 256:384], vT_[:, cs], ident_f), [vb_] + CB, [pQb])
                        qk, qkb = m128["qk"].get()
                        P.op("dve", lambda e, qk=qk, pQ=pQ, dti=dti: e.tensor_tensor(out=qk[:, :], in0=pQ[:, 0:128], in1=dti[:, :], op=ALU.mult),
                             [pQb, dtib], [qkb])
                        kd, kdb = m128["kd"].get()
                        P.op("dve", lambda e, kd=kd, pQ=pQ, cl=cl: e.tensor_scalar(out=kd[:, :], in0=pQ[:, 128:256], scalar1=cl[:, 3:4], scalar2=None,
                                                                                  op0=ALU.mult), [pQb, clb], [kdb])
                        vb2, vb2b = m128["vb"].get()
                        P.op("dve", lambda e, vb2=vb2, pQ=pQ, bcol=bcol: e.tensor_scalar(out=vb2[:, :], in0=pQ[:, 256:384], scalar1=bcol, scalar2=None,
                                                                                        op0=ALU.mult), [pQb, b_g], [vb2b])
                        qd, qdb = m128["qd"].get()
                        P.op("pool", lambda e, qd=qd, qT_=qT_, cs=cs, egb=egb: e.tensor_tensor(out=qd[:, :], in0=qT_[:, cs], in1=egb[:, :], op=ALU.mult),
                             [qb_, egbb], [qdb])
                        pS, pSb = getps()
                        pO, pOb = getps()
                        P.op("pe", lambda e, pS=pS, kT_=kT_, cs=cs: e.matmul(pS[:, 0:128], kT_[:, cs], Sst[:, :], start=True, stop=True),
                             [kb_, b_S], [pSb])
                        P.op("pe", lambda e, pO=pO, qd=qd: e.matmul(pO[:, 0:128], qd[:, :], Sst[:, :], start=True, stop=False),
                             [qdb, b_S], [pOb])
                        R, Rb = m128["R"].get()
                        P.op("dve", lambda e, R=R, pS=pS, cl=cl, vb2=vb2: e.scalar_tensor_tensor(
                            out=R[:, :], in0=pS[:, 0:128], scalar=cl[:, 5:6], in1=vb2[:, :], op0=ALU.mult, op1=ALU.add),
                            [pSb, clb, vb2b], [Rb])
                        P.op("pe", lambda e, pS=pS, TTm=TTm, R=R: e.matmul(pS[:, 128:256], TTm[:, :], R[:, :], start=True, stop=True),
                             [TTb, Rb], [pSb])
                        vn, vnb = m128["vn"].get()
                        P.op("act", lambda e, vn=vn, pS=pS: e.copy(out=vn[:, :], in_=pS[:, 128:256]), [pSb], [vnb])
                        P.op("pe", lambda e, pO=pO, qk=qk, vn=vn: e.matmul(pO[:, 0:128], qk[:, :], vn[:, :], start=False, stop=True),
                             [qkb, vnb], [pOb])
                        P.op("pe", lambda e, pS=pS, kd=kd, vn=vn: e.matmul(pS[:, 256:384], kd[:, :], vn[:, :], start=True, stop=True),
                             [kdb, vnb], [pSb])
                        P.op("dve", lambda e, pS=pS, egb=egb: e.scalar_tensor_tensor(
                            out=Sst[:, :], in0=Sst[:, :], scalar=egb[:, 127:128], in1=pS[:, 256:384], op0=ALU.mult, op1=ALU.add),
                            [pSb, egbb, b_S], [b_S])
                        jk, jkb = m128["junk"].get()
                        P.op("act", lambda e, jk=jk, pO=pO: e.activation(out=jk[:, :], in_=pO[:, 0:128], func=AF.Square), [pOb], [jkb])
                        P.op("dve", lambda e, cl=cl, jk=jk: e.tensor_reduce(out=cl[:, 6:7], in_=jk[:, :], axis=mybir.AxisListType.X, op=ALU.add),
                             [jkb], [clb])
                        P.op("dve", lambda e, cl=cl: e.tensor_scalar(out=cl[:, 6:7], in0=cl[:, 6:7], scalar1=1.0 / 128, scalar2=EPS,
                                                                     op0=ALU.mult, op1=ALU.add), [clb], [clb])
                        P.op("act", lambda e, cl=cl: e.activation(out=cl[:, 6:7], in_=cl[:, 6:7], func=AF.Sqrt), [clb], [clb])
                        P.op("dve", lambda e, cl=cl: e.reciprocal(out=cl[:, 6:7], in_=cl[:, 6:7]), [clb], [clb])
                        ob, obb = m128["ob"].get()
                        P.op("dve", lambda e, ob=ob, pO=pO, cl=cl: e.scalar_tensor_tensor(
                            out=ob[:, :], in0=pO[:, 0:128], scalar=cl[:, 6:7], in1=gnB[:, :], op0=ALU.mult, op1=ALU.mult),
                            [pOb, clb] + CB, [obb])
                        P.op("pool", lambda e, ob=ob, ztile=ztile, ci=ci: e.tensor_tensor(out=ob[:, :], in0=ob[:, :], in1=ztile[:, ci, :], op=ALU.mult),
                             [obb, zb], [obb])
                        P.op("pe", lambda e, pO=pO, ob=ob: e.transpose(pO[:, 128:256], ob[:, :], ident_f), [obb] + CB, [pOb])
                        P.op("act", lambda e, ogt=ogt, pO=pO, cs=cs: e.copy(out=ogt[:, cs], in_=pO[:, 128:256]), [pOb], [ogb])
                    P.dma("sp", lambda e, ogt=ogt, t0=t0: e.dma_start(out=outB[1, :, t0:t0 + SEG], in_=ogt[:, :]), [ogb], [])
            caps = []
            for b in range(NB):
                P.capture()
                gdn_lane(b, lanes[b])
                caps.append(P.end_capture())
            P.replay(caps)
            P.emit(final=True)
            ms3.close()
        else:
            P.emit(final=True)
    return nc
import numpy as np


def consts():
    i = np.arange(128)
    ident = np.eye(128, dtype=np.float32)
    ones = np.ones((128, 128), np.float32)
    tri = (i[:, None] <= i[None, :]).astype(np.float32)
    negls = np.where(i[None, :] < i[:, None], 0.0, NEG).astype(np.float32)
    negu = np.where(i[None, :] >= i[:, None], 0.0, NEG).astype(np.float32)
    blk64 = np.zeros((128, 128), np.float32)
    blk64[:64, :64] = 1
    blk64[64:, 64:] = 1
    cst = np.concatenate([ident, ones, tri, negls, negu, blk64], axis=1)
    negpast = np.zeros((128, 32, 32), np.float32)
    for blk in range(32):
        negpast[:, blk, blk:] = -1e30
    negpast = negpast.reshape(128, 1024)
    cbf = np.zeros((128, NCB16), np.float32)
    cbf[:, B_ID:B_ID + 128] = ident
    cbf[:, B_ONES:B_ONES + 128] = 1
    kl = i[:, None]
    ql = i[None, :]
    m0 = np.where(kl <= ql, 0.0, NEG)
    m1 = np.where(kl > ql, 0.0, NEG)
    cbf[:, B_SWAM:B_SWAM + 256] = np.concatenate([m0, m0], axis=1)
    cbf[:, B_SWAM + 256:B_SWAM + 512] = np.concatenate([m1, m1], axis=1)
    qib = np.arange(256)[None, :]
    for v in range(2):
        cbf[:, B_CM + v * 256:B_CM + (v + 1) * 256] = np.where(v * 128 + kl <= qib, 0.0, NEG)
    for n in range(32):
        cbf[n, B_ESEL + n * 128:B_ESEL + (n + 1) * 128] = 1
    return cst, negpast, cbf


def pack_b_inputs(proj, prm, c, S, NB):
    TT = NB * S
    sl = slice(c * 128, (c + 1) * 128)
    kvh = c // 4
    fm = np.zeros((NF, 128, TT), np.float32)
    fm[F_MQ] = proj["mq"][:, sl].T
    fm[F_MK] = proj["mk"][:, sl].T
    fm[F_GQ] = proj["gqkv"][:, c * 128:(c + 1) * 128].T
    fm[F_GK] = proj["gqkv"][:, 1024 + c * 128:1024 + (c + 1) * 128].T
    fm[F_GV] = proj["gqkv"][:, 2048 + c * 128:2048 + (c + 1) * 128].T
    fm[F_SCB] = proj["scb"][:, sl].T
    fm[F_SCC] = proj["scc"][:, sl].T
    fm[F_SCX] = proj["scx"][:, sl].T
    fm[F_SQ] = proj["sq"][:, sl].T
    skh = proj["sk"][:, kvh * 64:(kvh + 1) * 64].T
    fm[F_SK] = np.concatenate([skh, 0 * skh], axis=0)
    fm[F_SKB] = np.concatenate([0 * skh, skh], axis=0)
    pc = np.zeros((128, NPC), np.float32)
    pc[:, PC_MQG] = prm["moba_q_norm"]
    pc[:, PC_MKG] = prm["moba_k_norm"]
    gc = prm["gdn_conv"]
    pc[:, PC_CQ:PC_CQ + 4] = gc[:, c * 128:(c + 1) * 128].T
    pc[:, PC_CK:PC_CK + 4] = gc[:, 1024 + c * 128:1024 + (c + 1) * 128].T
    pc[:, PC_CV:PC_CV + 4] = gc[:, 2048 + c * 128:2048 + (c + 1) * 128].T
    pc[:, PC_ALOG] = prm["gdn_a_log"][c]
    pc[:, PC_DTB] = prm["gdn_dt_bias"][c]
    pc[:, PC_SC:PC_SC + 3] = prm["sc_conv"][:, sl].T
    pc[:, PC_SQG] = np.concatenate([prm["swa_q_norm"], prm["swa_q_norm"]])
    pc[:, PC_SKG] = np.concatenate([prm["swa_k_norm"], prm["swa_k_norm"]])
    pc[:, PC_SINK] = prm["swa_sinks"][2 * c]
    pc[:, PC_SINK + 1] = prm["swa_sinks"][2 * c + 1]
    gab = np.stack([proj["ga"][:, c].reshape(TT // 128, 128).T, proj["gb"][:, c].reshape(TT // 128, 128).T], axis=1)
    cst, negpast, cbf = consts()
    return {
        "fmT": fm,
        "mv_tm": np.ascontiguousarray(proj["mv"][:, sl]),
        "sv_tm": np.ascontiguousarray(proj["sv"][:, kvh * 64:(kvh + 1) * 64]),
        "gz_tm": np.ascontiguousarray(proj["gz"][:, sl]),
        "gab_tm": np.ascontiguousarray(gab),
        "pcol": pc,
        "gnB": np.broadcast_to(prm["gdn_out_norm"][None, :], (128, 128)).copy(),
        "cst": cst, "negpast": negpast, "cbf": cbf,
    }


N_CORES = 8
D_MODEL = 4096
SEQ = 8192
BATCH = 2
NTOK = BATCH * SEQ
TPC = NTOK // N_CORES
NCB_IN = 91
IN_PERM = np.concatenate([np.arange(0, 6144), np.arange(6160, 11536), np.arange(6144, 6160)])
R_MQ, R_MK, R_MV, R_GQ, R_GK, R_GV, R_GZ, R_SCB, R_SCC, R_SCX, R_SQ, R_SK, R_SV, R_GA, R_GB = (
    0, 1024, 2048, 3072, 4096, 5120, 6144, 7168, 8192, 9216, 10240, 11264, 11392, 11520, 11528)

_PROGS = {}


def _prog(name):
    if name not in _PROGS:
        if name == "A":
            _PROGS[name] = build_stage_a(TPC, NCB_IN)
        elif name == "B":
            _PROGS[name] = build_stage_b(SEQ, BATCH)
        else:
            _PROGS[name] = build_stage_c(TPC)
    return _PROGS[name]


def _shard_fm(aT, i):
    return np.ascontiguousarray(aT[:, i * TPC:(i + 1) * TPC]).reshape(KC, 128, TPC)


def _b_inputs(projT, l, c, P_):
    TT = NTOK
    kvh = c // 4
    r = lambda base: projT[base + c * 128: base + (c + 1) * 128]
    fm = np.empty((NF, 128, TT), np.float32)
    fm[F_MQ] = r(R_MQ)
    fm[F_MK] = r(R_MK)
    fm[F_GQ] = r(R_GQ)
    fm[F_GK] = r(R_GK)
    fm[F_GV] = r(R_GV)
    fm[F_SCB] = r(R_SCB)
    fm[F_SCC] = r(R_SCC)
    fm[F_SCX] = r(R_SCX)
    fm[F_SQ] = r(R_SQ)
    skh = projT[R_SK + kvh * 64: R_SK + (kvh + 1) * 64]
    fm[F_SK] = 0.0
    fm[F_SKB] = 0.0
    fm[F_SK, 0:64] = skh
    fm[F_SKB, 64:128] = skh
    pc = np.zeros((128, NPC), np.float32)
    pc[:, PC_MQG] = P_["moba_q_norm"][l]
    pc[:, PC_MKG] = P_["moba_k_norm"][l]
    gc = P_["gdn_conv"][l]
    pc[:, PC_CQ:PC_CQ + 4] = gc[:, c * 128:(c + 1) * 128].T
    pc[:, PC_CK:PC_CK + 4] = gc[:, 1024 + c * 128:1024 + (c + 1) * 128].T
    pc[:, PC_CV:PC_CV + 4] = gc[:, 2048 + c * 128:2048 + (c + 1) * 128].T
    pc[:, PC_ALOG] = P_["gdn_a_log"][l][c]
    pc[:, PC_DTB] = P_["gdn_dt_bias"][l][c]
    pc[:, PC_SC:PC_SC + 3] = P_["sc_conv"][l][:, c * 128:(c + 1) * 128].T
    pc[0:64, PC_SQG] = P_["swa_q_norm"][l]
    pc[64:128, PC_SQG] = P_["swa_q_norm"][l]
    pc[0:64, PC_SKG] = P_["swa_k_norm"][l]
    pc[64:128, PC_SKG] = P_["swa_k_norm"][l]
    pc[:, PC_SINK] = P_["swa_sinks"][l][2 * c]
    pc[:, PC_SINK + 1] = P_["swa_sinks"][l][2 * c + 1]
    gab = np.stack([projT[R_GA + c].reshape(TT // 128, 128).T, projT[R_GB + c].reshape(TT // 128, 128).T], axis=1)
    cst, negpast, cbf = consts()
    return {
        "fmT": fm,
        "mv_tm": np.ascontiguousarray(r(R_MV).T),
        "sv_tm": np.ascontiguousarray(projT[R_SV + kvh * 64: R_SV + (kvh + 1) * 64].T),
        "gz_tm": np.ascontiguousarray(r(R_GZ).T),
        "gab_tm": np.ascontiguousarray(gab),
        "pcol": pc,
        "gnB": np.ascontiguousarray(np.broadcast_to(P_["gdn_out_norm"][l][None, :], (128, 128))),
        "cst": cst, "negpast": negpast, "cbf": cbf,
    }


def kernel(x, norm_mix, w_in, moba_q_norm, moba_k_norm, gdn_conv, gdn_a_log, gdn_dt_bias,
           gdn_out_norm, sc_conv, swa_q_norm, swa_k_norm, swa_sinks, w_out, norm_ffn,
           w_gate, w_up, w_down):
    f32 = np.float32
    P_ = {k: np.asarray(v, f32) for k, v in dict(
        moba_q_norm=moba_q_norm, moba_k_norm=moba_k_norm, gdn_conv=gdn_conv, gdn_a_log=gdn_a_log,
        gdn_dt_bias=gdn_dt_bias, gdn_out_norm=gdn_out_norm, sc_conv=sc_conv, swa_q_norm=swa_q_norm,
        swa_k_norm=swa_k_norm, swa_sinks=swa_sinks).items()}
    x = np.asarray(x, f32)
    depth = w_in.shape[0]
    cores = list(range(N_CORES))
    ones = np.ones((128, 128), f32)
    xT = np.ascontiguousarray(x.reshape(NTOK, D_MODEL).T)
    for l in range(depth):
        wl = host_w_layout(np.asarray(w_in[l], f32), IN_PERM, NCB_IN)
        gcol = np.ascontiguousarray(np.asarray(norm_mix[l], f32).reshape(KC, 128).T)
        in_maps = [{"xT": _shard_fm(xT, i), "gcol": gcol, "w": wl, "ones": ones} for i in cores]
        res = run_bass_kernel_spmd(_prog("A"), in_maps, core_ids=cores)
        projT = np.concatenate([res.results[i]["projT"].reshape(NCB_IN * 128, TPC) for i in cores], axis=1)
        del wl, in_maps, res
        in_maps = [_b_inputs(projT, l, c, P_) for c in cores]
        res = run_bass_kernel_spmd(_prog("B"), in_maps, core_ids=cores)
        mixT = np.empty((D_MODEL, NTOK), f32)
        for c in cores:
            ob = res.results[c]["outB"]
            for m in range(4):
                mixT[m * 1024 + c * 128: m * 1024 + (c + 1) * 128] = ob[m]
        del in_maps, res, projT
        wo = lay_cols(np.asarray(w_out[l], f32), KC)
        wg = lay_cols(np.asarray(w_gate[l], f32), JB)
        wu = lay_cols(np.asarray(w_up[l], f32), JB)
        wd = lay_wd(np.asarray(w_down[l], f32), JB)
        gcol2 = np.ascontiguousarray(np.asarray(norm_ffn[l], f32).reshape(KC, 128).T)
        in_maps = [{"xT": _shard_fm(xT, i), "mixT": _shard_fm(mixT, i), "gcol": gcol2, "wo": wo, "wg": wg,
                    "wu": wu, "wd": wd, "ones": ones} for i in cores]
        res = run_bass_kernel_spmd(_prog("C"), in_maps, core_ids=cores)
        xT = np.concatenate([res.results[i]["x2T"].reshape(D_MODEL, TPC) for i in cores], axis=1)
        del wo, wg, wu, wd, in_maps, res, mixT
    return np.ascontiguousarray(xT.T).reshape(BATCH, SEQ, D_MODEL).astype(f32)
```
